# Optimizing a Trainium2 kernel written in Bass

```python
import jax, jax.numpy as jnp
from jax import lax
import numpy as np

D_MODEL = 1024
BATCH = 4
SEQ = 8192
DEPTH = 2

MIX_WIDTH = D_MODEL
SGU_WIDTH = MIX_WIDTH // 2
MLSTM_WIDTH = MIX_WIDTH - SGU_WIDTH
SGU_HEADS = 4
SGU_HEAD_DIM = SGU_WIDTH // SGU_HEADS
SGU_CHUNK = 128
MLSTM_HEADS = 4
MLSTM_HEAD_DIM = MLSTM_WIDTH // MLSTM_HEADS
MLSTM_CHUNK = 128
QKV_BLOCK = 4
CONV_WIDTH = 5
FFN_HIDDEN = ((8 * D_MODEL // 3 + 255) // 256) * 256
IN_WIDTH = 2 * SGU_WIDTH + 2 * MLSTM_WIDTH
EPS = 1e-6

kernel_name = 'hybrid_gmlp_mlstm_macaron_encoder'


def rms_norm(x, g):
    xf = x.astype(jnp.float32)
    y = xf * lax.rsqrt(jnp.mean(xf * xf, axis=-1, keepdims=True) + EPS)
    return (y * g.astype(jnp.float32)).astype(x.dtype)


def head_layer_norm(x, g):
    xf = x.astype(jnp.float32)
    mu = jnp.mean(xf, axis=-1, keepdims=True)
    var = jnp.mean(jnp.square(xf - mu), axis=-1, keepdims=True)
    y = (xf - mu) * lax.rsqrt(var + EPS) * g.astype(jnp.float32)
    return y.astype(x.dtype)


def swiglu(x, w_gate, w_up, w_down):
    return (jax.nn.silu(x @ w_gate) * (x @ w_up)) @ w_down


def spatial_gating(u, v, g, w_s, b_s):
    B, S, _ = u.shape
    nc = S // SGU_CHUNK
    u = jax.nn.gelu(u)
    v = jax.nn.gelu(v)
    vh = head_layer_norm(v.reshape(B, nc, SGU_CHUNK, SGU_HEADS, SGU_HEAD_DIM), g)
    s = jnp.einsum('hpq,bcqhd->bcphd', w_s, vh) + b_s.T[:, :, None]
    return u * s.reshape(B, S, SGU_WIDTH)


def depthwise_conv(x, w, b):
    C = x.shape[-1]
    pad = CONV_WIDTH // 2
    y = lax.conv_general_dilated(x, w[:, None, :], window_strides=(1,), padding=[(pad, pad)],
                                 dimension_numbers=('NWC', 'WIO', 'NWC'), feature_group_count=C)
    return y + b


def headwise(x, w):
    B, S, C = x.shape
    xb = x.reshape(B, S, C // QKV_BLOCK, QKV_BLOCK)
    return jnp.einsum('bsgi,gio->bsgo', xb, w).reshape(B, S, C)


def mlstm_chunkwise(q, k, v, ig, lf):
    B, S, H, Dh = q.shape
    L = MLSTM_CHUNK
    nc = S // L
    to_chunks = lambda t: t.reshape(B, nc, L, H, Dh).transpose(1, 0, 3, 2, 4)
    g_chunks = lambda t: t.reshape(B, nc, L, H).transpose(1, 0, 3, 2)
    tri = jnp.tril(jnp.ones((L, L), dtype=bool))

    def step(carry, inp):
        C, n, m = carry
        qc, kc, vc, ic, fc = inp
        b = jnp.cumsum(fc, axis=-1)
        d = jnp.where(tri, b[..., :, None] - b[..., None, :] + ic[..., None, :], -jnp.inf)
        inter = b + m[..., None]
        m_j = jnp.maximum(inter, jnp.max(d, axis=-1))
        w_intra = jnp.exp(d - m_j[..., None])
        w_inter = jnp.exp(inter - m_j)
        s = jnp.einsum('bhjd,bhsd->bhjs', qc, kc) * w_intra
        num = (w_inter[..., None] * jnp.einsum('bhvk,bhjk->bhjv', C, qc)
               + jnp.einsum('bhjs,bhsv->bhjv', s, vc))
        nq = w_inter * jnp.einsum('bhk,bhjk->bhj', n, qc) + jnp.sum(s, axis=-1)
        h = num / jnp.maximum(jnp.abs(nq), jnp.exp(-m_j))[..., None]
        b_last = b[..., -1]
        dec = b_last[..., None] - b + ic
        m_new = jnp.maximum(b_last + m, jnp.max(dec, axis=-1))
        wk = jnp.exp(dec - m_new[..., None])
        scale = jnp.exp(b_last + m - m_new)
        C_new = scale[..., None, None] * C + jnp.einsum('bhs,bhsv,bhsk->bhvk', wk, vc, kc)
        n_new = scale[..., None] * n + jnp.einsum('bhs,bhsk->bhk', wk, kc)
        return (C_new, n_new, m_new), h

    init = (jnp.zeros((B, H, Dh, Dh), jnp.float32),
            jnp.zeros((B, H, Dh), jnp.float32),
            jnp.zeros((B, H), jnp.float32))
    _, hs = lax.scan(step, init, (to_chunks(q), to_chunks(k), to_chunks(v), g_chunks(ig), g_chunks(lf)))
    return hs.transpose(1, 0, 3, 2, 4).reshape(B, S, H, Dh)


def mlstm_mixer(xm, og, conv_w, conv_b, w_q, w_k, w_v, gate_w_fwd, gate_b_fwd,
                gate_w_bwd, gate_b_bwd, mh_norm, skip):
    B, S, _ = xm.shape
    H, Dh = MLSTM_HEADS, MLSTM_HEAD_DIM
    xc = jax.nn.silu(depthwise_conv(xm, conv_w, conv_b))
    q = headwise(xc, w_q)
    k = headwise(xc, w_k) * (Dh ** -0.5)
    v = headwise(xm, w_v)
    qkv = jnp.concatenate([q, k, v], axis=-1)

    def gates(w, b):
        g = (qkv @ w + b).astype(jnp.float32)
        return g[..., :H], jax.nn.log_sigmoid(g[..., H:])

    qh = q.reshape(B, S, H, Dh).astype(jnp.float32)
    kh = k.reshape(B, S, H, Dh).astype(jnp.float32)
    vh = v.reshape(B, S, H, Dh).astype(jnp.float32)
    ig_f, lf_f = gates(gate_w_fwd, gate_b_fwd)
    ig_b, lf_b = gates(gate_w_bwd, gate_b_bwd)
    h_fwd = mlstm_chunkwise(qh, kh, vh, ig_f, lf_f)
    flip = lambda t: jnp.flip(t, axis=1)
    h_bwd = flip(mlstm_chunkwise(flip(qh), flip(kh), flip(vh), flip(ig_b), flip(lf_b)))
    hn = head_layer_norm(h_fwd + h_bwd, mh_norm).reshape(B, S, MLSTM_WIDTH).astype(xm.dtype)
    return (hn + skip * xc) * jax.nn.sigmoid(og)


def setup_inputs(seed: int = 0) -> dict:
    key = jax.random.key(seed)
    ks = iter(jax.random.split(key, 40))
    nrm = lambda shape, scale: jax.random.normal(next(ks), shape, jnp.float32) * scale
    gain = lambda shape: 1.0 + nrm(shape, 0.02)
    H = MLSTM_HEADS

    def gate_bias():
        ib = nrm((DEPTH, H), 0.1)
        fb = jnp.linspace(3.0, 6.0, H, dtype=jnp.float32)[None, :] + nrm((DEPTH, H), 0.01)
        return jnp.concatenate([ib, fb], axis=-1)

    return {
        'x': nrm((BATCH, SEQ, D_MODEL), 1.0),
        'ffn1_norm': gain((DEPTH, D_MODEL)),
        'ffn1_w_gate': nrm((DEPTH, D_MODEL, FFN_HIDDEN), D_MODEL ** -0.5),
        'ffn1_w_up': nrm((DEPTH, D_MODEL, FFN_HIDDEN), D_MODEL ** -0.5),
        'ffn1_w_down': nrm((DEPTH, FFN_HIDDEN, D_MODEL), FFN_HIDDEN ** -0.5),
        'mix_norm': gain((DEPTH, D_MODEL)),
        'w_in': nrm((DEPTH, D_MODEL, IN_WIDTH), D_MODEL ** -0.5),
        'sgu_norm': gain((DEPTH, SGU_HEADS, SGU_HEAD_DIM)),
        'sgu_w': nrm((DEPTH, SGU_HEADS, SGU_CHUNK, SGU_CHUNK), 0.5 * SGU_CHUNK ** -0.5),
        'sgu_b': 1.0 + nrm((DEPTH, SGU_HEADS, SGU_CHUNK), 0.1),
        'conv_w': nrm((DEPTH, CONV_WIDTH, MLSTM_WIDTH), CONV_WIDTH ** -0.5),
        'conv_b': nrm((DEPTH, MLSTM_WIDTH), 0.02),
        'w_q': nrm((DEPTH, MLSTM_WIDTH // QKV_BLOCK, QKV_BLOCK, QKV_BLOCK), QKV_BLOCK ** -0.5),
        'w_k': nrm((DEPTH, MLSTM_WIDTH // QKV_BLOCK, QKV_BLOCK, QKV_BLOCK), QKV_BLOCK ** -0.5),
        'w_v': nrm((DEPTH, MLSTM_WIDTH // QKV_BLOCK, QKV_BLOCK, QKV_BLOCK), QKV_BLOCK ** -0.5),
        'gate_w_fwd': nrm((DEPTH, 3 * MLSTM_WIDTH, 2 * H), 0.1 * (3 * MLSTM_WIDTH) ** -0.5),
        'gate_b_fwd': gate_bias(),
        'gate_w_bwd': nrm((DEPTH, 3 * MLSTM_WIDTH, 2 * H), 0.1 * (3 * MLSTM_WIDTH) ** -0.5),
        'gate_b_bwd': gate_bias(),
        'mh_norm': gain((DEPTH, MLSTM_HEADS, MLSTM_HEAD_DIM)),
        'mlstm_skip': gain((DEPTH, MLSTM_WIDTH)),
        'w_out': nrm((DEPTH, MIX_WIDTH, D_MODEL), MIX_WIDTH ** -0.5),
        'ffn2_norm': gain((DEPTH, D_MODEL)),
        'ffn2_w_gate': nrm((DEPTH, D_MODEL, FFN_HIDDEN), D_MODEL ** -0.5),
        'ffn2_w_up': nrm((DEPTH, D_MODEL, FFN_HIDDEN), D_MODEL ** -0.5),
        'ffn2_w_down': nrm((DEPTH, FFN_HIDDEN, D_MODEL), FFN_HIDDEN ** -0.5),
        'final_norm': gain((D_MODEL,)),
    }


def reference(x, ffn1_norm, ffn1_w_gate, ffn1_w_up, ffn1_w_down, mix_norm, w_in,
              sgu_norm, sgu_w, sgu_b, conv_w, conv_b, w_q, w_k, w_v,
              gate_w_fwd, gate_b_fwd, gate_w_bwd, gate_b_bwd, mh_norm, mlstm_skip, w_out,
              ffn2_norm, ffn2_w_gate, ffn2_w_up, ffn2_w_down, final_norm):
    split_at = [SGU_WIDTH, 2 * SGU_WIDTH, 2 * SGU_WIDTH + MLSTM_WIDTH]
    for l in range(DEPTH):
        h = x + 0.5 * swiglu(rms_norm(x, ffn1_norm[l]), ffn1_w_gate[l], ffn1_w_up[l], ffn1_w_down[l])
        z = rms_norm(h, mix_norm[l]) @ w_in[l]
        u, v, xm, og = jnp.split(z, split_at, axis=-1)
        y_sgu = spatial_gating(u, v, sgu_norm[l], sgu_w[l], sgu_b[l])
        y_mlstm = mlstm_mixer(xm, og, conv_w[l], conv_b[l], w_q[l], w_k[l], w_v[l],
                              gate_w_fwd[l], gate_b_fwd[l], gate_w_bwd[l], gate_b_bwd[l],
                              mh_norm[l], mlstm_skip[l])
        h = h + jnp.concatenate([y_sgu, y_mlstm], axis=-1) @ w_out[l]
        x = h + 0.5 * swiglu(rms_norm(h, ffn2_norm[l]), ffn2_w_gate[l], ffn2_w_up[l], ffn2_w_down[l])
    return rms_norm(x, final_norm)
```

```python
import contextlib
import numpy as np
import ml_dtypes
import concourse.bass as bass
import concourse.mybir as mybir
from concourse.bass_utils import run_bass_kernel_spmd

F32 = mybir.dt.float32
BF16 = mybir.dt.bfloat16
AF = mybir.ActivationFunctionType
ALU = mybir.AluOpType

D = 1024
HID = 2816
NJ = HID // 128
EPS = 1e-6
ENGS = ("sync", "act", "dve", "pool", "pe")


class Buf:
    __slots__ = ("w", "r")

    def __init__(self):
        self.w = None
        self.r = []


class Op:
    __slots__ = ("eng", "fn", "deps", "ev", "dma")


class DmaPool:
    def __init__(self, nc, eng, n):
        self.slots = [[nc.alloc_semaphore(name=f"dq_{eng}_{i}"), 0, None] for i in range(n)]
        self.i = 0


class Sched:
    def __init__(self, nc, pools, tag):
        self.nc = nc
        self.pools = pools
        self.lists = {e: [] for e in ENGS}
        self.esem = {e: nc.alloc_semaphore(name=f"e_{tag}_{e}") for e in ENGS if e != "sync"}
        self.ecnt = {e: 0 for e in ENGS}

    def _deps(self, reads, writes):
        deps = []
        for b in reads:
            if b.w is not None:
                deps.append(b.w)
        for b in writes:
            if b.w is not None:
                deps.append(b.w)
            deps.extend(b.r)
        return deps

    def _commit(self, o, reads, writes):
        for b in reads:
            b.r.append(o)
        for b in writes:
            b.w = o
            b.r = []
        self.lists[o.eng].append(o)

    def op(self, eng, fn, reads=(), writes=()):
        o = Op()
        o.eng, o.fn, o.dma = eng, fn, False
        o.deps = self._deps(reads, writes)
        self.ecnt[eng] += 1
        o.ev = (self.esem[eng], self.ecnt[eng])
        self._commit(o, reads, writes)
        return o

    def dma(self, eng, out, in_, reads=(), writes=(), **kw):
        o = Op()
        o.eng, o.dma = eng, True
        o.fn = lambda e: e.dma_start(out=out, in_=in_, **kw)
        o.deps = self._deps(reads, writes)
        pool = self.pools[eng]
        slot = pool.slots[pool.i % len(pool.slots)]
        pool.i += 1
        if slot[2] is not None:
            o.deps.append(slot[2])
        slot[1] += 16
        slot[2] = o
        o.ev = (slot[0], slot[1])
        self._commit(o, reads, writes)
        return o

    def emit(self):
        nc = self.nc
        finals = [(s, c) for e, s in self.esem.items() for c in [self.ecnt[e]] if c > 0]
        for p in self.pools.values():
            for s, c, last in p.slots:
                if c > 0:
                    finals.append((s, c))
        names = {"sync": "sync", "act": "scalar", "dve": "vector", "pool": "gpsimd", "pe": "tensor"}
        with nc.Block() as blk:
            for eng in ENGS:
                def body(e, eng=eng):
                    waited = {}
                    for o in self.lists[eng]:
                        need = {}
                        for d in o.deps:
                            if d.eng == "pe" and eng == "pe" and not d.dma:
                                continue
                            s, v = d.ev
                            k = id(s)
                            if need.get(k, (None, 0))[1] < v:
                                need[k] = (s, v)
                        for k, (s, v) in need.items():
                            if waited.get(k, 0) >= v:
                                continue
                            e.wait_ge(s, v)
                            waited[k] = v
                        ins = o.fn(e)
                        ins.then_inc(o.ev[0], 16 if o.dma else 1)
                    for s, v in finals:
                        if waited.get(id(s), 0) < v:
                            e.wait_ge(s, v)
                getattr(blk, names[eng])(body)


class Ctx:
    def __init__(self, nc):
        self.nc = nc
        self.pools = {"sync": DmaPool(nc, "sync", 24), "pool": DmaPool(nc, "pool", 12), "act": DmaPool(nc, "act", 6)}
        self.nphase = 0

    def sched(self):
        self.nphase += 1
        return Sched(self.nc, self.pools, f"p{self.nphase}")


class Tile:
    def __init__(self, t, nslots=1):
        self.t = t
        self.b = [Buf() for _ in range(nslots)]


_UID = [0]


def _uname(name):
    _UID[0] += 1
    return f"t{_UID[0]}_{name}"


def sb(es, nc, name, shape, dt, nslots=1):
    t = es.enter_context(nc.sbuf_tensor(_uname(name), [128 if shape[0] is None else shape[0]] + list(shape[1:]), dt))
    return Tile(t, nslots)


def ps(es, nc, name, shape, dt, nslots=1):
    t = es.enter_context(nc.psum_tensor(_uname(name), list(shape), dt))
    return Tile(t, nslots)


def convert_wgu_phase(C, wg, wu, wgu):
    nc = C.nc
    S = C.sched()
    with contextlib.ExitStack() as es:
        wst = sb(es, nc, "wst", [128, 2, 2, 8, 256], F32, 2)
        wbf = sb(es, nc, "wbf", [128, 2, 2, 2, 8, 128], BF16, 2)
        engs = ("dve", "pool", "act", "dve")
        for jp in range(NJ // 2):
            sl = jp % 2
            for a, w in enumerate((wg, wu)):
                src = w.rearrange("(kc p) n -> p kc n", p=128)[:, :, jp * 256:(jp + 1) * 256]
                S.dma("sync", wst.t[:, sl, a, :, :], src, writes=[wst.b[sl]])
            i = 0
            for jj in range(2):
                for a in range(2):
                    eng = engs[i]; i += 1
                    o = wbf.t[:, sl, jj, a, :, :]
                    src = wst.t[:, sl, a, :, jj * 128:(jj + 1) * 128]
                    if eng == "act":
                        S.op("act", lambda e, o=o, src=src: e.activation(out=o, in_=src, func=AF.Copy),
                             reads=[wst.b[sl]], writes=[wbf.b[sl]])
                    else:
                        S.op(eng, lambda e, o=o, src=src: e.tensor_copy(out=o, in_=src),
                             reads=[wst.b[sl]], writes=[wbf.b[sl]])
            S.dma("sync", wgu[jp * 2:jp * 2 + 2].rearrange("j p a k c -> p j (a k c)"),
                  wbf.t[:, sl].rearrange("p j a k c -> p j (a k c)"), reads=[wbf.b[sl]])
        S.emit()


def ffn_phase(C, T, xin, xout, gvec, wgu, wd_f32, final_g=None):
    nc = C.nc
    S = C.sched()
    NG = T // 1024
    with contextlib.ExitStack() as es:
        ident = sb(es, nc, "ident", [128, 128], BF16)
        gb = sb(es, nc, "gb", [128, D], F32)
        gfb = sb(es, nc, "gfb", [128, D], F32)
        pw = sb(es, nc, "pw", [128, 1], F32)
        wd = sb(es, nc, "wd", [128, NJ, D], BF16)
        wgus = sb(es, nc, "wgus", [128, 3, 2, 2, 8, 128], BF16, 3)
        xprep = sb(es, nc, "xprep", [128, 4, D], F32, 4)
        xres = sb(es, nc, "xres", [128, 3, D], F32, 3)
        xn = sb(es, nc, "xn", [128, 2, D], BF16, 2)
        stat = sb(es, nc, "stat", [128, 8, 4], F32, 8)
        xnT = sb(es, nc, "xnT", [128, 2, 8, 1024], BF16, 2)
        aT = sb(es, nc, "aT", [128, NJ, 1024], BF16, 1)
        sg = sb(es, nc, "sg", [128, 2, 512], BF16, 2)
        junk = sb(es, nc, "junk", [128, D], BF16, 1)
        pg = ps(es, nc, "pg", [128, 2, 512], F32, 2)
        pu = ps(es, nc, "pu", [128, 2, 512], F32, 2)
        pt = ps(es, nc, "pt", [128, 2, 1024], BF16, 2)
        po = ps(es, nc, "po", [128, 2, 512], F32, 2)

        S.op("pool", lambda e: e.memset(ident.t[:], 0.0), writes=[ident.b[0]])
        S.op("pool", lambda e: e.affine_select(out=ident.t[:], in_=ident.t[:], pattern=[[-1, 128]],
                                               compare_op=ALU.not_equal, fill=1.0, base=0, channel_multiplier=1),
             reads=[ident.b[0]], writes=[ident.b[0]])
        S.op("pool", lambda e: e.memset(pw.t[:], -0.5), writes=[pw.b[0]])
        S.dma("sync", gb.t[:], gvec.partition_broadcast(128), writes=[gb.b[0]])
        if final_g is not None:
            S.dma("sync", gfb.t[:], final_g.partition_broadcast(128), writes=[gfb.b[0]])
        wdsrc = wd_f32.rearrange("(j p) n -> p j n", p=128)
        for j0 in range(0, NJ, 2):
            S.dma("pool", wd.t[:, j0:j0 + 2, :], wdsrc[:, j0:j0 + 2, :], writes=[wd.b[0]])

        cnt = {"xp": 0, "xn": 0, "st": 0, "pt": 0, "wg": 0, "g": 0, "sg": 0, "po": 0, "xr": 0}

        def prep_items(g):
            fronts, backs = [], []
            slot = g % 2
            for s in range(8):
                def it(s=s):
                    i = cnt["xp"]; cnt["xp"] += 1
                    k = i % 4
                    tok0 = (g * 8 + s) * 128
                    xp = xprep.t[:, k, :]
                    S.dma("sync", xp, xin[tok0:tok0 + 128, :], writes=[xprep.b[k]])
                    q = cnt["st"] % 8; cnt["st"] += 1
                    ss = stat.t[:, q, 0:1]
                    rstd = stat.t[:, q, 1:2]
                    S.op("dve", lambda e: e.scalar_tensor_tensor(out=junk.t[:], in0=xp, scalar=1.0, in1=xp,
                                                                 op0=ALU.mult, op1=ALU.mult, accum_out=ss),
                         reads=[xprep.b[k]], writes=[junk.b[0], stat.b[q]])
                    S.op("pool", lambda e: e.tensor_scalar(out=rstd, in0=ss, scalar1=1.0 / D, scalar2=EPS,
                                                           op0=ALU.mult, op1=ALU.add),
                         reads=[stat.b[q]], writes=[stat.b[q]])
                    S.op("pool", lambda e: e.tensor_tensor(out=rstd, in0=rstd, in1=pw.t[:], op=ALU.pow),
                         reads=[stat.b[q], pw.b[0]], writes=[stat.b[q]])
                    n = cnt["xn"] % 2; cnt["xn"] += 1
                    S.op("dve", lambda e: e.scalar_tensor_tensor(out=xn.t[:, n, :], in0=xp, scalar=rstd, in1=gb.t[:],
                                                                 op0=ALU.mult, op1=ALU.mult),
                         reads=[xprep.b[k], stat.b[q], gb.b[0]], writes=[xn.b[n]])
                    return n

                def bk(s=s, n=None):
                    p = cnt["pt"] % 2; cnt["pt"] += 1
                    for kc in range(8):
                        S.op("pe", lambda e, kc=kc: e.transpose(out=pt.t[:, p, kc * 128:(kc + 1) * 128],
                                                                in_=xn.t[:, n, kc * 128:(kc + 1) * 128],
                                                                identity=ident.t[:]),
                             reads=[xn.b[n], ident.b[0]], writes=[pt.b[p]])
                    S.op("act", lambda e: e.activation(
                        out=xnT.t[:, slot, :, s * 128:(s + 1) * 128],
                        in_=pt.t[:, p, :].rearrange("p (k t) -> p k t", k=8), func=AF.Copy),
                         reads=[pt.b[p]], writes=[xnT.b[slot]])
                fronts.append(it)
                backs.append(bk)
            nsl = {}

            def mk_f(i):
                def f():
                    nsl[i] = fronts[i]()
                return f

            def mk_b(i):
                def f():
                    backs[i](n=nsl[i])
                return f
            order = [mk_f(0)]
            for i in range(1, 8):
                order.append(mk_f(i))
                order.append(mk_b(i - 1))
            order.append(mk_b(7))
            return order

        def gateup(g, extra):
            slot = g % 2
            for j in range(NJ):
                if j % 2 == 0:
                    w = cnt["wg"] % 3; cnt["wg"] += 1
                    S.dma("sync", wgus.t[:, w].rearrange("p j a k c -> p j (a k c)"),
                          wgu[j:j + 2].rearrange("j p a k c -> p j (a k c)"), writes=[wgus.b[w]])
                    wcur = w
                jj = j % 2
                for half in range(2):
                    q = cnt["g"] % 2; cnt["g"] += 1
                    for (pp, a) in ((pg, 0), (pu, 1)):
                        for kc in range(8):
                            S.op("pe", lambda e, pp=pp, a=a, kc=kc, q=q, half=half, wcur=wcur, jj=jj: e.matmul(
                                out=pp.t[:, q, :], lhsT=wgus.t[:, wcur, jj, a, kc, :],
                                rhs=xnT.t[:, slot, kc, half * 512:(half + 1) * 512],
                                start=(kc == 0), stop=(kc == 7)),
                                 reads=[wgus.b[wcur], xnT.b[slot]], writes=[pp.b[q]])
                    r = cnt["sg"] % 2; cnt["sg"] += 1
                    S.op("act", lambda e, q=q, r=r: e.activation(out=sg.t[:, r, :], in_=pg.t[:, q, :], func=AF.Silu),
                         reads=[pg.b[q]], writes=[sg.b[r]])
                    S.op("dve", lambda e, q=q, r=r, j=j, half=half: e.tensor_tensor(
                        out=aT.t[:, j, half * 512:(half + 1) * 512], in0=pu.t[:, q, :], in1=sg.t[:, r, :], op=ALU.mult),
                         reads=[pu.b[q], sg.b[r]], writes=[aT.b[0]])
                    if extra:
                        extra.pop(0)()

        def down(g, extra):
            for m in range(8):
                tok0 = (g * 8 + m) * 128
                x = cnt["xr"] % 3; cnt["xr"] += 1
                S.dma("sync", xres.t[:, x, :], xin[tok0:tok0 + 128, :], writes=[xres.b[x]])
                for n in range(2):
                    q = cnt["po"] % 2; cnt["po"] += 1
                    for j in range(NJ):
                        S.op("pe", lambda e, j=j, q=q, m=m, n=n: e.matmul(
                            out=po.t[:, q, :], lhsT=aT.t[:, j, m * 128:(m + 1) * 128],
                            rhs=wd.t[:, j, n * 512:(n + 1) * 512], start=(j == 0), stop=(j == NJ - 1)),
                             reads=[aT.b[0], wd.b[0]], writes=[po.b[q]])
                    S.op("dve", lambda e, q=q, x=x, n=n: e.scalar_tensor_tensor(
                        out=xres.t[:, x, n * 512:(n + 1) * 512], in0=po.t[:, q, :], scalar=0.5,
                        in1=xres.t[:, x, n * 512:(n + 1) * 512], op0=ALU.mult, op1=ALU.add),
                         reads=[po.b[q], xres.b[x]], writes=[xres.b[x]])
                if final_g is not None:
                    qs = cnt["st"] % 8; cnt["st"] += 1
                    ss = stat.t[:, qs, 0:1]
                    rstd = stat.t[:, qs, 1:2]
                    xr = xres.t[:, x, :]
                    S.op("dve", lambda e, xr=xr, ss=ss: e.scalar_tensor_tensor(
                        out=junk.t[:], in0=xr, scalar=1.0, in1=xr, op0=ALU.mult, op1=ALU.mult, accum_out=ss),
                         reads=[xres.b[x]], writes=[junk.b[0], stat.b[qs]])
                    S.op("pool", lambda e, ss=ss, rstd=rstd: e.tensor_scalar(
                        out=rstd, in0=ss, scalar1=1.0 / D, scalar2=EPS, op0=ALU.mult, op1=ALU.add),
                         reads=[stat.b[qs]], writes=[stat.b[qs]])
                    S.op("pool", lambda e, rstd=rstd: e.tensor_tensor(out=rstd, in0=rstd, in1=pw.t[:], op=ALU.pow),
                         reads=[stat.b[qs], pw.b[0]], writes=[stat.b[qs]])
                    S.op("dve", lambda e, xr=xr, rstd=rstd: e.scalar_tensor_tensor(
                        out=xr, in0=xr, scalar=rstd, in1=gfb.t[:], op0=ALU.mult, op1=ALU.mult),
                         reads=[xres.b[x], stat.b[qs], gfb.b[0]], writes=[xres.b[x]])
                S.dma("pool", xout[tok0:tok0 + 128, :], xres.t[:, x, :], reads=[xres.b[x]])
                if extra:
                    extra.pop(0)()

        for it in prep_items(0):
            it()
        for g in range(NG):
            nxt = prep_items(g + 1) if g + 1 < NG else []
            gateup(g, nxt)
            down(g, nxt)
            while nxt:
                nxt.pop(0)()
        S.emit()


class Rot:
    def __init__(self, n):
        self.n, self.i = n, 0

    def __call__(self):
        k = self.i % self.n
        self.i += 1
        return k


def make_ident(S, tile, dt_is_f32=False):
    S.op("pool", lambda e: e.memset(tile.t[:], 0.0), writes=[tile.b[0]])
    S.op("pool", lambda e: e.affine_select(out=tile.t[:], in_=tile.t[:], pattern=[[-1, 128]],
                                           compare_op=ALU.not_equal, fill=1.0, base=0, channel_multiplier=1),
         reads=[tile.b[0]], writes=[tile.b[0]])


def rstd_ops(S, out, in_, pwt, b_in, b_out, scale=1.0):
    S.op("pool", lambda e: e.tensor_scalar(out=out, in0=in_, scalar1=scale, scalar2=EPS, op0=ALU.mult, op1=ALU.add),
         reads=[b_in], writes=[b_out])
    S.op("pool", lambda e: e.tensor_tensor(out=out, in0=out, in1=pwt, op=ALU.pow), reads=[b_out], writes=[b_out])


GELU_C = 0.7978845608028654


def mixa_phase(C, T, xin, gvec, w_in, sgu_norm, sgu_w, sgu_b, xmT_pad, sog, ysguT, gelu_fn):
    nc = C.nc
    S = C.sched()
    NT = T // 512
    with contextlib.ExitStack() as es:
        es.enter_context(nc.allow_non_contiguous_dma(reason="tiny parameter transposes"))
        ident = sb(es, nc, "ident", [128, 128], BF16)
        identf = sb(es, nc, "identf", [128, 128], F32)
        gb = sb(es, nc, "gb", [128, D], F32)
        pw = sb(es, nc, "pw", [128, 4], F32)
        win = sb(es, nc, "win", [128, 8, 2048], BF16)
        wsf = sb(es, nc, "wsf", [128, 4, 128], F32)
        wsT = sb(es, nc, "wsT", [128, 4, 128], BF16)
        gT = sb(es, nc, "gT", [128, 4], F32)
        bsb = sb(es, nc, "bsb", [128, 4, 128], F32)
        zt = sb(es, nc, "zt", [128, 4, 2], F32)
        xprep = sb(es, nc, "xprep", [128, 4, D], F32, 4)
        junk = sb(es, nc, "junk", [128, D], BF16)
        stat = sb(es, nc, "stat", [128, 8, 4], F32, 8)
        xn = sb(es, nc, "xn", [128, 2, D], BF16, 2)
        xnT = sb(es, nc, "xnT", [128, 2, 8, 512], BF16, 2)
        guT = sb(es, nc, "guT", [128, 2, 4, 512], F32, 2)
        xms = sb(es, nc, "xms", [128, 2, 4, 512], F32, 2)
        gv = sb(es, nc, "gv", [128, 2, 512], F32, 2)
        bst = sb(es, nc, "bst", [128, 2, 4, 6], F32, 2)
        mv = sb(es, nc, "mv", [128, 2, 4, 2], F32, 2)
        rs = sb(es, nc, "rs", [128, 2, 4], F32, 2)
        vhn = sb(es, nc, "vhn", [128, 2, 4, 4, 128], BF16, 2)
        sgs = sb(es, nc, "sgs", [128, 2, 512], BF16, 2)
        tmp = sb(es, nc, "tmp", [128, 2, 512], F32, 2)
        ys = sb(es, nc, "ys", [128, 2, 4, 512], BF16, 2)
        gtmp = sb(es, nc, "gtmp", [128, 2, 512], F32, 2)
        P = ps(es, nc, "P", [128, 8, 512], F32, 8)
        rP = Rot(8)

        make_ident(S, ident)
        make_ident(S, identf)
        S.op("pool", lambda e: e.memset(pw.t[:], -0.5), writes=[pw.b[0]])
        S.op("pool", lambda e: e.memset(zt.t[:], 0.0), writes=[zt.b[0]])
        S.dma("sync", gb.t[:], gvec.partition_broadcast(128), writes=[gb.b[0]])
        S.dma("pool", win.t[:, 0:4, :], w_in.rearrange("(kc p) n -> p kc n", p=128)[:, 0:4, :], writes=[win.b[0]])
        S.dma("pool", win.t[:, 4:8, :], w_in.rearrange("(kc p) n -> p kc n", p=128)[:, 4:8, :], writes=[win.b[0]])
        S.dma("sync", wsf.t[:], sgu_w.rearrange("h p q -> p h q"), writes=[wsf.b[0]])
        S.dma("sync", gT.t[:], sgu_norm.rearrange("h d -> d h"), writes=[gT.b[0]])
        S.dma("sync", bsb.t[:].rearrange("p h q -> p (h q)"), sgu_b.rearrange("h q -> (h q)").partition_broadcast(128),
              writes=[bsb.b[0]])
        xmv = xmT_pad.rearrange("(cc p) t -> p cc t", p=128)
        S.dma("sync", xmv[:, :, 0:2], zt.t[:], reads=[zt.b[0]])
        S.dma("sync", xmv[:, :, T + 2:T + 4], zt.t[:], reads=[zt.b[0]])
        for hh in range(4):
            k = rP()
            S.op("pe", lambda e, hh=hh, k=k: e.transpose(out=P.t[:, k, 0:128], in_=wsf.t[:, hh, :], identity=identf.t[:]),
                 reads=[wsf.b[0], identf.b[0]], writes=[P.b[k]])
            S.op("dve", lambda e, hh=hh, k=k: e.tensor_copy(out=wsT.t[:, hh, :], in_=P.t[:, k, 0:128]),
                 reads=[P.b[k]], writes=[wsT.b[0]])

        rxp, rst, rxn = Rot(4), Rot(8), Rot(2)

        def gelu(out, in_ap, in_buf, out_buf, n):
            if gelu_fn is not None:
                S.op("act", lambda e: e.activation(out=out, in_=in_ap, func=gelu_fn), reads=[in_buf], writes=[out_buf])
                return
            k = rgt()
            t = gtmp.t[:, k, 0:n]
            S.op("act", lambda e: e.activation(out=t, in_=in_ap, func=AF.Square), reads=[in_buf], writes=[gtmp.b[k]])
            S.op("pool", lambda e: e.tensor_scalar(out=t, in0=t, scalar1=0.044715, scalar2=1.0, op0=ALU.mult, op1=ALU.add),
                 reads=[gtmp.b[k]], writes=[gtmp.b[k]])
            S.op("dve", lambda e: e.tensor_tensor(out=t, in0=in_ap, in1=t, op=ALU.mult), reads=[in_buf, gtmp.b[k]],
                 writes=[gtmp.b[k]])
            S.op("act", lambda e: e.activation(out=t, in_=t, func=AF.Sigmoid, scale=2.0 * GELU_C),
                 reads=[gtmp.b[k]], writes=[gtmp.b[k]])
            S.op("dve", lambda e: e.tensor_tensor(out=out, in0=in_ap, in1=t, op=ALU.mult), reads=[in_buf, gtmp.b[k]],
                 writes=[out_buf])
        rgt = Rot(2)

        def prep(ti):
            slot = ti % 2
            for s in range(4):
                tok0 = ti * 512 + s * 128
                k = rxp()
                xp = xprep.t[:, k, :]
                S.dma("sync", xp, xin[tok0:tok0 + 128, :], writes=[xprep.b[k]])
                q = rst()
                ss, rstd = stat.t[:, q, 0:1], stat.t[:, q, 1:2]
                S.op("dve", lambda e, xp=xp, ss=ss: e.scalar_tensor_tensor(
                    out=junk.t[:], in0=xp, scalar=1.0, in1=xp, op0=ALU.mult, op1=ALU.mult, accum_out=ss),
                     reads=[xprep.b[k]], writes=[junk.b[0], stat.b[q]])
                rstd_ops(S, rstd, ss, pw.t[:, 0:1], stat.b[q], stat.b[q], 1.0 / D)
                n = rxn()
                S.op("dve", lambda e, xp=xp, rstd=rstd, n=n: e.scalar_tensor_tensor(
                    out=xn.t[:, n, :], in0=xp, scalar=rstd, in1=gb.t[:], op0=ALU.mult, op1=ALU.mult),
                     reads=[xprep.b[k], stat.b[q], gb.b[0]], writes=[xn.b[n]])
                p = rP()
                ptv = P.t[:, p, :].bitcast(BF16)
                for kc in range(8):
                    S.op("pe", lambda e, kc=kc, n=n, ptv=ptv: e.transpose(
                        out=ptv[:, kc * 128:(kc + 1) * 128], in_=xn.t[:, n, kc * 128:(kc + 1) * 128], identity=ident.t[:]),
                         reads=[xn.b[n], ident.b[0]], writes=[P.b[p]])
                S.op("act", lambda e, s=s, ptv=ptv: e.activation(
                    out=xnT.t[:, slot, :, s * 128:(s + 1) * 128], in_=ptv.rearrange("p (k t) -> p k t", k=8), func=AF.Copy),
                     reads=[P.b[p]], writes=[xnT.b[slot]])

        def body(ti):
            slot = ti % 2
            tok0 = ti * 512
            for cc in range(4):
                k = rP()
                for kc in range(8):
                    S.op("pe", lambda e, cc=cc, kc=kc, k=k: e.matmul(
                        out=P.t[:, k, :], lhsT=win.t[:, kc, cc * 128:(cc + 1) * 128], rhs=xnT.t[:, slot, kc, :],
                        start=(kc == 0), stop=(kc == 7)), reads=[win.b[0], xnT.b[slot]], writes=[P.b[k]])
                gelu(guT.t[:, slot, cc, :], P.t[:, k, :], P.b[k], guT.b[slot], 512)
            for cc in range(4):
                k = rP()
                for kc in range(8):
                    S.op("pe", lambda e, cc=cc, kc=kc, k=k: e.matmul(
                        out=P.t[:, k, :], lhsT=win.t[:, kc, 1024 + cc * 128:1024 + (cc + 1) * 128],
                        rhs=xnT.t[:, slot, kc, :], start=(kc == 0), stop=(kc == 7)),
                         reads=[win.b[0], xnT.b[slot]], writes=[P.b[k]])
                S.op("act", lambda e, cc=cc, k=k: e.activation(out=xms.t[:, slot, cc, :], in_=P.t[:, k, :], func=AF.Copy),
                     reads=[P.b[k]], writes=[xms.b[slot]])
            S.dma("sync", xmv[:, :, 2 + tok0:2 + tok0 + 512], xms.t[:, slot], reads=[xms.b[slot]])
            for m in range(4):
                k = rP()
                for kc in range(8):
                    S.op("pe", lambda e, m=m, kc=kc, k=k: e.matmul(
                        out=P.t[:, k, :], lhsT=xnT.t[:, slot, kc, m * 128:(m + 1) * 128], rhs=win.t[:, kc, 512:1024],
                        start=(kc == 0), stop=(kc == 7)), reads=[win.b[0], xnT.b[slot]], writes=[P.b[k]])
                v = m % 2
                gelu(gv.t[:, v, :], P.t[:, k, :], P.b[k], gv.b[v], 512)
                for hh in range(4):
                    S.op("dve", lambda e, hh=hh, v=v: e.bn_stats(out=bst.t[:, v, hh, :], in_=gv.t[:, v, hh * 128:(hh + 1) * 128]),
                         reads=[gv.b[v]], writes=[bst.b[v]])
                for hh in range(4):
                    S.op("dve", lambda e, hh=hh, v=v: e.bn_aggr(out=mv.t[:, v, hh, :], in_=bst.t[:, v, hh, :]),
                         reads=[bst.b[v]], writes=[mv.b[v]])
                rstd_ops(S, rs.t[:, v, :], mv.t[:, v, :, 1], pw.t[:], mv.b[v], rs.b[v])
                for hh in range(4):
                    S.op("dve", lambda e, hh=hh, v=v, m=m: e.tensor_scalar(
                        out=vhn.t[:, slot, m, hh, :], in0=gv.t[:, v, hh * 128:(hh + 1) * 128],
                        scalar1=mv.t[:, v, hh, 0:1], scalar2=rs.t[:, v, hh:hh + 1], op0=ALU.subtract, op1=ALU.mult),
                         reads=[gv.b[v], mv.b[v], rs.b[v]], writes=[vhn.b[slot]])
                k = rP()
                for kc in range(8):
                    S.op("pe", lambda e, m=m, kc=kc, k=k: e.matmul(
                        out=P.t[:, k, :], lhsT=xnT.t[:, slot, kc, m * 128:(m + 1) * 128], rhs=win.t[:, kc, 1536:2048],
                        start=(kc == 0), stop=(kc == 7)), reads=[win.b[0], xnT.b[slot]], writes=[P.b[k]])
                S.op("act", lambda e, k=k, v=v: e.activation(out=sgs.t[:, v, :], in_=P.t[:, k, :], func=AF.Sigmoid),
                     reads=[P.b[k]], writes=[sgs.b[v]])
                S.dma("sync", sog[tok0 + m * 128:tok0 + (m + 1) * 128, :], sgs.t[:, v, :], reads=[sgs.b[v]])
            for hh in range(4):
                k = rP()
                for m in range(4):
                    S.op("pe", lambda e, hh=hh, m=m, k=k: e.matmul(
                        out=P.t[:, k, m * 128:(m + 1) * 128], lhsT=vhn.t[:, slot, m, hh, :], rhs=wsT.t[:, hh, :],
                        start=True, stop=True), reads=[vhn.b[slot], wsT.b[0]], writes=[P.b[k]])
                tq = hh % 2
                S.op("dve", lambda e, hh=hh, k=k, tq=tq: e.scalar_tensor_tensor(
                    out=tmp.t[:, tq, :].rearrange("p (m q) -> p m q", m=4), in0=P.t[:, k, :].rearrange("p (m q) -> p m q", m=4),
                    scalar=gT.t[:, hh:hh + 1], in1=bsb.t[:, hh:hh + 1, :].broadcast_to([128, 4, 128]),
                    op0=ALU.mult, op1=ALU.add), reads=[P.b[k], gT.b[0], bsb.b[0]], writes=[tmp.b[tq]])
                S.op("pool", lambda e, hh=hh, tq=tq: e.tensor_tensor(
                    out=ys.t[:, slot, hh, :], in0=tmp.t[:, tq, :], in1=guT.t[:, slot, hh, :], op=ALU.mult),
                     reads=[tmp.b[tq], guT.b[slot]], writes=[ys.b[slot]])
            S.dma("sync", ysguT.rearrange("(h p) t -> p h t", p=128)[:, :, tok0:tok0 + 512], ys.t[:, slot],
                  reads=[ys.b[slot]])

        prep(0)
        for ti in range(NT):
            if ti + 1 < NT:
                prep(ti + 1)
            body(ti)
        S.emit()


def mixb_phase(C, T, bwd, xmT_pad, hf, conv_w, conv_b, w_q, w_k, w_v, gate_w, gate_b,
               sog=None, ysguT=None, mh_norm=None, skip=None, w_out=None, xin=None, xout=None, dbg=None):
    nc = C.nc
    S = C.sched()
    NCH = T // 128
    NT = T // 512
    with contextlib.ExitStack() as es:
        es.enter_context(nc.allow_non_contiguous_dma(reason="tiny parameter transposes"))
        ident = sb(es, nc, "ident", [128, 128], BF16)
        identf = sb(es, nc, "identf", [128, 128], F32)
        maskf = sb(es, nc, "maskf", [128, 128], F32)
        onesf = sb(es, nc, "onesf", [128, 128], F32)
        mbd = sb(es, nc, "mbd", [128, 32], F32)
        pw = sb(es, nc, "pw", [128, 4], F32)
        cwT = sb(es, nc, "cwT", [128, 4, 5], F32)
        cbT = sb(es, nc, "cbT", [128, 4], F32)
        cdiag = sb(es, nc, "cdiag", [128, 4, 5, 128], BF16)
        wl = sb(es, nc, "wl", [128, 3, 4, 4], F32)
        bdf = sb(es, nc, "bdf", [128, 3, 4, 128], F32)
        bd = sb(es, nc, "bd", [128, 3, 4, 128], BF16)
        bdT = sb(es, nc, "bdT", [128, 3, 4, 128], F32)
        gw = sb(es, nc, "gw", [128, 12, 8], F32)
        Gf = sb(es, nc, "Gf", [128, 2, 4, 8], BF16)
        gbb = sb(es, nc, "gbb", [128, 8], F32)
        xmf = sb(es, nc, "xmf", [128, 2, 4, 516], F32, 2)
        xmb = sb(es, nc, "xmb", [128, 2, 4, 516], BF16, 2)
        xcT = sb(es, nc, "xcT", [128, 3, 4, 512], BF16, 3)
        gsb = sb(es, nc, "gsb", [128, 3, 8], F32, 3)
        lfn = sb(es, nc, "lfn", [128, 3, 8], F32, 3)
        ebt = sb(es, nc, "ebt", [128, 3, 12], F32, 3)
        qs = sb(es, nc, "qs", [128, 3, 4, 128], BF16, 3)
        ks = sb(es, nc, "ks", [128, 3, 4, 128], BF16, 3)
        vext = sb(es, nc, "vext", [128, 3, 4, 130], BF16, 3)
        qkT = sb(es, nc, "qkT", [128, 3, 2, 4, 128], BF16, 3)
        Sm = sb(es, nc, "Sm", [128, 2, 4, 128], BF16, 2)
        Cst = sb(es, nc, "Cst", [128, 4, 129], F32)
        Cbf = sb(es, nc, "Cbf", [128, 4, 130], BF16)
        den = sb(es, nc, "den", [128, 2, 8], F32, 2)
        hd = sb(es, nc, "hd", [128, 3, 4, 128], F32, 3)
        P = ps(es, nc, "P", [128, 8, 512], F32, 8)
        rP = Rot(8)
        if bwd:
            mhg = sb(es, nc, "mhg", [128, 512], F32)
            skT = sb(es, nc, "skT", [128, 4], F32)
            sdiag = sb(es, nc, "sdiag", [128, 4, 128], BF16)
            wout = sb(es, nc, "wout", [128, 8, D], BF16)
            hfl = sb(es, nc, "hfl", [128, 2, 512], F32, 2)
            sgl = sb(es, nc, "sgl", [128, 2, 512], BF16, 2)
            ycT = sb(es, nc, "ycT", [128, 2, 8, 128], BF16, 2)
            xr = sb(es, nc, "xr", [128, 2, D], F32, 2)
            bst = sb(es, nc, "bst", [128, 2, 4, 6], F32, 2)
            mv = sb(es, nc, "mv", [128, 2, 4, 2], F32, 2)
            rs = sb(es, nc, "rs", [128, 2, 4], F32, 2)
            hn = sb(es, nc, "hn", [128, 2, 512], F32, 2)
            ym = sb(es, nc, "ym", [128, 2, 512], BF16, 2)

        make_ident(S, ident)
        make_ident(S, identf)
        S.op("pool", lambda e: e.memset(pw.t[:], -0.5), writes=[pw.b[0]])
        S.op("pool", lambda e: e.memset(onesf.t[:], 1.0), writes=[onesf.b[0]])
        S.op("pool", lambda e: e.memset(maskf.t[:], 1.0), writes=[maskf.b[0]])
        S.op("pool", lambda e: e.affine_select(out=maskf.t[:], in_=maskf.t[:], pattern=[[-1 if bwd else 1, 128]],
                                               compare_op=ALU.is_ge, fill=0.0, base=0,
                                               channel_multiplier=1 if bwd else -1),
             reads=[maskf.b[0]], writes=[maskf.b[0]])
        S.op("pool", lambda e: e.memset(mbd.t[:], 1.0), writes=[mbd.b[0]])
        S.op("pool", lambda e: e.affine_select(out=mbd.t[:], in_=mbd.t[:], pattern=[[-4, 32]], compare_op=ALU.is_ge,
                                               fill=0.0, base=0, channel_multiplier=1),
             reads=[mbd.b[0]], writes=[mbd.b[0]])
        S.op("pool", lambda e: e.affine_select(out=mbd.t[:], in_=mbd.t[:], pattern=[[4, 32]], compare_op=ALU.is_ge,
                                               fill=0.0, base=3, channel_multiplier=-1),
             reads=[mbd.b[0]], writes=[mbd.b[0]])
        for cc in range(4):
            S.dma("sync", cwT.t[:, cc, :], conv_w[:, cc * 128:(cc + 1) * 128].rearrange("j p -> p j"), writes=[cwT.b[0]])
        S.dma("sync", cbT.t[:], conv_b.rearrange("(cc p) -> p cc", p=128), writes=[cbT.b[0]])
        for i, w in enumerate((w_q, w_k, w_v)):
            S.dma("sync", wl.t[:, i, :, :], w.rearrange("(hh g) i o -> (g i) hh o", hh=4), writes=[wl.b[0]])
        S.dma("sync", gw.t[:], gate_w.rearrange("(r p) n -> p r n", p=128), writes=[gw.b[0]])
        S.dma("sync", gbb.t[:], gate_b.partition_broadcast(128), writes=[gbb.b[0]])
        S.op("dve", lambda e: e.tensor_scalar(out=wl.t[:, 1], in0=wl.t[:, 1], scalar1=128.0 ** -0.5, scalar2=None, op0=ALU.mult),
             reads=[wl.b[0]], writes=[wl.b[0]])
        for i in range(3):
            for hh in range(4):
                S.op("dve", lambda e, i=i, hh=hh: e.tensor_tensor(
                    out=bdf.t[:, i, hh, :].rearrange("p (g o) -> p g o", o=4),
                    in0=mbd.t[:].unsqueeze(2).broadcast_to([128, 32, 4]),
                    in1=wl.t[:, i, hh:hh + 1, :].broadcast_to([128, 32, 4]), op=ALU.mult),
                     reads=[mbd.b[0], wl.b[0]], writes=[bdf.b[0]])
                k = rP()
                S.op("pe", lambda e, i=i, hh=hh, k=k: e.transpose(out=P.t[:, k, 0:128], in_=bdf.t[:, i, hh, :], identity=identf.t[:]),
                     reads=[bdf.b[0], identf.b[0]], writes=[P.b[k]])
                S.op("act", lambda e, i=i, hh=hh, k=k: e.activation(out=bdT.t[:, i, hh, :], in_=P.t[:, k, 0:128], func=AF.Copy),
                     reads=[P.b[k]], writes=[bdT.b[0]])
        S.op("pool", lambda e: e.tensor_copy(out=bd.t[:], in_=bdf.t[:]), reads=[bdf.b[0]], writes=[bd.b[0]])
        for cc in range(4):
            for j in range(5):
                S.op("dve", lambda e, cc=cc, j=j: e.tensor_scalar(out=cdiag.t[:, cc, j, :], in0=identf.t[:],
                                                                  scalar1=cwT.t[:, cc, j:j + 1], scalar2=None, op0=ALU.mult),
                     reads=[identf.b[0], cwT.b[0]], writes=[cdiag.b[0]])
        for cc in range(4):
            k = rP()
            S.op("pe", lambda e, cc=cc, k=k: e.matmul(out=P.t[:, k, 0:8], lhsT=bdT.t[:, 0, cc, :], rhs=gw.t[:, cc, :],
                                                      start=True, stop=False), reads=[bdT.b[0], gw.b[0]], writes=[P.b[k]])
            S.op("pe", lambda e, cc=cc, k=k: e.matmul(out=P.t[:, k, 0:8], lhsT=bdT.t[:, 1, cc, :], rhs=gw.t[:, 4 + cc, :],
                                                      start=False, stop=True), reads=[bdT.b[0], gw.b[0]], writes=[P.b[k]])
            S.op("pe", lambda e, cc=cc, k=k: e.matmul(out=P.t[:, k, 8:16], lhsT=bdT.t[:, 2, cc, :], rhs=gw.t[:, 8 + cc, :],
                                                      start=True, stop=True), reads=[bdT.b[0], gw.b[0]], writes=[P.b[k]])
            S.op("dve", lambda e, cc=cc, k=k: e.tensor_copy(out=Gf.t[:, :, cc, :], in_=P.t[:, k, 0:16].rearrange("p (a n) -> p a n", a=2)),
                 reads=[P.b[k]], writes=[Gf.b[0]])
        S.op("pool", lambda e: e.memset(Cst.t[:], 0.0), writes=[Cst.b[0]])
        S.op("pool", lambda e: e.memset(Cbf.t[:], 0.0), writes=[Cbf.b[0]])
        for i in range(3):
            S.op("pool", lambda e, i=i: e.memset(vext.t[:, i, :, 128:130], 1.0), writes=[vext.b[i]])
        if bwd:
            S.dma("sync", mhg.t[:], mh_norm.rearrange("h d -> (h d)").partition_broadcast(128), writes=[mhg.b[0]])
            S.dma("sync", skT.t[:], skip.rearrange("(cc p) -> p cc", p=128), writes=[skT.b[0]])
            for cc in range(4):
                S.op("dve", lambda e, cc=cc: e.tensor_scalar(out=sdiag.t[:, cc, :], in0=identf.t[:], scalar1=skT.t[:, cc:cc + 1],
                                                             scalar2=None, op0=ALU.mult),
                     reads=[identf.b[0], skT.b[0]], writes=[sdiag.b[0]])
            wsrc = w_out.rearrange("(kc p) n -> p kc n", p=128)
            S.dma("pool", wout.t[:, 0:4, :], wsrc[:, 0:4, :], writes=[wout.b[0]])
            S.dma("pool", wout.t[:, 4:8, :], wsrc[:, 4:8, :], writes=[wout.b[0]])

        xmv = xmT_pad.rearrange("(cc p) t -> p cc t", p=128)
        tiles_slot = {}
        rA = Rot(3)
        rxm = Rot(2)

        def stageA(ti):
            tok0 = ti * 512
            a = rxm()
            S.dma("sync", xmf.t[:, a], xmv[:, :, tok0:tok0 + 516], writes=[xmf.b[a]])
            S.op("pool", lambda e: e.tensor_copy(out=xmb.t[:, a], in_=xmf.t[:, a]), reads=[xmf.b[a]], writes=[xmb.b[a]])
            sl = rA()
            for cc in range(4):
                k = rP()
                for j in range(5):
                    S.op("pe", lambda e, cc=cc, j=j, k=k: e.matmul(out=P.t[:, k, :], lhsT=cdiag.t[:, cc, j, :],
                                                                   rhs=xmb.t[:, a, cc, j:j + 512], start=(j == 0), stop=(j == 4)),
                         reads=[cdiag.b[0], xmb.b[a]], writes=[P.b[k]])
                S.op("act", lambda e, cc=cc, k=k: e.activation(out=xcT.t[:, sl, cc, :], in_=P.t[:, k, :], func=AF.Silu,
                                                               bias=cbT.t[:, cc:cc + 1]),
                     reads=[P.b[k], cbT.b[0]], writes=[xcT.b[sl]])
            tiles_slot[ti] = (a, sl)
            if dbg and ti == 0:
                S.dma("sync", dbg["xcT"], xcT.t[:, sl], reads=[xcT.b[sl]])
                S.dma("sync", dbg["bd"], bd.t[:], reads=[bd.b[0]])
                S.dma("sync", dbg["Gf"], Gf.t[:], reads=[Gf.b[0]])
                S.dma("sync", dbg["maskf"], maskf.t[:], reads=[maskf.b[0]])
                S.dma("sync", dbg["cwT"], cwT.t[:], reads=[cwT.b[0]])
                S.dma("sync", dbg["cdiag"], cdiag.t[:], reads=[cdiag.b[0]])
                S.dma("sync", dbg["xmb"], xmb.t[:, a], reads=[xmb.b[a]])
                S.dma("sync", dbg["cbT"], cbT.t[:], reads=[cbT.b[0]])

        rB = Rot(3)
        st = {}

        def stageB(c):
            ti, m = divmod(c, 4)
            a, sl = tiles_slot[ti]
            b = rB()
            st[c] = b
            kq, kk, kv, kg = rP(), rP(), rP(), rP()
            for (kx, wi, src) in ((kq, 0, "xc"), (kk, 1, "xc"), (kv, 2, "xm")):
                for hh in range(4):
                    lhsT = (xcT.t[:, sl, hh, m * 128:(m + 1) * 128] if src == "xc"
                            else xmb.t[:, a, hh, 2 + m * 128:2 + (m + 1) * 128])
                    S.op("pe", lambda e, kx=kx, wi=wi, hh=hh, lhsT=lhsT: e.matmul(
                        out=P.t[:, kx, hh * 128:(hh + 1) * 128], lhsT=lhsT, rhs=bd.t[:, wi, hh, :], start=True, stop=True),
                         reads=[xcT.b[sl], xmb.b[a], bd.b[0]], writes=[P.b[kx]])
            for i in range(8):
                cc = i % 4
                lhsT = (xcT.t[:, sl, cc, m * 128:(m + 1) * 128] if i < 4 else xmb.t[:, a, cc, 2 + m * 128:2 + (m + 1) * 128])
                S.op("pe", lambda e, i=i, cc=cc, lhsT=lhsT: e.matmul(out=P.t[:, kg, 0:8], lhsT=lhsT, rhs=Gf.t[:, i // 4, cc, :],
                                                                     start=(i == 0), stop=(i == 7)),
                     reads=[xcT.b[sl], xmb.b[a], Gf.b[0]], writes=[P.b[kg]])
            S.op("dve", lambda e: e.tensor_tensor(out=gsb.t[:, b, :], in0=P.t[:, kg, 0:8], in1=gbb.t[:], op=ALU.add),
                 reads=[P.b[kg], gbb.b[0]], writes=[gsb.b[b]])
            S.op("act", lambda e: e.activation(out=lfn.t[:, b, 0:4], in_=gsb.t[:, b, 4:8], func=AF.Exp, scale=-1.0),
                 reads=[gsb.b[b]], writes=[lfn.b[b]])
            S.op("act", lambda e: e.activation(out=lfn.t[:, b, 4:8], in_=lfn.t[:, b, 0:4], func=AF.Ln, bias=1.0),
                 reads=[lfn.b[b]], writes=[lfn.b[b]])
            S.op("pe", lambda e: e.matmul(out=P.t[:, kg, 8:12], lhsT=maskf.t[:], rhs=lfn.t[:, b, 4:8], start=True, stop=True),
                 reads=[maskf.b[0], lfn.b[b]], writes=[P.b[kg]])
            S.op("pe", lambda e: e.matmul(out=P.t[:, kg, 12:16], lhsT=onesf.t[:], rhs=lfn.t[:, b, 4:8], start=True, stop=True),
                 reads=[onesf.b[0], lfn.b[b]], writes=[P.b[kg]])
            S.op("act", lambda e: e.activation(out=ebt.t[:, b, 0:8], in_=P.t[:, kg, 8:16], func=AF.Exp, scale=-1.0),
                 reads=[P.b[kg]], writes=[ebt.b[b]])
            S.op("dve", lambda e: e.tensor_tensor(out=gsb.t[:, b, 4:8], in0=P.t[:, kg, 8:12], in1=gsb.t[:, b, 0:4], op=ALU.add),
                 reads=[P.b[kg], gsb.b[b]], writes=[gsb.b[b]])
            S.op("act", lambda e: e.activation(out=ebt.t[:, b, 8:12], in_=gsb.t[:, b, 4:8], func=AF.Exp),
                 reads=[gsb.b[b]], writes=[ebt.b[b]])
            S.op("dve", lambda e: e.tensor_tensor(
                out=qs.t[:, b], in0=P.t[:, kq, :].rearrange("p (h d) -> p h d", h=4),
                in1=ebt.t[:, b, 0:4].unsqueeze(2).broadcast_to([128, 4, 128]), op=ALU.mult),
                 reads=[P.b[kq], ebt.b[b]], writes=[qs.b[b]])
            S.op("dve", lambda e: e.tensor_tensor(
                out=ks.t[:, b], in0=P.t[:, kk, :].rearrange("p (h d) -> p h d", h=4),
                in1=ebt.t[:, b, 8:12].unsqueeze(2).broadcast_to([128, 4, 128]), op=ALU.mult),
                 reads=[P.b[kk], ebt.b[b]], writes=[ks.b[b]])
            S.op("act", lambda e: e.activation(out=vext.t[:, b, :, 0:128], in_=P.t[:, kv, :].rearrange("p (h d) -> p h d", h=4),
                                               func=AF.Copy), reads=[P.b[kv]], writes=[vext.b[b]])
            kt = rP()
            ptv = P.t[:, kt, :].bitcast(BF16)
            for i, src in enumerate((qs, ks)):
                for hh in range(4):
                    S.op("pe", lambda e, i=i, hh=hh, src=src: e.transpose(
                        out=ptv[:, (i * 4 + hh) * 128:(i * 4 + hh + 1) * 128], in_=src.t[:, b, hh, :], identity=ident.t[:]),
                         reads=[src.b[b], ident.b[0]], writes=[P.b[kt]])
            S.op("act", lambda e: e.activation(out=qkT.t[:, b].rearrange("p a h t -> p (a h t)"), in_=ptv, func=AF.Copy),
                 reads=[P.b[kt]], writes=[qkT.b[b]])

            if dbg and c == 0:
                S.dma("sync", dbg["gsb"], gsb.t[:, b], reads=[gsb.b[b]])
                S.dma("sync", dbg["lfn"], lfn.t[:, b], reads=[lfn.b[b]])
                S.dma("sync", dbg["ebt"], ebt.t[:, b], reads=[ebt.b[b]])
                S.dma("sync", dbg["qs"], qs.t[:, b], reads=[qs.b[b]])
                S.dma("sync", dbg["ks"], ks.t[:, b], reads=[ks.b[b]])
                S.dma("sync", dbg["vext"], vext.t[:, b], reads=[vext.b[b]])
                S.dma("sync", dbg["qkT"], qkT.t[:, b], reads=[qkT.b[b]])

        rS = Rot(2)
        rH = Rot(3)
        sh = {}

        def stageC(c):
            b = st[c]
            k1 = rP()
            for hh in range(4):
                S.op("pe", lambda e, hh=hh: e.matmul(out=P.t[:, k1, hh * 128:(hh + 1) * 128], lhsT=qkT.t[:, b, 1, hh, :],
                                                     rhs=qkT.t[:, b, 0, hh, :], start=True, stop=True),
                     reads=[qkT.b[b]], writes=[P.b[k1]])
            s = rS()
            S.op("dve", lambda e: e.tensor_tensor(out=Sm.t[:, s], in0=P.t[:, k1, :].rearrange("p (h j) -> p h j", h=4),
                                                  in1=maskf.t[:].unsqueeze(1).broadcast_to([128, 4, 128]), op=ALU.mult),
                 reads=[P.b[k1], maskf.b[0]], writes=[Sm.b[s]])
            kn = [rP(), rP()]
            kd = [rP(), rP()]
            for hh in range(4):
                hp, h2 = divmod(hh, 2)
                o = P.t[:, kn[hp], h2 * 256:h2 * 256 + 129]
                S.op("pe", lambda e, hh=hh, o=o: e.matmul(out=o, lhsT=Sm.t[:, s, hh, :], rhs=vext.t[:, b, hh, 0:129],
                                                          start=True, stop=False),
                     reads=[Sm.b[s], vext.b[b]], writes=[P.b[kn[hp]]])
                S.op("pe", lambda e, hh=hh, o=o: e.matmul(out=o, lhsT=qkT.t[:, b, 0, hh, :], rhs=Cbf.t[:, hh, 0:129],
                                                          start=False, stop=True),
                     reads=[qkT.b[b], Cbf.b[0]], writes=[P.b[kn[hp]]])
            for hh in range(4):
                hp, h2 = divmod(hh, 2)
                S.op("pe", lambda e, hh=hh, hp=hp, h2=h2: e.matmul(out=P.t[:, kd[hp], h2 * 256:h2 * 256 + 129],
                                                                   lhsT=ks.t[:, b, hh, :], rhs=vext.t[:, b, hh, 0:129],
                                                                   start=True, stop=True),
                     reads=[ks.b[b], vext.b[b]], writes=[P.b[kd[hp]]])
            for hp in range(2):
                S.op("dve", lambda e, hp=hp: e.tensor_tensor(
                    out=Cst.t[:, hp * 2:hp * 2 + 2, :], in0=P.t[:, kd[hp], :].rearrange("p (h x) -> p h x", h=2)[:, :, 0:129],
                    in1=Cst.t[:, hp * 2:hp * 2 + 2, :], op=ALU.add), reads=[P.b[kd[hp]], Cst.b[0]], writes=[Cst.b[0]])
            S.op("dve", lambda e: e.tensor_tensor(out=Cst.t[:], in0=Cst.t[:],
                                                  in1=ebt.t[:, b, 4:8].unsqueeze(2).broadcast_to([128, 4, 129]), op=ALU.mult),
                 reads=[Cst.b[0], ebt.b[b]], writes=[Cst.b[0]])
            S.op("pool", lambda e: e.tensor_copy(out=Cbf.t[:, :, 0:129], in_=Cst.t[:]), reads=[Cst.b[0]], writes=[Cbf.b[0]])
            dn = s
            for hp in range(2):
                S.op("dve", lambda e, hp=hp: e.tensor_copy(
                    out=den.t[:, dn, hp * 2:hp * 2 + 2].unsqueeze(2),
                    in_=P.t[:, kn[hp], :].rearrange("p (h x) -> p h x", h=2)[:, :, 128:129]),
                     reads=[P.b[kn[hp]]], writes=[den.b[dn]])
            S.op("dve", lambda e: e.scalar_tensor_tensor(out=den.t[:, dn, 4:8], in0=den.t[:, dn, 0:4], scalar=-1.0,
                                                         in1=den.t[:, dn, 0:4], op0=ALU.mult, op1=ALU.max),
                 reads=[den.b[dn]], writes=[den.b[dn]])
            S.op("dve", lambda e: e.tensor_scalar(out=den.t[:, dn, 0:4], in0=den.t[:, dn, 4:8], scalar1=1.0, scalar2=None,
                                                  op0=ALU.max), reads=[den.b[dn]], writes=[den.b[dn]])
            S.op("dve", lambda e: e.reciprocal(out=den.t[:, dn, 4:8], in_=den.t[:, dn, 0:4]), reads=[den.b[dn]], writes=[den.b[dn]])
            h = rH()
            sh[c] = h
            for hp in range(2):
                S.op("dve", lambda e, hp=hp: e.tensor_tensor(
                    out=hd.t[:, h, hp * 2:hp * 2 + 2, :],
                    in0=P.t[:, kn[hp], :].rearrange("p (h x) -> p h x", h=2)[:, :, 0:128],
                    in1=den.t[:, dn, 4 + hp * 2:6 + hp * 2].unsqueeze(2).broadcast_to([128, 2, 128]), op=ALU.mult),
                     reads=[P.b[kn[hp]], den.b[dn]], writes=[hd.b[h]])
            if dbg and c == 0:
                S.dma("sync", dbg["Sm"], Sm.t[:, s], reads=[Sm.b[s]])
                S.dma("sync", dbg["den"], den.t[:, dn], reads=[den.b[dn]])
                S.dma("sync", dbg["Cst"], Cst.t[:], reads=[Cst.b[0]])
            if not bwd:
                S.dma("sync", hf[c * 128:(c + 1) * 128, :], hd.t[:, h].rearrange("p h d -> p (h d)"), reads=[hd.b[h]])

        rD = Rot(2)

        def stageD(c):
            ti, m = divmod(c, 4)
            a, sl = tiles_slot[ti]
            h = sh[c]
            d = rD()
            t0 = c * 128
            S.dma("sync", hfl.t[:, d, :], hf[t0:t0 + 128, :], writes=[hfl.b[d]])
            S.dma("sync", sgl.t[:, d, :], sog[t0:t0 + 128, :], writes=[sgl.b[d]])
            S.dma("sync", ycT.t[:, d, 0:4, :], ysguT.rearrange("(h p) t -> p h t", p=128)[:, :, t0:t0 + 128], writes=[ycT.b[d]])
            S.dma("sync", xr.t[:, d, :], xin[t0:t0 + 128, :], writes=[xr.b[d]])
            S.op("pool", lambda e: e.tensor_tensor(out=hfl.t[:, d, :], in0=hfl.t[:, d, :],
                                                   in1=hd.t[:, h].rearrange("p h d -> p (h d)"), op=ALU.add),
                 reads=[hfl.b[d], hd.b[h]], writes=[hfl.b[d]])
            for hh in range(4):
                S.op("dve", lambda e, hh=hh: e.bn_stats(out=bst.t[:, d, hh, :], in_=hfl.t[:, d, hh * 128:(hh + 1) * 128]),
                     reads=[hfl.b[d]], writes=[bst.b[d]])
            for hh in range(4):
                S.op("dve", lambda e, hh=hh: e.bn_aggr(out=mv.t[:, d, hh, :], in_=bst.t[:, d, hh, :]),
                     reads=[bst.b[d]], writes=[mv.b[d]])
            rstd_ops(S, rs.t[:, d, :], mv.t[:, d, :, 1], pw.t[:], mv.b[d], rs.b[d])
            for hh in range(4):
                S.op("dve", lambda e, hh=hh: e.tensor_scalar(
                    out=hn.t[:, d, hh * 128:(hh + 1) * 128], in0=hfl.t[:, d, hh * 128:(hh + 1) * 128],
                    scalar1=mv.t[:, d, hh, 0:1], scalar2=rs.t[:, d, hh:hh + 1], op0=ALU.subtract, op1=ALU.mult),
                     reads=[hfl.b[d], mv.b[d], rs.b[d]], writes=[hn.b[d]])
            S.op("pool", lambda e: e.tensor_tensor(out=hn.t[:, d, :], in0=hn.t[:, d, :], in1=mhg.t[:], op=ALU.mult),
                 reads=[hn.b[d], mhg.b[0]], writes=[hn.b[d]])
            kx = rP()
            for hh in range(4):
                S.op("pe", lambda e, hh=hh: e.matmul(out=P.t[:, kx, hh * 128:(hh + 1) * 128],
                                                     lhsT=xcT.t[:, sl, hh, m * 128:(m + 1) * 128], rhs=sdiag.t[:, hh, :],
                                                     start=True, stop=True), reads=[xcT.b[sl], sdiag.b[0]], writes=[P.b[kx]])
            S.op("dve", lambda e: e.tensor_tensor(out=hn.t[:, d, :], in0=P.t[:, kx, :], in1=hn.t[:, d, :], op=ALU.add),
                 reads=[P.b[kx], hn.b[d]], writes=[hn.b[d]])
            S.op("pool", lambda e: e.tensor_tensor(out=ym.t[:, d, :], in0=hn.t[:, d, :], in1=sgl.t[:, d, :], op=ALU.mult),
                 reads=[hn.b[d], sgl.b[d]], writes=[ym.b[d]])
            kt = rP()
            ptv = P.t[:, kt, :].bitcast(BF16)
            for hh in range(4):
                S.op("pe", lambda e, hh=hh: e.transpose(out=ptv[:, hh * 128:(hh + 1) * 128], in_=ym.t[:, d, hh * 128:(hh + 1) * 128],
                                                        identity=ident.t[:]), reads=[ym.b[d], ident.b[0]], writes=[P.b[kt]])
            S.op("act", lambda e: e.activation(out=ycT.t[:, d, 4:8, :].rearrange("p h t -> p (h t)"), in_=ptv[:, 0:512], func=AF.Copy),
                 reads=[P.b[kt]], writes=[ycT.b[d]])
            for half in range(2):
                ko = rP()
                for cc in range(8):
                    S.op("pe", lambda e, cc=cc, half=half, ko=ko: e.matmul(
                        out=P.t[:, ko, :], lhsT=ycT.t[:, d, cc, :], rhs=wout.t[:, cc, half * 512:(half + 1) * 512],
                        start=(cc == 0), stop=(cc == 7)), reads=[ycT.b[d], wout.b[0]], writes=[P.b[ko]])
                S.op("dve", lambda e, half=half, ko=ko: e.tensor_tensor(
                    out=xr.t[:, d, half * 512:(half + 1) * 512], in0=P.t[:, ko, :], in1=xr.t[:, d, half * 512:(half + 1) * 512],
                    op=ALU.add), reads=[P.b[ko], xr.b[d]], writes=[xr.b[d]])
            S.dma("pool", xout[t0:t0 + 128, :], xr.t[:, d, :], reads=[xr.b[d]])

        order = list(range(NCH))
        if bwd:
            order.reverse()
        done_tiles = set()
        for i in range(NCH + 2):
            for la in (0, 2):
                if i + la < NCH and order[i + la] // 4 not in done_tiles:
                    stageA(order[i + la] // 4)
                    done_tiles.add(order[i + la] // 4)
            if i < NCH:
                stageB(order[i])
            if 1 <= i <= NCH:
                stageC(order[i - 1])
            if bwd and 2 <= i <= NCH + 1:
                stageD(order[i - 2])
        S.emit()


DEPTH = 2
SEQ = 8192
NCORES = 4

PARAM_SHAPES = {
    'ffn1_norm': (DEPTH, D), 'ffn1_w_gate': (DEPTH, D, HID), 'ffn1_w_up': (DEPTH, D, HID), 'ffn1_w_down': (DEPTH, HID, D),
    'mix_norm': (DEPTH, D), 'w_in': (DEPTH, D, 2048), 'sgu_norm': (DEPTH, 4, 128), 'sgu_w': (DEPTH, 4, 128, 128),
    'sgu_b': (DEPTH, 4, 128), 'conv_w': (DEPTH, 5, 512), 'conv_b': (DEPTH, 512), 'w_q': (DEPTH, 128, 4, 4),
    'w_k': (DEPTH, 128, 4, 4), 'w_v': (DEPTH, 128, 4, 4), 'gate_w_fwd': (DEPTH, 1536, 8), 'gate_b_fwd': (DEPTH, 8),
    'gate_w_bwd': (DEPTH, 1536, 8), 'gate_b_bwd': (DEPTH, 8), 'mh_norm': (DEPTH, 4, 128), 'mlstm_skip': (DEPTH, 512),
    'w_out': (DEPTH, D, D), 'ffn2_norm': (DEPTH, D), 'ffn2_w_gate': (DEPTH, D, HID), 'ffn2_w_up': (DEPTH, D, HID),
    'ffn2_w_down': (DEPTH, HID, D), 'final_norm': (D,),
}


def build_program(T=SEQ, depth=DEPTH):
    nc = bass.Bass("TRN2", target_bir_lowering=False)
    x = nc.dram_tensor("x", [T, D], F32, kind="ExternalInput").ap()
    p = {k: nc.dram_tensor(k, list(s), F32, kind="ExternalInput").ap() for k, s in PARAM_SHAPES.items()}
    y = nc.dram_tensor("y", [T, D], F32, kind="ExternalOutput").ap()
    xres = nc.dram_tensor("xres", [T, D], F32, kind="Internal").ap()
    wgu = nc.dram_tensor("wgu", [NJ, 128, 2, 8, 128], BF16, kind="Internal").ap()
    xmT = nc.dram_tensor("xmT", [512, T + 4], F32, kind="Internal").ap()
    sog = nc.dram_tensor("sog", [T, 512], BF16, kind="Internal").ap()
    ysguT = nc.dram_tensor("ysguT", [512, T], BF16, kind="Internal").ap()
    hf = nc.dram_tensor("hf", [T, 512], F32, kind="Internal").ap()
    C = Ctx(nc)
    for l in range(depth):
        last = (l == depth - 1)
        convert_wgu_phase(C, p['ffn1_w_gate'][l], p['ffn1_w_up'][l], wgu)
        ffn_phase(C, T, x if l == 0 else xres, xres, p['ffn1_norm'][l], wgu, p['ffn1_w_down'][l])
        mixa_phase(C, T, xres, p['mix_norm'][l], p['w_in'][l], p['sgu_norm'][l], p['sgu_w'][l], p['sgu_b'][l],
                   xmT, sog, ysguT, AF.Gelu_apprx_tanh)
        mixb_phase(C, T, False, xmT, hf, p['conv_w'][l], p['conv_b'][l], p['w_q'][l], p['w_k'][l], p['w_v'][l],
                   p['gate_w_fwd'][l], p['gate_b_fwd'][l])
        mixb_phase(C, T, True, xmT, hf, p['conv_w'][l], p['conv_b'][l], p['w_q'][l], p['w_k'][l], p['w_v'][l],
                   p['gate_w_bwd'][l], p['gate_b_bwd'][l], sog=sog, ysguT=ysguT, mh_norm=p['mh_norm'][l],
                   skip=p['mlstm_skip'][l], w_out=p['w_out'][l], xin=xres, xout=xres)
        convert_wgu_phase(C, p['ffn2_w_gate'][l], p['ffn2_w_up'][l], wgu)
        ffn_phase(C, T, xres, y if last else xres, p['ffn2_norm'][l], wgu, p['ffn2_w_down'][l],
                  final_g=p['final_norm'] if last else None)
    return nc


def kernel(**inputs):
    x = np.ascontiguousarray(np.asarray(inputs['x'], dtype=np.float32))
    B = x.shape[0]
    params = {k: np.ascontiguousarray(np.asarray(inputs[k], dtype=np.float32)) for k in PARAM_SHAPES}
    nc = build_program()
    in_maps = [dict(params, x=x[b]) for b in range(B)]
    res = run_bass_kernel_spmd(nc, in_maps, core_ids=list(range(B)))
    return np.stack([np.asarray(res.results[b]["y"], dtype=np.float32) for b in range(B)], axis=0)
```

```python
import contextlib
import numpy as np
import ml_dtypes
import concourse.bass as bass
import concourse.mybir as mybir
from concourse.bass_utils import run_bass_kernel_spmd

F32 = mybir.dt.float32
BF16 = mybir.dt.bfloat16
AF = mybir.ActivationFunctionType
ALU = mybir.AluOpType

D = 1024
HID = 2816
NJ = HID // 128
EPS = 1e-6
ENGS = ("sync", "act", "dve", "pool", "pe")


class Buf:
    __slots__ = ("w", "r")

    def __init__(self):
        self.w = None
        self.r = []


class Op:
    __slots__ = ("eng", "fn", "deps", "ev", "dma")


class DmaPool:
    def __init__(self, nc, eng, n):
        self.slots = [[nc.alloc_semaphore(name=f"dq_{eng}_{i}"), 0, None] for i in range(n)]
        self.i = 0


class Sched:
    def __init__(self, nc, pools, tag):
        self.nc = nc
        self.pools = pools
        self.lists = {e: [] for e in ENGS}
        self.esem = {e: nc.alloc_semaphore(name=f"e_{tag}_{e}") for e in ENGS if e != "sync"}
        self.ecnt = {e: 0 for e in ENGS}

    def _deps(self, reads, writes):
        deps = []
        for b in reads:
            if b.w is not None:
                deps.append(b.w)
        for b in writes:
            if b.w is not None:
                deps.append(b.w)
            deps.extend(b.r)
        return deps

    def _commit(self, o, reads, writes):
        for b in reads:
            b.r.append(o)
        for b in writes:
            b.w = o
            b.r = []
        self.lists[o.eng].append(o)

    def op(self, eng, fn, reads=(), writes=()):
        o = Op()
        o.eng, o.fn, o.dma = eng, fn, False
        o.deps = self._deps(reads, writes)
        self.ecnt[eng] += 1
        o.ev = (self.esem[eng], self.ecnt[eng])
        self._commit(o, reads, writes)
        return o

    def dma(self, eng, out, in_, reads=(), writes=(), **kw):
        o = Op()
        o.eng, o.dma = eng, True
        o.fn = lambda e: e.dma_start(out=out, in_=in_, **kw)
        o.deps = self._deps(reads, writes)
        pool = self.pools[eng]
        slot = pool.slots[pool.i % len(pool.slots)]
        pool.i += 1
        if slot[2] is not None:
            o.deps.append(slot[2])
        slot[1] += 16
        slot[2] = o
        o.ev = (slot[0], slot[1])
        self._commit(o, reads, writes)
        return o

    def emit(self):
        nc = self.nc
        finals = [(s, c) for e, s in self.esem.items() for c in [self.ecnt[e]] if c > 0]
        for p in self.pools.values():
            for s, c, last in p.slots:
                if c > 0:
                    finals.append((s, c))
        names = {"sync": "sync", "act": "scalar", "dve": "vector", "pool": "gpsimd", "pe": "tensor"}
        with nc.Block() as blk:
            for eng in ENGS:
                def body(e, eng=eng):
                    waited = {}
                    for o in self.lists[eng]:
                        need = {}
                        for d in o.deps:
                            if d.eng == "pe" and eng == "pe" and not d.dma:
                                continue
                            s, v = d.ev
                            k = id(s)
                            if need.get(k, (None, 0))[1] < v:
                                need[k] = (s, v)
                        for k, (s, v) in need.items():
                            if waited.get(k, 0) >= v:
                                continue
                            e.wait_ge(s, v)
                            waited[k] = v
                        ins = o.fn(e)
                        ins.then_inc(o.ev[0], 16 if o.dma else 1)
                    for s, v in finals:
                        if waited.get(id(s), 0) < v:
                            e.wait_ge(s, v)
                getattr(blk, names[eng])(body)


class Ctx:
    def __init__(self, nc):
        self.nc = nc
        self.pools = {"sync": DmaPool(nc, "sync", 24), "pool": DmaPool(nc, "pool", 12), "act": DmaPool(nc, "act", 6)}
        self.nphase = 0

    def sched(self):
        self.nphase += 1
        return Sched(self.nc, self.pools, f"p{self.nphase}")


class Tile:
    def __init__(self, t, nslots=1):
        self.t = t
        self.b = [Buf() for _ in range(nslots)]


_UID = [0]


def _uname(name):
    _UID[0] += 1
    return f"t{_UID[0]}_{name}"


def sb(es, nc, name, shape, dt, nslots=1):
    t = es.enter_context(nc.sbuf_tensor(_uname(name), [128 if shape[0] is None else shape[0]] + list(shape[1:]), dt))
    return Tile(t, nslots)


def ps(es, nc, name, shape, dt, nslots=1):
    t = es.enter_context(nc.psum_tensor(_uname(name), list(shape), dt))
    return Tile(t, nslots)


def convert_wgu_phase(C, wg, wu, wgu):
    nc = C.nc
    S = C.sched()
    with contextlib.ExitStack() as es:
        wst = sb(es, nc, "wst", [128, 2, 2, 8, 256], F32, 2)
        wbf = sb(es, nc, "wbf", [128, 2, 2, 2, 8, 128], BF16, 2)
        engs = ("dve", "pool", "act", "dve")
        for jp in range(NJ // 2):
            sl = jp % 2
            for a, w in enumerate((wg, wu)):
                src = w.rearrange("(kc p) n -> p kc n", p=128)[:, :, jp * 256:(jp + 1) * 256]
                S.dma("sync", wst.t[:, sl, a, :, :], src, writes=[wst.b[sl]])
            i = 0
            for jj in range(2):
                for a in range(2):
                    eng = engs[i]; i += 1
                    o = wbf.t[:, sl, jj, a, :, :]
                    src = wst.t[:, sl, a, :, jj * 128:(jj + 1) * 128]
                    if eng == "act":
                        S.op("act", lambda e, o=o, src=src: e.activation(out=o, in_=src, func=AF.Copy),
                             reads=[wst.b[sl]], writes=[wbf.b[sl]])
                    else:
                        S.op(eng, lambda e, o=o, src=src: e.tensor_copy(out=o, in_=src),
                             reads=[wst.b[sl]], writes=[wbf.b[sl]])
            S.dma("sync", wgu[jp * 2:jp * 2 + 2].rearrange("j p a k c -> p j (a k c)"),
                  wbf.t[:, sl].rearrange("p j a k c -> p j (a k c)"), reads=[wbf.b[sl]])
        S.emit()


def convert_items(S, es, nc, wg, wu, wgu):
    wst = sb(es, nc, "cwst", [128, 1, 2, 8, 128], F32, 1)
    wbf = sb(es, nc, "cwbf", [128, 1, 2, 8, 128], BF16, 1)
    r = Rot(1)
    items = []
    for j in range(NJ):
        def it(j=j):
            sl = r()
            for a, w in enumerate((wg, wu)):
                src = w.rearrange("(kc p) n -> p kc n", p=128)[:, :, j * 128:(j + 1) * 128]
                S.dma("sync", wst.t[:, sl, a, :, :], src, writes=[wst.b[sl]])
            S.op("act", lambda e: e.activation(out=wbf.t[:, sl, 0], in_=wst.t[:, sl, 0], func=AF.Copy),
                 reads=[wst.b[sl]], writes=[wbf.b[sl]])
            S.op("pool", lambda e: e.tensor_copy(out=wbf.t[:, sl, 1], in_=wst.t[:, sl, 1]),
                 reads=[wst.b[sl]], writes=[wbf.b[sl]])
            S.dma("sync", wgu[j].rearrange("p a k c -> p (a k c)"), wbf.t[:, sl].rearrange("p a k c -> p (a k c)"),
                  reads=[wbf.b[sl]])
        items.append(it)
    return items


def ffn_phase(C, T, xin, xout, gvec, wgu, wd_f32, final_g=None):
    nc = C.nc
    S = C.sched()
    NG = T // 1024
    with contextlib.ExitStack() as es:
        ident = sb(es, nc, "ident", [128, 128], BF16)
        gb = sb(es, nc, "gb", [128, D], F32)
        gfb = sb(es, nc, "gfb", [128, D], F32)
        pw = sb(es, nc, "pw", [128, 1], F32)
        wd = sb(es, nc, "wd", [128, NJ, D], BF16)
        wgus = sb(es, nc, "wgus", [128, 3, 2, 2, 8, 128], BF16, 3)
        xprep = sb(es, nc, "xprep", [128, 4, D], F32, 4)
        xres = sb(es, nc, "xres", [128, 3, D], F32, 3)
        xn = sb(es, nc, "xn", [128, 2, D], BF16, 2)
        stat = sb(es, nc, "stat", [128, 8, 4], F32, 8)
        xnT = sb(es, nc, "xnT", [128, 2, 8, 1024], BF16, 2)
        aT = sb(es, nc, "aT", [128, NJ, 1024], BF16, 1)
        sg = sb(es, nc, "sg", [128, 2, 512], BF16, 2)
        junk = sb(es, nc, "junk", [128, D], BF16, 1)
        pg = ps(es, nc, "pg", [128, 2, 512], F32, 2)
        pu = ps(es, nc, "pu", [128, 2, 512], F32, 2)
        pt = ps(es, nc, "pt", [128, 2, 1024], BF16, 2)
        po = ps(es, nc, "po", [128, 2, 512], F32, 2)

        S.op("pool", lambda e: e.memset(ident.t[:], 0.0), writes=[ident.b[0]])
        S.op("pool", lambda e: e.affine_select(out=ident.t[:], in_=ident.t[:], pattern=[[-1, 128]],
                                               compare_op=ALU.not_equal, fill=1.0, base=0, channel_multiplier=1),
             reads=[ident.b[0]], writes=[ident.b[0]])
        S.op("pool", lambda e: e.memset(pw.t[:], -0.5), writes=[pw.b[0]])
        S.dma("sync", gb.t[:], gvec.partition_broadcast(128), writes=[gb.b[0]])
        if final_g is not None:
            S.dma("sync", gfb.t[:], final_g.partition_broadcast(128), writes=[gfb.b[0]])
        wdsrc = wd_f32.rearrange("(j p) n -> p j n", p=128)
        for j0 in range(0, NJ, 2):
            S.dma("pool", wd.t[:, j0:j0 + 2, :], wdsrc[:, j0:j0 + 2, :], writes=[wd.b[0]])

        cnt = {"xp": 0, "xn": 0, "st": 0, "pt": 0, "wg": 0, "g": 0, "sg": 0, "po": 0, "xr": 0}

        def prep_items(g):
            fronts, backs = [], []
            slot = g % 2
            for s in range(8):
                def it(s=s):
                    i = cnt["xp"]; cnt["xp"] += 1
                    k = i % 4
                    tok0 = (g * 8 + s) * 128
                    xp = xprep.t[:, k, :]
                    S.dma("sync", xp, xin[tok0:tok0 + 128, :], writes=[xprep.b[k]])
                    q = cnt["st"] % 8; cnt["st"] += 1
                    ss = stat.t[:, q, 0:1]
                    rstd = stat.t[:, q, 1:2]
                    S.op("dve", lambda e: e.scalar_tensor_tensor(out=junk.t[:], in0=xp, scalar=1.0, in1=xp,
                                                                 op0=ALU.mult, op1=ALU.mult, accum_out=ss),
                         reads=[xprep.b[k]], writes=[junk.b[0], stat.b[q]])
                    S.op("pool", lambda e: e.tensor_scalar(out=rstd, in0=ss, scalar1=1.0 / D, scalar2=EPS,
                                                           op0=ALU.mult, op1=ALU.add),
                         reads=[stat.b[q]], writes=[stat.b[q]])
                    S.op("pool", lambda e: e.tensor_tensor(out=rstd, in0=rstd, in1=pw.t[:], op=ALU.pow),
                         reads=[stat.b[q], pw.b[0]], writes=[stat.b[q]])
                    n = cnt["xn"] % 2; cnt["xn"] += 1
                    S.op("dve", lambda e: e.scalar_tensor_tensor(out=xn.t[:, n, :], in0=xp, scalar=rstd, in1=gb.t[:],
                                                                 op0=ALU.mult, op1=ALU.mult),
                         reads=[xprep.b[k], stat.b[q], gb.b[0]], writes=[xn.b[n]])
                    return n

                def bk(s=s, n=None):
                    p = cnt["pt"] % 2; cnt["pt"] += 1
                    for kc in range(8):
                        S.op("pe", lambda e, kc=kc: e.transpose(out=pt.t[:, p, kc * 128:(kc + 1) * 128],
                                                                in_=xn.t[:, n, kc * 128:(kc + 1) * 128],
                                                                identity=ident.t[:]),
                             reads=[xn.b[n], ident.b[0]], writes=[pt.b[p]])
                    S.op("act", lambda e: e.activation(
                        out=xnT.t[:, slot, :, s * 128:(s + 1) * 128],
                        in_=pt.t[:, p, :].rearrange("p (k t) -> p k t", k=8), func=AF.Copy),
                         reads=[pt.b[p]], writes=[xnT.b[slot]])
                fronts.append(it)
                backs.append(bk)
            nsl = {}

            def mk_f(i):
                def f():
                    nsl[i] = fronts[i]()
                return f

            def mk_b(i):
                def f():
                    backs[i](n=nsl[i])
                return f
            order = [mk_f(0)]
            for i in range(1, 8):
                order.append(mk_f(i))
                order.append(mk_b(i - 1))
            order.append(mk_b(7))
            return order

        def gateup(g, extra):
            slot = g % 2
            for j in range(NJ):
                if j % 2 == 0:
                    w = cnt["wg"] % 3; cnt["wg"] += 1
                    S.dma("sync", wgus.t[:, w].rearrange("p j a k c -> p j (a k c)"),
                          wgu[j:j + 2].rearrange("j p a k c -> p j (a k c)"), writes=[wgus.b[w]])
                    wcur = w
                jj = j % 2
                for half in range(2):
                    q = cnt["g"] % 2; cnt["g"] += 1
                    for (pp, a) in ((pg, 0), (pu, 1)):
                        for kc in range(8):
                            S.op("pe", lambda e, pp=pp, a=a, kc=kc, q=q, half=half, wcur=wcur, jj=jj: e.matmul(
                                out=pp.t[:, q, :], lhsT=wgus.t[:, wcur, jj, a, kc, :],
                                rhs=xnT.t[:, slot, kc, half * 512:(half + 1) * 512],
                                start=(kc == 0), stop=(kc == 7)),
                                 reads=[wgus.b[wcur], xnT.b[slot]], writes=[pp.b[q]])
                    r = cnt["sg"] % 2; cnt["sg"] += 1
                    S.op("act", lambda e, q=q, r=r: e.activation(out=sg.t[:, r, :], in_=pg.t[:, q, :], func=AF.Silu),
                         reads=[pg.b[q]], writes=[sg.b[r]])
                    S.op("dve", lambda e, q=q, r=r, j=j, half=half: e.tensor_tensor(
                        out=aT.t[:, j, half * 512:(half + 1) * 512], in0=pu.t[:, q, :], in1=sg.t[:, r, :], op=ALU.mult),
                         reads=[pu.b[q], sg.b[r]], writes=[aT.b[0]])
                    if extra:
                        extra.pop(0)()

        def down(g, extra):
            for m in range(8):
                tok0 = (g * 8 + m) * 128
                x = cnt["xr"] % 3; cnt["xr"] += 1
                S.dma("sync", xres.t[:, x, :], xin[tok0:tok0 + 128, :], writes=[xres.b[x]])
                for n in range(2):
                    q = cnt["po"] % 2; cnt["po"] += 1
                    for j in range(NJ):
                        S.op("pe", lambda e, j=j, q=q, m=m, n=n: e.matmul(
                            out=po.t[:, q, :], lhsT=aT.t[:, j, m * 128:(m + 1) * 128],
                            rhs=wd.t[:, j, n * 512:(n + 1) * 512], start=(j == 0), stop=(j == NJ - 1)),
                             reads=[aT.b[0], wd.b[0]], writes=[po.b[q]])
                    S.op("dve", lambda e, q=q, x=x, n=n: e.scalar_tensor_tensor(
                        out=xres.t[:, x, n * 512:(n + 1) * 512], in0=po.t[:, q, :], scalar=0.5,
                        in1=xres.t[:, x, n * 512:(n + 1) * 512], op0=ALU.mult, op1=ALU.add),
                         reads=[po.b[q], xres.b[x]], writes=[xres.b[x]])
                if final_g is not None:
                    qs = cnt["st"] % 8; cnt["st"] += 1
                    ss = stat.t[:, qs, 0:1]
                    rstd = stat.t[:, qs, 1:2]
                    xr = xres.t[:, x, :]
                    S.op("dve", lambda e, xr=xr, ss=ss: e.scalar_tensor_tensor(
                        out=junk.t[:], in0=xr, scalar=1.0, in1=xr, op0=ALU.mult, op1=ALU.mult, accum_out=ss),
                         reads=[xres.b[x]], writes=[junk.b[0], stat.b[qs]])
                    S.op("pool", lambda e, ss=ss, rstd=rstd: e.tensor_scalar(
                        out=rstd, in0=ss, scalar1=1.0 / D, scalar2=EPS, op0=ALU.mult, op1=ALU.add),
                         reads=[stat.b[qs]], writes=[stat.b[qs]])
                    S.op("pool", lambda e, rstd=rstd: e.tensor_tensor(out=rstd, in0=rstd, in1=pw.t[:], op=ALU.pow),
                         reads=[stat.b[qs], pw.b[0]], writes=[stat.b[qs]])
                    S.op("dve", lambda e, xr=xr, rstd=rstd: e.scalar_tensor_tensor(
                        out=xr, in0=xr, scalar=rstd, in1=gfb.t[:], op0=ALU.mult, op1=ALU.mult),
                         reads=[xres.b[x], stat.b[qs], gfb.b[0]], writes=[xres.b[x]])
                S.dma("pool", xout[tok0:tok0 + 128, :], xres.t[:, x, :], reads=[xres.b[x]])
                if extra:
                    extra.pop(0)()

        for it in prep_items(0):
            it()
        for g in range(NG):
            nxt = prep_items(g + 1) if g + 1 < NG else []
            gateup(g, nxt)
            down(g, nxt)
            while nxt:
                nxt.pop(0)()
        S.emit()


class Rot:
    def __init__(self, n):
        self.n, self.i = n, 0

    def __call__(self):
        k = self.i % self.n
        self.i += 1
        return k


def make_ident(S, tile, dt_is_f32=False):
    S.op("pool", lambda e: e.memset(tile.t[:], 0.0), writes=[tile.b[0]])
    S.op("pool", lambda e: e.affine_select(out=tile.t[:], in_=tile.t[:], pattern=[[-1, 128]],
                                           compare_op=ALU.not_equal, fill=1.0, base=0, channel_multiplier=1),
         reads=[tile.b[0]], writes=[tile.b[0]])


def rstd_ops(S, out, in_, pwt, b_in, b_out, scale=1.0):
    S.op("pool", lambda e: e.tensor_scalar(out=out, in0=in_, scalar1=scale, scalar2=EPS, op0=ALU.mult, op1=ALU.add),
         reads=[b_in], writes=[b_out])
    S.op("pool", lambda e: e.tensor_tensor(out=out, in0=out, in1=pwt, op=ALU.pow), reads=[b_out], writes=[b_out])


GELU_C = 0.7978845608028654


def mixa_phase(C, T, xin, gvec, w_in, sgu_norm, sgu_w, sgu_b, xmT_pad, sog, ysguT, gelu_fn):
    nc = C.nc
    S = C.sched()
    NT = T // 512
    with contextlib.ExitStack() as es:
        es.enter_context(nc.allow_non_contiguous_dma(reason="tiny parameter transposes"))
        ident = sb(es, nc, "ident", [128, 128], BF16)
        identf = sb(es, nc, "identf", [128, 128], F32)
        gb = sb(es, nc, "gb", [128, D], F32)
        pw = sb(es, nc, "pw", [128, 4], F32)
        win = sb(es, nc, "win", [128, 8, 2048], BF16)
        wsf = sb(es, nc, "wsf", [128, 4, 128], F32)
        wsT = sb(es, nc, "wsT", [128, 4, 128], BF16)
        wsb = sb(es, nc, "wsb", [128, 4, 128], BF16)
        gT = sb(es, nc, "gT", [128, 4], F32)
        bsb = sb(es, nc, "bsb", [128, 4, 128], F32)
        zt = sb(es, nc, "zt", [128, 4, 2], F32)
        xprep = sb(es, nc, "xprep", [128, 4, D], F32, 4)
        junk = sb(es, nc, "junk", [128, D], BF16)
        stat = sb(es, nc, "stat", [128, 8, 4], F32, 8)
        xn = sb(es, nc, "xn", [128, 2, D], BF16, 2)
        xnT = sb(es, nc, "xnT", [128, 2, 8, 512], BF16, 2)
        guT = sb(es, nc, "guT", [128, 2, 4, 512], F32, 2)
        xms = sb(es, nc, "xms", [128, 2, 4, 512], F32, 2)
        gv = sb(es, nc, "gv", [128, 2, 512], F32, 2)
        bst = sb(es, nc, "bst", [128, 4, 4, 6], F32, 4)
        mv = sb(es, nc, "mv", [128, 4, 4, 2], F32, 4)
        rs = sb(es, nc, "rs", [128, 4, 4], F32, 4)
        vhn = sb(es, nc, "vhn", [128, 2, 4, 4, 128], BF16, 2)
        sgs = sb(es, nc, "sgs", [128, 2, 512], BF16, 2)
        tmp = sb(es, nc, "tmp", [128, 2, 512], F32, 2)
        ys = sb(es, nc, "ys", [128, 2, 4, 512], BF16, 2)
        gtmp = sb(es, nc, "gtmp", [128, 2, 512], F32, 2)
        P = ps(es, nc, "P", [128, 8, 512], F32, 8)
        rP = Rot(8)

        make_ident(S, ident)
        make_ident(S, identf)
        S.op("pool", lambda e: e.memset(pw.t[:], -0.5), writes=[pw.b[0]])
        S.op("pool", lambda e: e.memset(zt.t[:], 0.0), writes=[zt.b[0]])
        S.dma("sync", gb.t[:], gvec.partition_broadcast(128), writes=[gb.b[0]])
        S.dma("pool", win.t[:, 0:4, :], w_in.rearrange("(kc p) n -> p kc n", p=128)[:, 0:4, :], writes=[win.b[0]])
        S.dma("pool", win.t[:, 4:8, :], w_in.rearrange("(kc p) n -> p kc n", p=128)[:, 4:8, :], writes=[win.b[0]])
        S.dma("sync", wsf.t[:], sgu_w.rearrange("h p q -> p h q"), writes=[wsf.b[0]])
        S.dma("sync", gT.t[:], sgu_norm.rearrange("h d -> d h"), writes=[gT.b[0]])
        S.dma("sync", bsb.t[:].rearrange("p h q -> p (h q)"), sgu_b.rearrange("h q -> (h q)").partition_broadcast(128),
              writes=[bsb.b[0]])
        xmv = xmT_pad.rearrange("(cc p) t -> p cc t", p=128)
        S.dma("sync", xmv[:, :, 0:2], zt.t[:], reads=[zt.b[0]])
        S.dma("sync", xmv[:, :, T + 2:T + 4], zt.t[:], reads=[zt.b[0]])
        S.op("pool", lambda e: e.tensor_copy(out=wsb.t[:], in_=wsf.t[:]), reads=[wsf.b[0]], writes=[wsb.b[0]])
        for hh in range(4):
            k = rP()
            pv = P.t[:, k, :].bitcast(BF16)
            S.op("pe", lambda e, hh=hh, pv=pv: e.transpose(out=pv[:, 0:128], in_=wsb.t[:, hh, :], identity=ident.t[:]),
                 reads=[wsb.b[0], ident.b[0]], writes=[P.b[k]])
            S.op("dve", lambda e, hh=hh, pv=pv: e.tensor_copy(out=wsT.t[:, hh, :], in_=pv[:, 0:128]),
                 reads=[P.b[k]], writes=[wsT.b[0]])

        rxp, rst, rxn = Rot(4), Rot(8), Rot(2)

        def gelu(out, in_ap, in_buf, out_buf, n):
            if gelu_fn is not None:
                S.op("act", lambda e: e.activation(out=out, in_=in_ap, func=gelu_fn), reads=[in_buf], writes=[out_buf])
                return
            k = rgt()
            t = gtmp.t[:, k, 0:n]
            S.op("act", lambda e: e.activation(out=t, in_=in_ap, func=AF.Square), reads=[in_buf], writes=[gtmp.b[k]])
            S.op("pool", lambda e: e.tensor_scalar(out=t, in0=t, scalar1=0.044715, scalar2=1.0, op0=ALU.mult, op1=ALU.add),
                 reads=[gtmp.b[k]], writes=[gtmp.b[k]])
            S.op("dve", lambda e: e.tensor_tensor(out=t, in0=in_ap, in1=t, op=ALU.mult), reads=[in_buf, gtmp.b[k]],
                 writes=[gtmp.b[k]])
            S.op("act", lambda e: e.activation(out=t, in_=t, func=AF.Sigmoid, scale=2.0 * GELU_C),
                 reads=[gtmp.b[k]], writes=[gtmp.b[k]])
            S.op("dve", lambda e: e.tensor_tensor(out=out, in0=in_ap, in1=t, op=ALU.mult), reads=[in_buf, gtmp.b[k]],
                 writes=[out_buf])
        rgt = Rot(2)

        def prep(ti):
            slot = ti % 2
            for s in range(4):
                tok0 = ti * 512 + s * 128
                k = rxp()
                xp = xprep.t[:, k, :]
                S.dma("sync", xp, xin[tok0:tok0 + 128, :], writes=[xprep.b[k]])
                q = rst()
                ss, rstd = stat.t[:, q, 0:1], stat.t[:, q, 1:2]
                S.op("dve", lambda e, xp=xp, ss=ss: e.scalar_tensor_tensor(
                    out=junk.t[:], in0=xp, scalar=1.0, in1=xp, op0=ALU.mult, op1=ALU.mult, accum_out=ss),
                     reads=[xprep.b[k]], writes=[junk.b[0], stat.b[q]])
                rstd_ops(S, rstd, ss, pw.t[:, 0:1], stat.b[q], stat.b[q], 1.0 / D)
                n = rxn()
                S.op("dve", lambda e, xp=xp, rstd=rstd, n=n: e.scalar_tensor_tensor(
                    out=xn.t[:, n, :], in0=xp, scalar=rstd, in1=gb.t[:], op0=ALU.mult, op1=ALU.mult),
                     reads=[xprep.b[k], stat.b[q], gb.b[0]], writes=[xn.b[n]])
                p = rP()
                ptv = P.t[:, p, :].bitcast(BF16)
                for kc in range(8):
                    S.op("pe", lambda e, kc=kc, n=n, ptv=ptv: e.transpose(
                        out=ptv[:, kc * 128:(kc + 1) * 128], in_=xn.t[:, n, kc * 128:(kc + 1) * 128], identity=ident.t[:]),
                         reads=[xn.b[n], ident.b[0]], writes=[P.b[p]])
                S.op("act", lambda e, s=s, ptv=ptv: e.activation(
                    out=xnT.t[:, slot, :, s * 128:(s + 1) * 128], in_=ptv.rearrange("p (k t) -> p k t", k=8), func=AF.Copy),
                     reads=[P.b[p]], writes=[xnT.b[slot]])

        def body(ti):
            slot = ti % 2
            tok0 = ti * 512
            for cc in range(4):
                k = rP()
                for kc in range(8):
                    S.op("pe", lambda e, cc=cc, kc=kc, k=k: e.matmul(
                        out=P.t[:, k, :], lhsT=win.t[:, kc, cc * 128:(cc + 1) * 128], rhs=xnT.t[:, slot, kc, :],
                        start=(kc == 0), stop=(kc == 7)), reads=[win.b[0], xnT.b[slot]], writes=[P.b[k]])
                gelu(guT.t[:, slot, cc, :], P.t[:, k, :], P.b[k], guT.b[slot], 512)
            for cc in range(4):
                k = rP()
                for kc in range(8):
                    S.op("pe", lambda e, cc=cc, kc=kc, k=k: e.matmul(
                        out=P.t[:, k, :], lhsT=win.t[:, kc, 1024 + cc * 128:1024 + (cc + 1) * 128],
                        rhs=xnT.t[:, slot, kc, :], start=(kc == 0), stop=(kc == 7)),
                         reads=[win.b[0], xnT.b[slot]], writes=[P.b[k]])
                S.op("act", lambda e, cc=cc, k=k: e.activation(out=xms.t[:, slot, cc, :], in_=P.t[:, k, :], func=AF.Copy),
                     reads=[P.b[k]], writes=[xms.b[slot]])
            S.dma("sync", xmv[:, :, 2 + tok0:2 + tok0 + 512], xms.t[:, slot], reads=[xms.b[slot]])
            for m in range(4):
                k = rP()
                for kc in range(8):
                    S.op("pe", lambda e, m=m, kc=kc, k=k: e.matmul(
                        out=P.t[:, k, :], lhsT=xnT.t[:, slot, kc, m * 128:(m + 1) * 128], rhs=win.t[:, kc, 512:1024],
                        start=(kc == 0), stop=(kc == 7)), reads=[win.b[0], xnT.b[slot]], writes=[P.b[k]])
                v = m % 2
                gelu(gv.t[:, v, :], P.t[:, k, :], P.b[k], gv.b[v], 512)
                for hh in range(4):
                    S.op("dve", lambda e, hh=hh, v=v: e.bn_stats(out=bst.t[:, v, hh, :], in_=gv.t[:, v, hh * 128:(hh + 1) * 128]),
                         reads=[gv.b[v]], writes=[bst.b[v]])
                for hh in range(4):
                    S.op("dve", lambda e, hh=hh, v=v: e.bn_aggr(out=mv.t[:, v, hh, :], in_=bst.t[:, v, hh, :]),
                         reads=[bst.b[v]], writes=[mv.b[v]])
                rstd_ops(S, rs.t[:, v, :], mv.t[:, v, :, 1], pw.t[:], mv.b[v], rs.b[v])
                for hh in range(4):
                    S.op("dve", lambda e, hh=hh, v=v, m=m: e.tensor_scalar(
                        out=vhn.t[:, slot, m, hh, :], in0=gv.t[:, v, hh * 128:(hh + 1) * 128],
                        scalar1=mv.t[:, v, hh, 0:1], scalar2=rs.t[:, v, hh:hh + 1], op0=ALU.subtract, op1=ALU.mult),
                         reads=[gv.b[v], mv.b[v], rs.b[v]], writes=[vhn.b[slot]])
                k = rP()
                for kc in range(8):
                    S.op("pe", lambda e, m=m, kc=kc, k=k: e.matmul(
                        out=P.t[:, k, :], lhsT=xnT.t[:, slot, kc, m * 128:(m + 1) * 128], rhs=win.t[:, kc, 1536:2048],
                        start=(kc == 0), stop=(kc == 7)), reads=[win.b[0], xnT.b[slot]], writes=[P.b[k]])
                S.op("act", lambda e, k=k, v=v: e.activation(out=sgs.t[:, v, :], in_=P.t[:, k, :], func=AF.Sigmoid),
                     reads=[P.b[k]], writes=[sgs.b[v]])
                S.dma("sync", sog[tok0 + m * 128:tok0 + (m + 1) * 128, :], sgs.t[:, v, :], reads=[sgs.b[v]])
            for hh in range(4):
                k = rP()
                for m in range(4):
                    S.op("pe", lambda e, hh=hh, m=m, k=k: e.matmul(
                        out=P.t[:, k, m * 128:(m + 1) * 128], lhsT=vhn.t[:, slot, m, hh, :], rhs=wsT.t[:, hh, :],
                        start=True, stop=True), reads=[vhn.b[slot], wsT.b[0]], writes=[P.b[k]])
                tq = hh % 2
                S.op("dve", lambda e, hh=hh, k=k, tq=tq: e.scalar_tensor_tensor(
                    out=tmp.t[:, tq, :].rearrange("p (m q) -> p m q", m=4), in0=P.t[:, k, :].rearrange("p (m q) -> p m q", m=4),
                    scalar=gT.t[:, hh:hh + 1], in1=bsb.t[:, hh:hh + 1, :].broadcast_to([128, 4, 128]),
                    op0=ALU.mult, op1=ALU.add), reads=[P.b[k], gT.b[0], bsb.b[0]], writes=[tmp.b[tq]])
                S.op("pool", lambda e, hh=hh, tq=tq: e.tensor_tensor(
                    out=ys.t[:, slot, hh, :], in0=tmp.t[:, tq, :], in1=guT.t[:, slot, hh, :], op=ALU.mult),
                     reads=[tmp.b[tq], guT.b[slot]], writes=[ys.b[slot]])
            S.dma("sync", ysguT.rearrange("(h p) t -> p h t", p=128)[:, :, tok0:tok0 + 512], ys.t[:, slot],
                  reads=[ys.b[slot]])

        prep(0)
        for ti in range(NT):
            if ti + 1 < NT:
                prep(ti + 1)
            body(ti)
        S.emit()


def mixb_phase(C, T, bwd, xmT_pad, hf, conv_w, conv_b, w_q, w_k, w_v, gate_w, gate_b,
               sog=None, ysguT=None, mh_norm=None, skip=None, w_out=None, xin=None, xout=None, dbg=None, conv_jobs=()):
    nc = C.nc
    S = C.sched()
    NCH = T // 128
    NT = T // 512
    with contextlib.ExitStack() as es:
        es.enter_context(nc.allow_non_contiguous_dma(reason="tiny parameter transposes"))
        ident = sb(es, nc, "ident", [128, 128], BF16)
        identf = sb(es, nc, "identf", [128, 128], F32)
        maskf = sb(es, nc, "maskf", [128, 128], F32)
        onesf = sb(es, nc, "onesf", [128, 128], F32)
        mbd = sb(es, nc, "mbd", [128, 32], F32)
        pw = sb(es, nc, "pw", [128, 4], F32)
        cwT = sb(es, nc, "cwT", [128, 4, 5], F32)
        cbT = sb(es, nc, "cbT", [128, 4], F32)
        cdiag = sb(es, nc, "cdiag", [128, 4, 5, 128], BF16)
        wl = sb(es, nc, "wl", [128, 3, 4, 4], F32)
        bdf = sb(es, nc, "bdf", [128, 3, 4, 128], F32)
        bd = sb(es, nc, "bd", [128, 3, 4, 128], BF16)
        bdT = sb(es, nc, "bdT", [128, 3, 4, 128], BF16)
        gwb = sb(es, nc, "gwb", [128, 12, 8], BF16)
        maskb = sb(es, nc, "maskb", [128, 2, 128], BF16)
        lh = sb(es, nc, "lh", [128, 8, 2, 4], BF16, 8)
        gw = sb(es, nc, "gw", [128, 12, 8], F32)
        Gf = sb(es, nc, "Gf", [128, 2, 4, 8], BF16)
        gbb = sb(es, nc, "gbb", [128, 8], F32)
        xmf = sb(es, nc, "xmf", [128, 2, 4, 516], F32, 2)
        xmb = sb(es, nc, "xmb", [128, 3, 4, 516], BF16, 3)
        xcT = sb(es, nc, "xcT", [128, 4, 4, 512], BF16, 4)
        gsb = sb(es, nc, "gsb", [128, 8, 8], F32, 8)
        lfn = sb(es, nc, "lfn", [128, 8, 8], F32, 8)
        ebt = sb(es, nc, "ebt", [128, 8, 12], F32, 8)
        qs = sb(es, nc, "qs", [128, 4, 4, 128], BF16, 4)
        ks = sb(es, nc, "ks", [128, 6, 4, 128], BF16, 6)
        vext = sb(es, nc, "vext", [128, 6, 4, 130], BF16, 6)
        qkT = sb(es, nc, "qkT", [128, 5, 2, 4, 128], BF16, 5)
        Sm = sb(es, nc, "Sm", [128, 3, 4, 128], BF16, 3)
        Cst = sb(es, nc, "Cst", [128, 4, 129], F32)
        Cbf = sb(es, nc, "Cbf", [128, 4, 130], BF16)
        den = sb(es, nc, "den", [128, 3, 8], F32, 3)
        hd = sb(es, nc, "hd", [128, 4, 4, 128], F32, 4)
        P = ps(es, nc, "P", [128, 8, 512], F32, 8)
        rP = Rot(8)
        if bwd:
            mhg = sb(es, nc, "mhg", [128, 512], F32)
            skT = sb(es, nc, "skT", [128, 4], F32)
            sdiag = sb(es, nc, "sdiag", [128, 4, 128], BF16)
            wout = sb(es, nc, "wout", [128, 8, D], BF16)
            hfl = sb(es, nc, "hfl", [128, 3, 512], F32, 3)
            sgl = sb(es, nc, "sgl", [128, 4, 512], BF16, 4)
            ycT = sb(es, nc, "ycT", [128, 4, 8, 128], BF16, 4)
            xr = sb(es, nc, "xr", [128, 3, D], F32, 3)
            bst = sb(es, nc, "bst", [128, 4, 4, 6], F32, 4)
            mv = sb(es, nc, "mv", [128, 4, 4, 2], F32, 4)
            rs = sb(es, nc, "rs", [128, 4, 4], F32, 4)
            hn = sb(es, nc, "hn", [128, 3, 512], F32, 3)
            ym = sb(es, nc, "ym", [128, 4, 512], BF16, 4)

        conv_items = []
        for (cwg, cwu, cdst) in conv_jobs:
            conv_items.extend(convert_items(S, es, nc, cwg, cwu, cdst))
        conv_per_step = 1 if conv_items else 0
        make_ident(S, ident)
        make_ident(S, identf)
        S.op("pool", lambda e: e.memset(pw.t[:], -0.5), writes=[pw.b[0]])
        S.op("pool", lambda e: e.memset(onesf.t[:], 1.0), writes=[onesf.b[0]])
        S.op("pool", lambda e: e.memset(maskf.t[:], 1.0), writes=[maskf.b[0]])
        S.op("pool", lambda e: e.affine_select(out=maskf.t[:], in_=maskf.t[:], pattern=[[-1 if bwd else 1, 128]],
                                               compare_op=ALU.is_ge, fill=0.0, base=0,
                                               channel_multiplier=1 if bwd else -1),
             reads=[maskf.b[0]], writes=[maskf.b[0]])
        S.op("pool", lambda e: e.memset(mbd.t[:], 1.0), writes=[mbd.b[0]])
        S.op("pool", lambda e: e.affine_select(out=mbd.t[:], in_=mbd.t[:], pattern=[[-4, 32]], compare_op=ALU.is_ge,
                                               fill=0.0, base=0, channel_multiplier=1),
             reads=[mbd.b[0]], writes=[mbd.b[0]])
        S.op("pool", lambda e: e.affine_select(out=mbd.t[:], in_=mbd.t[:], pattern=[[4, 32]], compare_op=ALU.is_ge,
                                               fill=0.0, base=3, channel_multiplier=-1),
             reads=[mbd.b[0]], writes=[mbd.b[0]])
        for cc in range(4):
            S.dma("sync", cwT.t[:, cc, :], conv_w[:, cc * 128:(cc + 1) * 128].rearrange("j p -> p j"), writes=[cwT.b[0]])
        S.dma("sync", cbT.t[:], conv_b.rearrange("(cc p) -> p cc", p=128), writes=[cbT.b[0]])
        for i, w in enumerate((w_q, w_k, w_v)):
            S.dma("sync", wl.t[:, i, :, :], w.rearrange("(hh g) i o -> (g i) hh o", hh=4), writes=[wl.b[0]])
        S.dma("sync", gw.t[:], gate_w.rearrange("(r p) n -> p r n", p=128), writes=[gw.b[0]])
        S.dma("sync", gbb.t[:], gate_b.partition_broadcast(128), writes=[gbb.b[0]])
        S.op("dve", lambda e: e.tensor_scalar(out=wl.t[:, 1], in0=wl.t[:, 1], scalar1=128.0 ** -0.5, scalar2=None, op0=ALU.mult),
             reads=[wl.b[0]], writes=[wl.b[0]])
        for i in range(3):
            for hh in range(4):
                S.op("dve", lambda e, i=i, hh=hh: e.tensor_tensor(
                    out=bdf.t[:, i, hh, :].rearrange("p (g o) -> p g o", o=4),
                    in0=mbd.t[:].unsqueeze(2).broadcast_to([128, 32, 4]),
                    in1=wl.t[:, i, hh:hh + 1, :].broadcast_to([128, 32, 4]), op=ALU.mult),
                     reads=[mbd.b[0], wl.b[0]], writes=[bdf.b[0]])
        S.op("pool", lambda e: e.tensor_copy(out=bd.t[:], in_=bdf.t[:]), reads=[bdf.b[0]], writes=[bd.b[0]])
        S.op("pool", lambda e: e.tensor_copy(out=gwb.t[:], in_=gw.t[:]), reads=[gw.b[0]], writes=[gwb.b[0]])
        S.op("pool", lambda e: e.tensor_copy(out=maskb.t[:, 0, :], in_=maskf.t[:]), reads=[maskf.b[0]], writes=[maskb.b[0]])
        S.op("pool", lambda e: e.memset(maskb.t[:, 1, :], 1.0), writes=[maskb.b[0]])
        for i in range(3):
            for hh in range(4):
                k = rP()
                pv = P.t[:, k, :].bitcast(BF16)
                S.op("pe", lambda e, i=i, hh=hh, pv=pv: e.transpose(out=pv[:, 0:128], in_=bd.t[:, i, hh, :], identity=ident.t[:]),
                     reads=[bd.b[0], ident.b[0]], writes=[P.b[k]])
                S.op("act", lambda e, i=i, hh=hh, pv=pv: e.activation(out=bdT.t[:, i, hh, :], in_=pv[:, 0:128], func=AF.Copy),
                     reads=[P.b[k]], writes=[bdT.b[0]])
        for cc in range(4):
            for j in range(5):
                S.op("dve", lambda e, cc=cc, j=j: e.tensor_scalar(out=cdiag.t[:, cc, j, :], in0=identf.t[:],
                                                                  scalar1=cwT.t[:, cc, j:j + 1], scalar2=None, op0=ALU.mult),
                     reads=[identf.b[0], cwT.b[0]], writes=[cdiag.b[0]])
        for cc in range(4):
            k = rP()
            S.op("pe", lambda e, cc=cc, k=k: e.matmul(out=P.t[:, k, 0:8], lhsT=bdT.t[:, 0, cc, :], rhs=gwb.t[:, cc, :],
                                                      start=True, stop=False), reads=[bdT.b[0], gwb.b[0]], writes=[P.b[k]])
            S.op("pe", lambda e, cc=cc, k=k: e.matmul(out=P.t[:, k, 0:8], lhsT=bdT.t[:, 1, cc, :], rhs=gwb.t[:, 4 + cc, :],
                                                      start=False, stop=True), reads=[bdT.b[0], gwb.b[0]], writes=[P.b[k]])
            S.op("pe", lambda e, cc=cc, k=k: e.matmul(out=P.t[:, k, 8:16], lhsT=bdT.t[:, 2, cc, :], rhs=gwb.t[:, 8 + cc, :],
                                                      start=True, stop=True), reads=[bdT.b[0], gwb.b[0]], writes=[P.b[k]])
            S.op("dve", lambda e, cc=cc, k=k: e.tensor_copy(out=Gf.t[:, :, cc, :], in_=P.t[:, k, 0:16].rearrange("p (a n) -> p a n", a=2)),
                 reads=[P.b[k]], writes=[Gf.b[0]])
        S.op("pool", lambda e: e.memset(Cst.t[:], 0.0), writes=[Cst.b[0]])
        S.op("pool", lambda e: e.memset(Cbf.t[:], 0.0), writes=[Cbf.b[0]])
        for i in range(6):
            S.op("pool", lambda e, i=i: e.memset(vext.t[:, i, :, 128:130], 1.0), writes=[vext.b[i]])
        if bwd:
            S.dma("sync", mhg.t[:], mh_norm.rearrange("h d -> (h d)").partition_broadcast(128), writes=[mhg.b[0]])
            S.dma("sync", skT.t[:], skip.rearrange("(cc p) -> p cc", p=128), writes=[skT.b[0]])
            for cc in range(4):
                S.op("dve", lambda e, cc=cc: e.tensor_scalar(out=sdiag.t[:, cc, :], in0=identf.t[:], scalar1=skT.t[:, cc:cc + 1],
                                                             scalar2=None, op0=ALU.mult),
                     reads=[identf.b[0], skT.b[0]], writes=[sdiag.b[0]])
            wsrc = w_out.rearrange("(kc p) n -> p kc n", p=128)
            S.dma("pool", wout.t[:, 0:4, :], wsrc[:, 0:4, :], writes=[wout.b[0]])
            S.dma("pool", wout.t[:, 4:8, :], wsrc[:, 4:8, :], writes=[wout.b[0]])

        xmv = xmT_pad.rearrange("(cc p) t -> p cc t", p=128)
        tiles_slot = {}
        rA, rxm, rxf = Rot(4), Rot(3), Rot(2)

        def stageA(ti):
            tok0 = ti * 512
            f = rxf()
            a = rxm()
            S.dma("sync", xmf.t[:, f], xmv[:, :, tok0:tok0 + 516], writes=[xmf.b[f]])
            S.op("pool", lambda e: e.tensor_copy(out=xmb.t[:, a], in_=xmf.t[:, f]), reads=[xmf.b[f]], writes=[xmb.b[a]])
            sl = rA()
            for cc in range(4):
                k = rP()
                for j in range(5):
                    S.op("pe", lambda e, cc=cc, j=j, k=k: e.matmul(out=P.t[:, k, :], lhsT=cdiag.t[:, cc, j, :],
                                                                   rhs=xmb.t[:, a, cc, j:j + 512], start=(j == 0), stop=(j == 4)),
                         reads=[cdiag.b[0], xmb.b[a]], writes=[P.b[k]])
                S.op("act", lambda e, cc=cc, k=k: e.activation(out=xcT.t[:, sl, cc, :], in_=P.t[:, k, :], func=AF.Silu,
                                                               bias=cbT.t[:, cc:cc + 1]),
                     reads=[P.b[k], cbT.b[0]], writes=[xcT.b[sl]])
            tiles_slot[ti] = (a, sl)

        rG, rQ, rK, rT, rS, rH, rD = Rot(8), Rot(4), Rot(6), Rot(5), Rot(3), Rot(4), Rot(3)
        sg_, sq_, sk_, st_, ss_, sh_, sd_ = {}, {}, {}, {}, {}, {}, {}

        def S1(c):
            ti, m = divmod(c, 4)
            a, sl = tiles_slot[ti]
            b = rG()
            sg_[c] = b
            kg = rP()
            for i in range(8):
                cc = i % 4
                lhsT = (xcT.t[:, sl, cc, m * 128:(m + 1) * 128] if i < 4 else xmb.t[:, a, cc, 2 + m * 128:2 + (m + 1) * 128])
                S.op("pe", lambda e, i=i, cc=cc, lhsT=lhsT: e.matmul(out=P.t[:, kg, 0:8], lhsT=lhsT, rhs=Gf.t[:, i // 4, cc, :],
                                                                     start=(i == 0), stop=(i == 7)),
                     reads=[xcT.b[sl], xmb.b[a], Gf.b[0]], writes=[P.b[kg]])
            S.op("dve", lambda e: e.tensor_tensor(out=gsb.t[:, b, :], in0=P.t[:, kg, 0:8], in1=gbb.t[:], op=ALU.add),
                 reads=[P.b[kg], gbb.b[0]], writes=[gsb.b[b]])
            S.op("act", lambda e: e.activation(out=lfn.t[:, b, 0:4], in_=gsb.t[:, b, 4:8], func=AF.Exp, scale=-1.0),
                 reads=[gsb.b[b]], writes=[lfn.b[b]])
            S.op("act", lambda e: e.activation(out=lfn.t[:, b, 4:8], in_=lfn.t[:, b, 0:4], func=AF.Ln, bias=1.0),
                 reads=[lfn.b[b]], writes=[lfn.b[b]])

        def S2(c):
            b = sg_[c]
            kg = rP()
            S.op("dve", lambda e: e.tensor_copy(out=lh.t[:, b, 0, :], in_=lfn.t[:, b, 4:8]), reads=[lfn.b[b]], writes=[lh.b[b]])
            S.op("dve", lambda e: e.tensor_tensor(out=lh.t[:, b, 1, :], in0=lfn.t[:, b, 4:8], in1=lh.t[:, b, 0, :], op=ALU.subtract),
                 reads=[lfn.b[b], lh.b[b]], writes=[lh.b[b]])
            for (o0, mi) in ((8, 0), (12, 1)):
                for part in range(2):
                    S.op("pe", lambda e, o0=o0, mi=mi, part=part: e.matmul(
                        out=P.t[:, kg, o0:o0 + 4], lhsT=maskb.t[:, mi, :], rhs=lh.t[:, b, part, :],
                        start=(part == 0), stop=(part == 1)), reads=[maskb.b[0], lh.b[b]], writes=[P.b[kg]])
            S.op("act", lambda e: e.activation(out=ebt.t[:, b, 0:8], in_=P.t[:, kg, 8:16], func=AF.Exp, scale=-1.0),
                 reads=[P.b[kg]], writes=[ebt.b[b]])
            S.op("dve", lambda e: e.tensor_tensor(out=gsb.t[:, b, 4:8], in0=P.t[:, kg, 8:12], in1=gsb.t[:, b, 0:4], op=ALU.add),
                 reads=[P.b[kg], gsb.b[b]], writes=[gsb.b[b]])
            S.op("act", lambda e: e.activation(out=ebt.t[:, b, 8:12], in_=gsb.t[:, b, 4:8], func=AF.Exp),
                 reads=[gsb.b[b]], writes=[ebt.b[b]])

        def S3(c):
            ti, m = divmod(c, 4)
            a, sl = tiles_slot[ti]
            b = sg_[c]
            q_, k_ = rQ(), rK()
            sq_[c], sk_[c] = q_, k_
            kq, kk, kv = rP(), rP(), rP()
            for (kx, wi, src) in ((kq, 0, "xc"), (kk, 1, "xc"), (kv, 2, "xm")):
                for hh in range(4):
                    lhsT = (xcT.t[:, sl, hh, m * 128:(m + 1) * 128] if src == "xc"
                            else xmb.t[:, a, hh, 2 + m * 128:2 + (m + 1) * 128])
                    S.op("pe", lambda e, kx=kx, wi=wi, hh=hh, lhsT=lhsT: e.matmul(
                        out=P.t[:, kx, hh * 128:(hh + 1) * 128], lhsT=lhsT, rhs=bd.t[:, wi, hh, :], start=True, stop=True),
                         reads=[xcT.b[sl], xmb.b[a], bd.b[0]], writes=[P.b[kx]])
            S.op("dve", lambda e: e.tensor_tensor(
                out=qs.t[:, q_], in0=P.t[:, kq, :].rearrange("p (h d) -> p h d", h=4),
                in1=ebt.t[:, b, 0:4].unsqueeze(2).broadcast_to([128, 4, 128]), op=ALU.mult),
                 reads=[P.b[kq], ebt.b[b]], writes=[qs.b[q_]])
            S.op("dve", lambda e: e.tensor_tensor(
                out=ks.t[:, k_], in0=P.t[:, kk, :].rearrange("p (h d) -> p h d", h=4),
                in1=ebt.t[:, b, 8:12].unsqueeze(2).broadcast_to([128, 4, 128]), op=ALU.mult),
                 reads=[P.b[kk], ebt.b[b]], writes=[ks.b[k_]])
            S.op("act", lambda e: e.activation(out=vext.t[:, k_, :, 0:128], in_=P.t[:, kv, :].rearrange("p (h d) -> p h d", h=4),
                                               func=AF.Copy), reads=[P.b[kv]], writes=[vext.b[k_]])

        def S4(c):
            q_, k_ = sq_[c], sk_[c]
            t_ = rT()
            st_[c] = t_
            kt = rP()
            ptv = P.t[:, kt, :].bitcast(BF16)
            for i, (src, sl_) in enumerate(((qs, q_), (ks, k_))):
                for hh in range(4):
                    S.op("pe", lambda e, i=i, hh=hh, src=src, sl_=sl_: e.transpose(
                        out=ptv[:, (i * 4 + hh) * 128:(i * 4 + hh + 1) * 128], in_=src.t[:, sl_, hh, :], identity=ident.t[:]),
                         reads=[src.b[sl_], ident.b[0]], writes=[P.b[kt]])
            S.op("act", lambda e: e.activation(out=qkT.t[:, t_].rearrange("p a h t -> p (a h t)"), in_=ptv, func=AF.Copy),
                 reads=[P.b[kt]], writes=[qkT.b[t_]])

        def S5(c):
            t_ = st_[c]
            k1 = rP()
            for hh in range(4):
                S.op("pe", lambda e, hh=hh: e.matmul(out=P.t[:, k1, hh * 128:(hh + 1) * 128], lhsT=qkT.t[:, t_, 1, hh, :],
                                                     rhs=qkT.t[:, t_, 0, hh, :], start=True, stop=True),
                     reads=[qkT.b[t_]], writes=[P.b[k1]])
            s = rS()
            ss_[c] = s
            S.op("dve", lambda e: e.tensor_tensor(out=Sm.t[:, s], in0=P.t[:, k1, :].rearrange("p (h j) -> p h j", h=4),
                                                  in1=maskf.t[:].unsqueeze(1).broadcast_to([128, 4, 128]), op=ALU.mult),
                 reads=[P.b[k1], maskf.b[0]], writes=[Sm.b[s]])

        def S6(c):
            b, k_, t_, s = sg_[c], sk_[c], st_[c], ss_[c]
            kd = [rP(), rP()]
            kn = [rP(), rP()]
            for hh in range(4):
                hp, h2 = divmod(hh, 2)
                S.op("pe", lambda e, hh=hh, hp=hp, h2=h2: e.matmul(out=P.t[:, kd[hp], h2 * 256:h2 * 256 + 129],
                                                                   lhsT=ks.t[:, k_, hh, :], rhs=vext.t[:, k_, hh, 0:129],
                                                                   start=True, stop=True),
                     reads=[ks.b[k_], vext.b[k_]], writes=[P.b[kd[hp]]])
            for hh in range(4):
                hp, h2 = divmod(hh, 2)
                o = P.t[:, kn[hp], h2 * 256:h2 * 256 + 129]
                S.op("pe", lambda e, hh=hh, o=o: e.matmul(out=o, lhsT=Sm.t[:, s, hh, :], rhs=vext.t[:, k_, hh, 0:129],
                                                          start=True, stop=False),
                     reads=[Sm.b[s], vext.b[k_]], writes=[P.b[kn[hp]]])
                S.op("pe", lambda e, hh=hh, o=o: e.matmul(out=o, lhsT=qkT.t[:, t_, 0, hh, :], rhs=Cbf.t[:, hh, 0:129],
                                                          start=False, stop=True),
                     reads=[qkT.b[t_], Cbf.b[0]], writes=[P.b[kn[hp]]])
            for hp in range(2):
                S.op("dve", lambda e, hp=hp: e.tensor_tensor(
                    out=Cst.t[:, hp * 2:hp * 2 + 2, :], in0=P.t[:, kd[hp], :].rearrange("p (h x) -> p h x", h=2)[:, :, 0:129],
                    in1=Cst.t[:, hp * 2:hp * 2 + 2, :], op=ALU.add), reads=[P.b[kd[hp]], Cst.b[0]], writes=[Cst.b[0]])
            S.op("dve", lambda e: e.tensor_tensor(out=Cst.t[:], in0=Cst.t[:],
                                                  in1=ebt.t[:, b, 4:8].unsqueeze(2).broadcast_to([128, 4, 129]), op=ALU.mult),
                 reads=[Cst.b[0], ebt.b[b]], writes=[Cst.b[0]])
            S.op("pool", lambda e: e.tensor_copy(out=Cbf.t[:, :, 0:129], in_=Cst.t[:]), reads=[Cst.b[0]], writes=[Cbf.b[0]])
            dn = s
            for hp in range(2):
                S.op("dve", lambda e, hp=hp: e.tensor_copy(
                    out=den.t[:, dn, hp * 2:hp * 2 + 2].unsqueeze(2),
                    in_=P.t[:, kn[hp], :].rearrange("p (h x) -> p h x", h=2)[:, :, 128:129]),
                     reads=[P.b[kn[hp]]], writes=[den.b[dn]])
            S.op("dve", lambda e: e.scalar_tensor_tensor(out=den.t[:, dn, 4:8], in0=den.t[:, dn, 0:4], scalar=-1.0,
                                                         in1=den.t[:, dn, 0:4], op0=ALU.mult, op1=ALU.max),
                 reads=[den.b[dn]], writes=[den.b[dn]])
            S.op("dve", lambda e: e.tensor_scalar(out=den.t[:, dn, 0:4], in0=den.t[:, dn, 4:8], scalar1=1.0, scalar2=None,
                                                  op0=ALU.max), reads=[den.b[dn]], writes=[den.b[dn]])
            S.op("dve", lambda e: e.reciprocal(out=den.t[:, dn, 4:8], in_=den.t[:, dn, 0:4]), reads=[den.b[dn]], writes=[den.b[dn]])
            h = rH()
            sh_[c] = h
            for hp in range(2):
                S.op("dve", lambda e, hp=hp: e.tensor_tensor(
                    out=hd.t[:, h, hp * 2:hp * 2 + 2, :],
                    in0=P.t[:, kn[hp], :].rearrange("p (h x) -> p h x", h=2)[:, :, 0:128],
                    in1=den.t[:, dn, 4 + hp * 2:6 + hp * 2].unsqueeze(2).broadcast_to([128, 2, 128]), op=ALU.mult),
                     reads=[P.b[kn[hp]], den.b[dn]], writes=[hd.b[h]])
            if not bwd:
                S.dma("sync", hf[c * 128:(c + 1) * 128, :], hd.t[:, h].rearrange("p h d -> p (h d)"), reads=[hd.b[h]])

        def D1(c):
            ti, m = divmod(c, 4)
            a, sl = tiles_slot[ti]
            h = sh_[c]
            d = rD()
            sd_[c] = d
            t0 = c * 128
            S.dma("sync", hfl.t[:, d, :], hf[t0:t0 + 128, :], writes=[hfl.b[d]])
            S.dma("sync", sgl.t[:, d, :], sog[t0:t0 + 128, :], writes=[sgl.b[d]])
            S.dma("sync", ycT.t[:, d, 0:4, :], ysguT.rearrange("(h p) t -> p h t", p=128)[:, :, t0:t0 + 128], writes=[ycT.b[d]])
            S.dma("sync", xr.t[:, d, :], xin[t0:t0 + 128, :], writes=[xr.b[d]])
            S.op("pool", lambda e: e.tensor_tensor(out=hfl.t[:, d, :], in0=hfl.t[:, d, :],
                                                   in1=hd.t[:, h].rearrange("p h d -> p (h d)"), op=ALU.add),
                 reads=[hfl.b[d], hd.b[h]], writes=[hfl.b[d]])
            for hh in range(4):
                S.op("dve", lambda e, hh=hh: e.bn_stats(out=bst.t[:, d, hh, :], in_=hfl.t[:, d, hh * 128:(hh + 1) * 128]),
                     reads=[hfl.b[d]], writes=[bst.b[d]])
            for hh in range(4):
                S.op("dve", lambda e, hh=hh: e.bn_aggr(out=mv.t[:, d, hh, :], in_=bst.t[:, d, hh, :]),
                     reads=[bst.b[d]], writes=[mv.b[d]])
            rstd_ops(S, rs.t[:, d, :], mv.t[:, d, :, 1], pw.t[:], mv.b[d], rs.b[d])
            for hh in range(4):
                S.op("dve", lambda e, hh=hh: e.tensor_scalar(
                    out=hn.t[:, d, hh * 128:(hh + 1) * 128], in0=hfl.t[:, d, hh * 128:(hh + 1) * 128],
                    scalar1=mv.t[:, d, hh, 0:1], scalar2=rs.t[:, d, hh:hh + 1], op0=ALU.subtract, op1=ALU.mult),
                     reads=[hfl.b[d], mv.b[d], rs.b[d]], writes=[hn.b[d]])
            S.op("pool", lambda e: e.tensor_tensor(out=hn.t[:, d, :], in0=hn.t[:, d, :], in1=mhg.t[:], op=ALU.mult),
                 reads=[hn.b[d], mhg.b[0]], writes=[hn.b[d]])
            kx = rP()
            for hh in range(4):
                S.op("pe", lambda e, hh=hh: e.matmul(out=P.t[:, kx, hh * 128:(hh + 1) * 128],
                                                     lhsT=xcT.t[:, sl, hh, m * 128:(m + 1) * 128], rhs=sdiag.t[:, hh, :],
                                                     start=True, stop=True), reads=[xcT.b[sl], sdiag.b[0]], writes=[P.b[kx]])
            S.op("dve", lambda e: e.tensor_tensor(out=hn.t[:, d, :], in0=P.t[:, kx, :], in1=hn.t[:, d, :], op=ALU.add),
                 reads=[P.b[kx], hn.b[d]], writes=[hn.b[d]])
            S.op("pool", lambda e: e.tensor_tensor(out=ym.t[:, d, :], in0=hn.t[:, d, :], in1=sgl.t[:, d, :], op=ALU.mult),
                 reads=[hn.b[d], sgl.b[d]], writes=[ym.b[d]])

        def D2(c):
            d = sd_[c]
            kt = rP()
            ptv = P.t[:, kt, :].bitcast(BF16)
            for hh in range(4):
                S.op("pe", lambda e, hh=hh: e.transpose(out=ptv[:, hh * 128:(hh + 1) * 128], in_=ym.t[:, d, hh * 128:(hh + 1) * 128],
                                                        identity=ident.t[:]), reads=[ym.b[d], ident.b[0]], writes=[P.b[kt]])
            S.op("act", lambda e: e.activation(out=ycT.t[:, d, 4:8, :].rearrange("p h t -> p (h t)"), in_=ptv[:, 0:512], func=AF.Copy),
                 reads=[P.b[kt]], writes=[ycT.b[d]])

        def D3(c):
            d = sd_[c]
            t0 = c * 128
            for half in range(2):
                ko = rP()
                for cc in range(8):
                    S.op("pe", lambda e, cc=cc, half=half, ko=ko: e.matmul(
                        out=P.t[:, ko, :], lhsT=ycT.t[:, d, cc, :], rhs=wout.t[:, cc, half * 512:(half + 1) * 512],
                        start=(cc == 0), stop=(cc == 7)), reads=[ycT.b[d], wout.b[0]], writes=[P.b[ko]])
                S.op("dve", lambda e, half=half, ko=ko: e.tensor_tensor(
                    out=xr.t[:, d, half * 512:(half + 1) * 512], in0=P.t[:, ko, :], in1=xr.t[:, d, half * 512:(half + 1) * 512],
                    op=ALU.add), reads=[P.b[ko], xr.b[d]], writes=[xr.b[d]])
            S.dma("pool", xout[t0:t0 + 128, :], xr.t[:, d, :], reads=[xr.b[d]])

        order = list(range(NCH))
        if bwd:
            order.reverse()
        done_tiles = set()
        stages = [(S6, 0), (S5, 1), (S4, 2), (S3, 3), (S2, 4), (S1, 5)]
        if bwd:
            stages = [(S6, 0), (D1, -1), (D2, -2), (D3, -3), (S5, 1), (S4, 2), (S3, 3), (S2, 4), (S1, 5)]
        for i in range(-5, NCH + 3):
            for la in (5, 7):
                if 0 <= i + la < NCH and order[i + la] // 4 not in done_tiles:
                    stageA(order[i + la] // 4)
                    done_tiles.add(order[i + la] // 4)
            for fn, off in stages:
                if 0 <= i + off < NCH:
                    fn(order[i + off])
            if conv_items and i >= 0:
                for _ in range(conv_per_step):
                    if conv_items:
                        conv_items.pop(0)()
        while conv_items:
            conv_items.pop(0)()
        S.emit()


DEPTH = 2
SEQ = 8192
NCORES = 4

PARAM_SHAPES = {
    'ffn1_norm': (DEPTH, D), 'ffn1_w_gate': (DEPTH, D, HID), 'ffn1_w_up': (DEPTH, D, HID), 'ffn1_w_down': (DEPTH, HID, D),
    'mix_norm': (DEPTH, D), 'w_in': (DEPTH, D, 2048), 'sgu_norm': (DEPTH, 4, 128), 'sgu_w': (DEPTH, 4, 128, 128),
    'sgu_b': (DEPTH, 4, 128), 'conv_w': (DEPTH, 5, 512), 'conv_b': (DEPTH, 512), 'w_q': (DEPTH, 128, 4, 4),
    'w_k': (DEPTH, 128, 4, 4), 'w_v': (DEPTH, 128, 4, 4), 'gate_w_fwd': (DEPTH, 1536, 8), 'gate_b_fwd': (DEPTH, 8),
    'gate_w_bwd': (DEPTH, 1536, 8), 'gate_b_bwd': (DEPTH, 8), 'mh_norm': (DEPTH, 4, 128), 'mlstm_skip': (DEPTH, 512),
    'w_out': (DEPTH, D, D), 'ffn2_norm': (DEPTH, D), 'ffn2_w_gate': (DEPTH, D, HID), 'ffn2_w_up': (DEPTH, D, HID),
    'ffn2_w_down': (DEPTH, HID, D), 'final_norm': (D,),
}


def build_program(T=SEQ, depth=DEPTH):
    nc = bass.Bass("TRN2", target_bir_lowering=False)
    x = nc.dram_tensor("x", [T, D], F32, kind="ExternalInput").ap()
    p = {k: nc.dram_tensor(k, list(s), F32, kind="ExternalInput").ap() for k, s in PARAM_SHAPES.items()}
    y = nc.dram_tensor("y", [T, D], F32, kind="ExternalOutput").ap()
    xres = nc.dram_tensor("xres", [T, D], F32, kind="Internal").ap()
    wguA = nc.dram_tensor("wguA", [NJ, 128, 2, 8, 128], BF16, kind="Internal").ap()
    wguB = nc.dram_tensor("wguB", [NJ, 128, 2, 8, 128], BF16, kind="Internal").ap()
    xmT = nc.dram_tensor("xmT", [512, T + 4], F32, kind="Internal").ap()
    sog = nc.dram_tensor("sog", [T, 512], BF16, kind="Internal").ap()
    ysguT = nc.dram_tensor("ysguT", [512, T], BF16, kind="Internal").ap()
    hf = nc.dram_tensor("hf", [T, 512], F32, kind="Internal").ap()
    C = Ctx(nc)
    for l in range(depth):
        last = (l == depth - 1)
        if l == 0:
            convert_wgu_phase(C, p['ffn1_w_gate'][l], p['ffn1_w_up'][l], wguA)
        ffn_phase(C, T, x if l == 0 else xres, xres, p['ffn1_norm'][l], wguA, p['ffn1_w_down'][l])
        mixa_phase(C, T, xres, p['mix_norm'][l], p['w_in'][l], p['sgu_norm'][l], p['sgu_w'][l], p['sgu_b'][l],
                   xmT, sog, ysguT, AF.Gelu_apprx_tanh)
        mixb_phase(C, T, False, xmT, hf, p['conv_w'][l], p['conv_b'][l], p['w_q'][l], p['w_k'][l], p['w_v'][l],
                   p['gate_w_fwd'][l], p['gate_b_fwd'][l])
        mixb_phase(C, T, True, xmT, hf, p['conv_w'][l], p['conv_b'][l], p['w_q'][l], p['w_k'][l], p['w_v'][l],
                   p['gate_w_bwd'][l], p['gate_b_bwd'][l], sog=sog, ysguT=ysguT, mh_norm=p['mh_norm'][l],
                   skip=p['mlstm_skip'][l], w_out=p['w_out'][l], xin=xres, xout=xres,
                   conv_jobs=[(p['ffn2_w_gate'][l], p['ffn2_w_up'][l], wguB)] +
                   ([] if last else [(p['ffn1_w_gate'][l + 1], p['ffn1_w_up'][l + 1], wguA)]))
        ffn_phase(C, T, xres, y if last else xres, p['ffn2_norm'][l], wguB, p['ffn2_w_down'][l],
                  final_g=p['final_norm'] if last else None)
    return nc


def kernel(**inputs):
    x = np.ascontiguousarray(np.asarray(inputs['x'], dtype=np.float32))
    B = x.shape[0]
    params = {k: np.ascontiguousarray(np.asarray(inputs[k], dtype=np.float32)) for k in PARAM_SHAPES}
    nc = build_program()
    in_maps = [dict(params, x=x[b]) for b in range(B)]
    res = run_bass_kernel_spmd(nc, in_maps, core_ids=list(range(B)))
    return np.stack([np.asarray(res.results[b]["y"], dtype=np.float32) for b in range(B)], axis=0)
```

```python
import contextlib
import numpy as np
import ml_dtypes
import concourse.bass as bass
import concourse.mybir as mybir
from concourse.bass_utils import run_bass_kernel_spmd

F32 = mybir.dt.float32
BF16 = mybir.dt.bfloat16
AF = mybir.ActivationFunctionType
ALU = mybir.AluOpType

D = 1024
HID = 2816
NJ = HID // 128
EPS = 1e-6
ENGS = ("sync", "act", "dve", "pool", "pe")


class Buf:
    __slots__ = ("w", "r")

    def __init__(self):
        self.w = None
        self.r = []


class Op:
    __slots__ = ("eng", "fn", "deps", "ev", "dma")


class DmaPool:
    def __init__(self, nc, eng, n):
        self.slots = [[nc.alloc_semaphore(name=f"dq_{eng}_{i}"), 0, None] for i in range(n)]
        self.i = 0


class Sched:
    def __init__(self, nc, pools, tag):
        self.nc = nc
        self.pools = pools
        self.lists = {e: [] for e in ENGS}
        self.esem = {e: nc.alloc_semaphore(name=f"e_{tag}_{e}") for e in ENGS if e != "sync"}
        self.ecnt = {e: 0 for e in ENGS}

    def _deps(self, reads, writes):
        deps = []
        for b in reads:
            if b.w is not None:
                deps.append(b.w)
        for b in writes:
            if b.w is not None:
                deps.append(b.w)
            deps.extend(b.r)
        return deps

    def _commit(self, o, reads, writes):
        for b in reads:
            b.r.append(o)
        for b in writes:
            b.w = o
            b.r = []
        self.lists[o.eng].append(o)

    def op(self, eng, fn, reads=(), writes=()):
        o = Op()
        o.eng, o.fn, o.dma = eng, fn, False
        o.deps = self._deps(reads, writes)
        self.ecnt[eng] += 1
        o.ev = (self.esem[eng], self.ecnt[eng])
        self._commit(o, reads, writes)
        return o

    def dma(self, eng, out, in_, reads=(), writes=(), **kw):
        o = Op()
        o.eng, o.dma = eng, True
        o.fn = lambda e: e.dma_start(out=out, in_=in_, **kw)
        o.deps = self._deps(reads, writes)
        pool = self.pools[eng]
        slot = pool.slots[pool.i % len(pool.slots)]
        pool.i += 1
        if slot[2] is not None:
            o.deps.append(slot[2])
        slot[1] += 16
        slot[2] = o
        o.ev = (slot[0], slot[1])
        self._commit(o, reads, writes)
        return o

    def emit(self):
        nc = self.nc
        finals = [(s, c) for e, s in self.esem.items() for c in [self.ecnt[e]] if c > 0]
        for p in self.pools.values():
            for s, c, last in p.slots:
                if c > 0:
                    finals.append((s, c))
        names = {"sync": "sync", "act": "scalar", "dve": "vector", "pool": "gpsimd", "pe": "tensor"}
        with nc.Block() as blk:
            for eng in ENGS:
                def body(e, eng=eng):
                    waited = {}
                    for o in self.lists[eng]:
                        need = {}
                        for d in o.deps:
                            if d.eng == "pe" and eng == "pe" and not d.dma:
                                continue
                            s, v = d.ev
                            k = id(s)
                            if need.get(k, (None, 0))[1] < v:
                                need[k] = (s, v)
                        for k, (s, v) in need.items():
                            if waited.get(k, 0) >= v:
                                continue
                            e.wait_ge(s, v)
                            waited[k] = v
                        ins = o.fn(e)
                        ins.then_inc(o.ev[0], 16 if o.dma else 1)
                    for s, v in finals:
                        if waited.get(id(s), 0) < v:
                            e.wait_ge(s, v)
                getattr(blk, names[eng])(body)


class Ctx:
    def __init__(self, nc):
        self.nc = nc
        self.pools = {"sync": DmaPool(nc, "sync", 24), "pool": DmaPool(nc, "pool", 12), "act": DmaPool(nc, "act", 6)}
        self.nphase = 0

    def sched(self):
        self.nphase += 1
        return Sched(self.nc, self.pools, f"p{self.nphase}")


class Tile:
    def __init__(self, t, nslots=1):
        self.t = t
        self.b = [Buf() for _ in range(nslots)]


_UID = [0]


def _uname(name):
    _UID[0] += 1
    return f"t{_UID[0]}_{name}"


def sb(es, nc, name, shape, dt, nslots=1):
    t = es.enter_context(nc.sbuf_tensor(_uname(name), [128 if shape[0] is None else shape[0]] + list(shape[1:]), dt))
    return Tile(t, nslots)


def ps(es, nc, name, shape, dt, nslots=1):
    t = es.enter_context(nc.psum_tensor(_uname(name), list(shape), dt))
    return Tile(t, nslots)


def convert_wgu_phase(C, wg, wu, wgu):
    nc = C.nc
    S = C.sched()
    with contextlib.ExitStack() as es:
        wst = sb(es, nc, "wst", [128, 2, 2, 8, 256], F32, 2)
        wbf = sb(es, nc, "wbf", [128, 2, 2, 2, 8, 128], BF16, 2)
        engs = ("dve", "pool", "act", "dve")
        for jp in range(NJ // 2):
            sl = jp % 2
            for a, w in enumerate((wg, wu)):
                src = w.rearrange("(kc p) n -> p kc n", p=128)[:, :, jp * 256:(jp + 1) * 256]
                S.dma("sync", wst.t[:, sl, a, :, :], src, writes=[wst.b[sl]])
            i = 0
            for jj in range(2):
                for a in range(2):
                    eng = engs[i]; i += 1
                    o = wbf.t[:, sl, jj, a, :, :]
                    src = wst.t[:, sl, a, :, jj * 128:(jj + 1) * 128]
                    if eng == "act":
                        S.op("act", lambda e, o=o, src=src: e.activation(out=o, in_=src, func=AF.Copy),
                             reads=[wst.b[sl]], writes=[wbf.b[sl]])
                    else:
                        S.op(eng, lambda e, o=o, src=src: e.tensor_copy(out=o, in_=src),
                             reads=[wst.b[sl]], writes=[wbf.b[sl]])
            S.dma("sync", wgu[jp * 2:jp * 2 + 2].rearrange("j p a k c -> p j (a k c)"),
                  wbf.t[:, sl].rearrange("p j a k c -> p j (a k c)"), reads=[wbf.b[sl]])
        S.emit()


def convert_items(S, es, nc, wg, wu, wgu):
    wst = sb(es, nc, "cwst", [128, 1, 2, 8, 128], F32, 1)
    wbf = sb(es, nc, "cwbf", [128, 1, 2, 8, 128], BF16, 1)
    r = Rot(1)
    items = []
    for j in range(NJ):
        def it(j=j):
            sl = r()
            for a, w in enumerate((wg, wu)):
                src = w.rearrange("(kc p) n -> p kc n", p=128)[:, :, j * 128:(j + 1) * 128]
                S.dma("sync", wst.t[:, sl, a, :, :], src, writes=[wst.b[sl]])
            S.op("act", lambda e: e.activation(out=wbf.t[:, sl, 0], in_=wst.t[:, sl, 0], func=AF.Copy),
                 reads=[wst.b[sl]], writes=[wbf.b[sl]])
            S.op("pool", lambda e: e.tensor_copy(out=wbf.t[:, sl, 1], in_=wst.t[:, sl, 1]),
                 reads=[wst.b[sl]], writes=[wbf.b[sl]])
            S.dma("sync", wgu[j].rearrange("p a k c -> p (a k c)"), wbf.t[:, sl].rearrange("p a k c -> p (a k c)"),
                  reads=[wbf.b[sl]])
        items.append(it)
    return items


def ffn_phase(C, T, xin, xout, gvec, wgu, wd_f32, final_g=None):
    nc = C.nc
    S = C.sched()
    NG = T // 1024
    with contextlib.ExitStack() as es:
        ident = sb(es, nc, "ident", [128, 128], BF16)
        gb = sb(es, nc, "gb", [128, D], F32)
        gfb = sb(es, nc, "gfb", [128, D], F32)
        pw = sb(es, nc, "pw", [128, 1], F32)
        wd = sb(es, nc, "wd", [128, NJ, D], BF16)
        wgus = sb(es, nc, "wgus", [128, 3, 2, 2, 8, 128], BF16, 3)
        xprep = sb(es, nc, "xprep", [128, 4, D], F32, 4)
        xres = sb(es, nc, "xres", [128, 3, D], F32, 3)
        xn = sb(es, nc, "xn", [128, 2, D], BF16, 2)
        stat = sb(es, nc, "stat", [128, 8, 4], F32, 8)
        xnT = sb(es, nc, "xnT", [128, 2, 8, 1024], BF16, 2)
        aT = sb(es, nc, "aT", [128, NJ, 1024], BF16, 1)
        sg = sb(es, nc, "sg", [128, 2, 512], BF16, 2)
        junk = sb(es, nc, "junk", [128, D], BF16, 1)
        pg = ps(es, nc, "pg", [128, 2, 512], F32, 2)
        pu = ps(es, nc, "pu", [128, 2, 512], F32, 2)
        pt = ps(es, nc, "pt", [128, 2, 1024], BF16, 2)
        po = ps(es, nc, "po", [128, 2, 512], F32, 2)

        S.op("pool", lambda e: e.memset(ident.t[:], 0.0), writes=[ident.b[0]])
        S.op("pool", lambda e: e.affine_select(out=ident.t[:], in_=ident.t[:], pattern=[[-1, 128]],
                                               compare_op=ALU.not_equal, fill=1.0, base=0, channel_multiplier=1),
             reads=[ident.b[0]], writes=[ident.b[0]])
        S.op("pool", lambda e: e.memset(pw.t[:], -0.5), writes=[pw.b[0]])
        S.dma("sync", gb.t[:], gvec.partition_broadcast(128), writes=[gb.b[0]])
        if final_g is not None:
            S.dma("sync", gfb.t[:], final_g.partition_broadcast(128), writes=[gfb.b[0]])
        wdsrc = wd_f32.rearrange("(j p) n -> p j n", p=128)
        for j0 in range(0, NJ, 2):
            S.dma("pool", wd.t[:, j0:j0 + 2, :], wdsrc[:, j0:j0 + 2, :], writes=[wd.b[0]])

        cnt = {"xp": 0, "xn": 0, "st": 0, "pt": 0, "wg": 0, "g": 0, "sg": 0, "po": 0, "xr": 0}

        def prep_items(g):
            fronts, backs = [], []
            slot = g % 2
            for s in range(8):
                def it(s=s):
                    i = cnt["xp"]; cnt["xp"] += 1
                    k = i % 4
                    tok0 = (g * 8 + s) * 128
                    xp = xprep.t[:, k, :]
                    S.dma("sync", xp, xin[tok0:tok0 + 128, :], writes=[xprep.b[k]])
                    q = cnt["st"] % 8; cnt["st"] += 1
                    ss = stat.t[:, q, 0:1]
                    rstd = stat.t[:, q, 1:2]
                    S.op("dve", lambda e: e.scalar_tensor_tensor(out=junk.t[:], in0=xp, scalar=1.0, in1=xp,
                                                                 op0=ALU.mult, op1=ALU.mult, accum_out=ss),
                         reads=[xprep.b[k]], writes=[junk.b[0], stat.b[q]])
                    S.op("pool", lambda e: e.tensor_scalar(out=rstd, in0=ss, scalar1=1.0 / D, scalar2=EPS,
                                                           op0=ALU.mult, op1=ALU.add),
                         reads=[stat.b[q]], writes=[stat.b[q]])
                    S.op("pool", lambda e: e.tensor_tensor(out=rstd, in0=rstd, in1=pw.t[:], op=ALU.pow),
                         reads=[stat.b[q], pw.b[0]], writes=[stat.b[q]])
                    n = cnt["xn"] % 2; cnt["xn"] += 1
                    S.op("dve", lambda e: e.scalar_tensor_tensor(out=xn.t[:, n, :], in0=xp, scalar=rstd, in1=gb.t[:],
                                                                 op0=ALU.mult, op1=ALU.mult),
                         reads=[xprep.b[k], stat.b[q], gb.b[0]], writes=[xn.b[n]])
                    return n

                def bk(s=s, n=None):
                    p = cnt["pt"] % 2; cnt["pt"] += 1
                    for kc in range(8):
                        S.op("pe", lambda e, kc=kc: e.transpose(out=pt.t[:, p, kc * 128:(kc + 1) * 128],
                                                                in_=xn.t[:, n, kc * 128:(kc + 1) * 128],
                                                                identity=ident.t[:]),
                             reads=[xn.b[n], ident.b[0]], writes=[pt.b[p]])
                    S.op("act", lambda e: e.activation(
                        out=xnT.t[:, slot, :, s * 128:(s + 1) * 128],
                        in_=pt.t[:, p, :].rearrange("p (k t) -> p k t", k=8), func=AF.Copy),
                         reads=[pt.b[p]], writes=[xnT.b[slot]])
                fronts.append(it)
                backs.append(bk)
            nsl = {}

            def mk_f(i):
                def f():
                    nsl[i] = fronts[i]()
                return f

            def mk_b(i):
                def f():
                    backs[i](n=nsl[i])
                return f
            order = [mk_f(0)]
            for i in range(1, 8):
                order.append(mk_f(i))
                order.append(mk_b(i - 1))
            order.append(mk_b(7))
            return order

        def gateup(g, extra):
            slot = g % 2
            for j in range(NJ):
                if j % 2 == 0:
                    w = cnt["wg"] % 3; cnt["wg"] += 1
                    S.dma("sync", wgus.t[:, w].rearrange("p j a k c -> p j (a k c)"),
                          wgu[j:j + 2].rearrange("j p a k c -> p j (a k c)"), writes=[wgus.b[w]])
                    wcur = w
                jj = j % 2
                for half in range(2):
                    q = cnt["g"] % 2; cnt["g"] += 1
                    for (pp, a) in ((pg, 0), (pu, 1)):
                        for kc in range(8):
                            S.op("pe", lambda e, pp=pp, a=a, kc=kc, q=q, half=half, wcur=wcur, jj=jj: e.matmul(
                                out=pp.t[:, q, :], lhsT=wgus.t[:, wcur, jj, a, kc, :],
                                rhs=xnT.t[:, slot, kc, half * 512:(half + 1) * 512],
                                start=(kc == 0), stop=(kc == 7)),
                                 reads=[wgus.b[wcur], xnT.b[slot]], writes=[pp.b[q]])
                    r = cnt["sg"] % 2; cnt["sg"] += 1
                    S.op("act", lambda e, q=q, r=r: e.activation(out=sg.t[:, r, :], in_=pg.t[:, q, :], func=AF.Silu),
                         reads=[pg.b[q]], writes=[sg.b[r]])
                    S.op("dve", lambda e, q=q, r=r, j=j, half=half: e.tensor_tensor(
                        out=aT.t[:, j, half * 512:(half + 1) * 512], in0=pu.t[:, q, :], in1=sg.t[:, r, :], op=ALU.mult),
                         reads=[pu.b[q], sg.b[r]], writes=[aT.b[0]])
                    if extra:
                        extra.pop(0)()

        def down(g, extra):
            for m in range(8):
                tok0 = (g * 8 + m) * 128
                x = cnt["xr"] % 3; cnt["xr"] += 1
                S.dma("sync", xres.t[:, x, :], xin[tok0:tok0 + 128, :], writes=[xres.b[x]])
                for n in range(2):
                    q = cnt["po"] % 2; cnt["po"] += 1
                    for j in range(NJ):
                        S.op("pe", lambda e, j=j, q=q, m=m, n=n: e.matmul(
                            out=po.t[:, q, :], lhsT=aT.t[:, j, m * 128:(m + 1) * 128],
                            rhs=wd.t[:, j, n * 512:(n + 1) * 512], start=(j == 0), stop=(j == NJ - 1)),
                             reads=[aT.b[0], wd.b[0]], writes=[po.b[q]])
                    S.op("dve", lambda e, q=q, x=x, n=n: e.scalar_tensor_tensor(
                        out=xres.t[:, x, n * 512:(n + 1) * 512], in0=po.t[:, q, :], scalar=0.5,
                        in1=xres.t[:, x, n * 512:(n + 1) * 512], op0=ALU.mult, op1=ALU.add),
                         reads=[po.b[q], xres.b[x]], writes=[xres.b[x]])
                if final_g is not None:
                    qs = cnt["st"] % 8; cnt["st"] += 1
                    ss = stat.t[:, qs, 0:1]
                    rstd = stat.t[:, qs, 1:2]
                    xr = xres.t[:, x, :]
                    S.op("dve", lambda e, xr=xr, ss=ss: e.scalar_tensor_tensor(
                        out=junk.t[:], in0=xr, scalar=1.0, in1=xr, op0=ALU.mult, op1=ALU.mult, accum_out=ss),
                         reads=[xres.b[x]], writes=[junk.b[0], stat.b[qs]])
                    S.op("pool", lambda e, ss=ss, rstd=rstd: e.tensor_scalar(
                        out=rstd, in0=ss, scalar1=1.0 / D, scalar2=EPS, op0=ALU.mult, op1=ALU.add),
                         reads=[stat.b[qs]], writes=[stat.b[qs]])
                    S.op("pool", lambda e, rstd=rstd: e.tensor_tensor(out=rstd, in0=rstd, in1=pw.t[:], op=ALU.pow),
                         reads=[stat.b[qs], pw.b[0]], writes=[stat.b[qs]])
                    S.op("dve", lambda e, xr=xr, rstd=rstd: e.scalar_tensor_tensor(
                        out=xr, in0=xr, scalar=rstd, in1=gfb.t[:], op0=ALU.mult, op1=ALU.mult),
                         reads=[xres.b[x], stat.b[qs], gfb.b[0]], writes=[xres.b[x]])
                S.dma("pool", xout[tok0:tok0 + 128, :], xres.t[:, x, :], reads=[xres.b[x]])
                if extra:
                    extra.pop(0)()

        for it in prep_items(0):
            it()
        for g in range(NG):
            nxt = prep_items(g + 1) if g + 1 < NG else []
            gateup(g, nxt)
            down(g, nxt)
            while nxt:
                nxt.pop(0)()
        S.emit()


class Rot:
    def __init__(self, n):
        self.n, self.i = n, 0

    def __call__(self):
        k = self.i % self.n
        self.i += 1
        return k


def make_ident(S, tile, dt_is_f32=False):
    S.op("pool", lambda e: e.memset(tile.t[:], 0.0), writes=[tile.b[0]])
    S.op("pool", lambda e: e.affine_select(out=tile.t[:], in_=tile.t[:], pattern=[[-1, 128]],
                                           compare_op=ALU.not_equal, fill=1.0, base=0, channel_multiplier=1),
         reads=[tile.b[0]], writes=[tile.b[0]])


def rstd_ops(S, out, in_, pwt, b_in, b_out, scale=1.0):
    S.op("pool", lambda e: e.tensor_scalar(out=out, in0=in_, scalar1=scale, scalar2=EPS, op0=ALU.mult, op1=ALU.add),
         reads=[b_in], writes=[b_out])
    S.op("pool", lambda e: e.tensor_tensor(out=out, in0=out, in1=pwt, op=ALU.pow), reads=[b_out], writes=[b_out])


GELU_C = 0.7978845608028654


def mixa_phase(C, T, xin, gvec, w_in, sgu_norm, sgu_w, sgu_b, xmT_pad, sog, ysguT, gelu_fn):
    nc = C.nc
    S = C.sched()
    NT = T // 512
    with contextlib.ExitStack() as es:
        es.enter_context(nc.allow_non_contiguous_dma(reason="tiny parameter transposes"))
        ident = sb(es, nc, "ident", [128, 128], BF16)
        identf = sb(es, nc, "identf", [128, 128], F32)
        gb = sb(es, nc, "gb", [128, D], F32)
        pw = sb(es, nc, "pw", [128, 4], F32)
        win = sb(es, nc, "win", [128, 8, 2048], BF16)
        wsf = sb(es, nc, "wsf", [128, 4, 128], F32)
        wsT = sb(es, nc, "wsT", [128, 4, 128], BF16)
        wsb = sb(es, nc, "wsb", [128, 4, 128], BF16)
        gT = sb(es, nc, "gT", [128, 4], F32)
        bsb = sb(es, nc, "bsb", [128, 4, 128], F32)
        zt = sb(es, nc, "zt", [128, 4, 2], F32)
        xprep = sb(es, nc, "xprep", [128, 4, D], F32, 4)
        junk = sb(es, nc, "junk", [128, D], BF16)
        stat = sb(es, nc, "stat", [128, 8, 4], F32, 8)
        xn = sb(es, nc, "xn", [128, 2, D], BF16, 2)
        xnT = sb(es, nc, "xnT", [128, 2, 8, 512], BF16, 2)
        guT = sb(es, nc, "guT", [128, 2, 4, 512], F32, 2)
        xms = sb(es, nc, "xms", [128, 2, 4, 512], F32, 2)
        gv = sb(es, nc, "gv", [128, 4, 512], F32, 4)
        bst = sb(es, nc, "bst", [128, 4, 4, 6], F32, 4)
        mv = sb(es, nc, "mv", [128, 4, 4, 2], F32, 4)
        rs = sb(es, nc, "rs", [128, 4, 4], F32, 4)
        vhn = sb(es, nc, "vhn", [128, 2, 4, 4, 128], BF16, 2)
        sgs = sb(es, nc, "sgs", [128, 2, 512], BF16, 2)
        tmp = sb(es, nc, "tmp", [128, 2, 512], F32, 2)
        ys = sb(es, nc, "ys", [128, 2, 4, 512], BF16, 2)
        gtmp = sb(es, nc, "gtmp", [128, 2, 512], F32, 2)
        P = ps(es, nc, "P", [128, 8, 512], F32, 8)
        rP = Rot(8)

        make_ident(S, ident)
        make_ident(S, identf)
        S.op("pool", lambda e: e.memset(pw.t[:], -0.5), writes=[pw.b[0]])
        S.op("pool", lambda e: e.memset(zt.t[:], 0.0), writes=[zt.b[0]])
        S.dma("sync", gb.t[:], gvec.partition_broadcast(128), writes=[gb.b[0]])
        S.dma("pool", win.t[:, 0:4, :], w_in.rearrange("(kc p) n -> p kc n", p=128)[:, 0:4, :], writes=[win.b[0]])
        S.dma("pool", win.t[:, 4:8, :], w_in.rearrange("(kc p) n -> p kc n", p=128)[:, 4:8, :], writes=[win.b[0]])
        S.dma("sync", wsf.t[:], sgu_w.rearrange("h p q -> p h q"), writes=[wsf.b[0]])
        S.dma("sync", gT.t[:], sgu_norm.rearrange("h d -> d h"), writes=[gT.b[0]])
        S.dma("sync", bsb.t[:].rearrange("p h q -> p (h q)"), sgu_b.rearrange("h q -> (h q)").partition_broadcast(128),
              writes=[bsb.b[0]])
        xmv = xmT_pad.rearrange("(cc p) t -> p cc t", p=128)
        S.dma("sync", xmv[:, :, 0:2], zt.t[:], reads=[zt.b[0]])
        S.dma("sync", xmv[:, :, T + 2:T + 4], zt.t[:], reads=[zt.b[0]])
        S.op("pool", lambda e: e.tensor_copy(out=wsb.t[:], in_=wsf.t[:]), reads=[wsf.b[0]], writes=[wsb.b[0]])
        for hh in range(4):
            k = rP()
            pv = P.t[:, k, :].bitcast(BF16)
            S.op("pe", lambda e, hh=hh, pv=pv: e.transpose(out=pv[:, 0:128], in_=wsb.t[:, hh, :], identity=ident.t[:]),
                 reads=[wsb.b[0], ident.b[0]], writes=[P.b[k]])
            S.op("dve", lambda e, hh=hh, pv=pv: e.tensor_copy(out=wsT.t[:, hh, :], in_=pv[:, 0:128]),
                 reads=[P.b[k]], writes=[wsT.b[0]])

        rxp, rst, rxn = Rot(4), Rot(8), Rot(2)

        def gelu(out, in_ap, in_buf, out_buf, n):
            if gelu_fn is not None:
                S.op("act", lambda e: e.activation(out=out, in_=in_ap, func=gelu_fn), reads=[in_buf], writes=[out_buf])
                return
            k = rgt()
            t = gtmp.t[:, k, 0:n]
            S.op("act", lambda e: e.activation(out=t, in_=in_ap, func=AF.Square), reads=[in_buf], writes=[gtmp.b[k]])
            S.op("pool", lambda e: e.tensor_scalar(out=t, in0=t, scalar1=0.044715, scalar2=1.0, op0=ALU.mult, op1=ALU.add),
                 reads=[gtmp.b[k]], writes=[gtmp.b[k]])
            S.op("dve", lambda e: e.tensor_tensor(out=t, in0=in_ap, in1=t, op=ALU.mult), reads=[in_buf, gtmp.b[k]],
                 writes=[gtmp.b[k]])
            S.op("act", lambda e: e.activation(out=t, in_=t, func=AF.Sigmoid, scale=2.0 * GELU_C),
                 reads=[gtmp.b[k]], writes=[gtmp.b[k]])
            S.op("dve", lambda e: e.tensor_tensor(out=out, in0=in_ap, in1=t, op=ALU.mult), reads=[in_buf, gtmp.b[k]],
                 writes=[out_buf])
        rgt = Rot(2)

        def prep(ti):
            slot = ti % 2
            info = []
            for s in range(4):
                tok0 = ti * 512 + s * 128
                k = rxp()
                xp = xprep.t[:, k, :]
                S.dma("sync", xp, xin[tok0:tok0 + 128, :], writes=[xprep.b[k]])
                q = rst()
                ss, rstd = stat.t[:, q, 0:1], stat.t[:, q, 1:2]
                S.op("dve", lambda e, xp=xp, ss=ss: e.scalar_tensor_tensor(
                    out=junk.t[:], in0=xp, scalar=1.0, in1=xp, op0=ALU.mult, op1=ALU.mult, accum_out=ss),
                     reads=[xprep.b[k]], writes=[junk.b[0], stat.b[q]])
                rstd_ops(S, rstd, ss, pw.t[:, 0:1], stat.b[q], stat.b[q], 1.0 / D)
                info.append((k, xp, q, rstd))
            for s in range(4):
                k, xp, q, rstd = info[s]
                n = rxn()
                S.op("dve", lambda e, xp=xp, rstd=rstd, n=n: e.scalar_tensor_tensor(
                    out=xn.t[:, n, :], in0=xp, scalar=rstd, in1=gb.t[:], op0=ALU.mult, op1=ALU.mult),
                     reads=[xprep.b[k], stat.b[q], gb.b[0]], writes=[xn.b[n]])
                p = rP()
                ptv = P.t[:, p, :].bitcast(BF16)
                for kc in range(8):
                    S.op("pe", lambda e, kc=kc, n=n, ptv=ptv: e.transpose(
                        out=ptv[:, kc * 128:(kc + 1) * 128], in_=xn.t[:, n, kc * 128:(kc + 1) * 128], identity=ident.t[:]),
                         reads=[xn.b[n], ident.b[0]], writes=[P.b[p]])
                S.op("act", lambda e, s=s, ptv=ptv: e.activation(
                    out=xnT.t[:, slot, :, s * 128:(s + 1) * 128], in_=ptv.rearrange("p (k t) -> p k t", k=8), func=AF.Copy),
                     reads=[P.b[p]], writes=[xnT.b[slot]])

        def body(ti):
            slot = ti % 2
            tok0 = ti * 512
            for cc in range(4):
                k = rP()
                for kc in range(8):
                    S.op("pe", lambda e, cc=cc, kc=kc, k=k: e.matmul(
                        out=P.t[:, k, :], lhsT=win.t[:, kc, cc * 128:(cc + 1) * 128], rhs=xnT.t[:, slot, kc, :],
                        start=(kc == 0), stop=(kc == 7)), reads=[win.b[0], xnT.b[slot]], writes=[P.b[k]])
                gelu(guT.t[:, slot, cc, :], P.t[:, k, :], P.b[k], guT.b[slot], 512)
            for m in range(4):
                k = rP()
                for kc in range(8):
                    S.op("pe", lambda e, m=m, kc=kc, k=k: e.matmul(
                        out=P.t[:, k, :], lhsT=xnT.t[:, slot, kc, m * 128:(m + 1) * 128], rhs=win.t[:, kc, 512:1024],
                        start=(kc == 0), stop=(kc == 7)), reads=[win.b[0], xnT.b[slot]], writes=[P.b[k]])
                v = m
                gelu(gv.t[:, v, :], P.t[:, k, :], P.b[k], gv.b[v], 512)
                for hh in range(4):
                    S.op("dve", lambda e, hh=hh, v=v: e.bn_stats(out=bst.t[:, v, hh, :], in_=gv.t[:, v, hh * 128:(hh + 1) * 128]),
                         reads=[gv.b[v]], writes=[bst.b[v]])
                for hh in range(4):
                    S.op("dve", lambda e, hh=hh, v=v: e.bn_aggr(out=mv.t[:, v, hh, :], in_=bst.t[:, v, hh, :]),
                         reads=[bst.b[v]], writes=[mv.b[v]])
                rstd_ops(S, rs.t[:, v, :], mv.t[:, v, :, 1], pw.t[:], mv.b[v], rs.b[v])
            for cc in range(4):
                k = rP()
                for kc in range(8):
                    S.op("pe", lambda e, cc=cc, kc=kc, k=k: e.matmul(
                        out=P.t[:, k, :], lhsT=win.t[:, kc, 1024 + cc * 128:1024 + (cc + 1) * 128],
                        rhs=xnT.t[:, slot, kc, :], start=(kc == 0), stop=(kc == 7)),
                         reads=[win.b[0], xnT.b[slot]], writes=[P.b[k]])
                S.op("dve", lambda e, cc=cc, k=k: e.tensor_copy(out=xms.t[:, slot, cc, :], in_=P.t[:, k, :]),
                     reads=[P.b[k]], writes=[xms.b[slot]])
            S.dma("sync", xmv[:, :, 2 + tok0:2 + tok0 + 512], xms.t[:, slot], reads=[xms.b[slot]])
            for m in range(4):
                v = m
                k = rP()
                for kc in range(8):
                    S.op("pe", lambda e, m=m, kc=kc, k=k: e.matmul(
                        out=P.t[:, k, :], lhsT=xnT.t[:, slot, kc, m * 128:(m + 1) * 128], rhs=win.t[:, kc, 1536:2048],
                        start=(kc == 0), stop=(kc == 7)), reads=[win.b[0], xnT.b[slot]], writes=[P.b[k]])
                S.op("act", lambda e, k=k, v=v: e.activation(out=sgs.t[:, v % 2, :], in_=P.t[:, k, :], func=AF.Sigmoid),
                     reads=[P.b[k]], writes=[sgs.b[v % 2]])
                S.dma("sync", sog[tok0 + m * 128:tok0 + (m + 1) * 128, :], sgs.t[:, v % 2, :], reads=[sgs.b[v % 2]])
            for m in range(4):
                v = m
                for hh in range(4):
                    S.op("dve", lambda e, hh=hh, v=v, m=m: e.tensor_scalar(
                        out=vhn.t[:, slot, m, hh, :], in0=gv.t[:, v, hh * 128:(hh + 1) * 128],
                        scalar1=mv.t[:, v, hh, 0:1], scalar2=rs.t[:, v, hh:hh + 1], op0=ALU.subtract, op1=ALU.mult),
                         reads=[gv.b[v], mv.b[v], rs.b[v]], writes=[vhn.b[slot]])

        def body_sgu(ti):
            slot = ti % 2
            tok0 = ti * 512
            for hh in range(4):
                k = rP()
                for m in range(4):
                    S.op("pe", lambda e, hh=hh, m=m, k=k: e.matmul(
                        out=P.t[:, k, m * 128:(m + 1) * 128], lhsT=vhn.t[:, slot, m, hh, :], rhs=wsT.t[:, hh, :],
                        start=True, stop=True), reads=[vhn.b[slot], wsT.b[0]], writes=[P.b[k]])
                tq = hh % 2
                S.op("dve", lambda e, hh=hh, k=k, tq=tq: e.scalar_tensor_tensor(
                    out=tmp.t[:, tq, :].rearrange("p (m q) -> p m q", m=4), in0=P.t[:, k, :].rearrange("p (m q) -> p m q", m=4),
                    scalar=gT.t[:, hh:hh + 1], in1=bsb.t[:, hh:hh + 1, :].broadcast_to([128, 4, 128]),
                    op0=ALU.mult, op1=ALU.add), reads=[P.b[k], gT.b[0], bsb.b[0]], writes=[tmp.b[tq]])
                S.op("pool", lambda e, hh=hh, tq=tq: e.tensor_tensor(
                    out=ys.t[:, slot, hh, :], in0=tmp.t[:, tq, :], in1=guT.t[:, slot, hh, :], op=ALU.mult),
                     reads=[tmp.b[tq], guT.b[slot]], writes=[ys.b[slot]])
            S.dma("sync", ysguT.rearrange("(h p) t -> p h t", p=128)[:, :, tok0:tok0 + 512], ys.t[:, slot],
                  reads=[ys.b[slot]])

        prep(0)
        for ti in range(NT):
            if ti + 1 < NT:
                prep(ti + 1)
            body(ti)
            if ti >= 1:
                body_sgu(ti - 1)
        body_sgu(NT - 1)
        S.emit()


def mixb_phase(C, T, bwd, xmT_pad, hf, conv_w, conv_b, w_q, w_k, w_v, gate_w, gate_b,
               sog=None, ysguT=None, mh_norm=None, skip=None, w_out=None, xin=None, xout=None, dbg=None, conv_jobs=()):
    nc = C.nc
    S = C.sched()
    NCH = T // 128
    NT = T // 512
    with contextlib.ExitStack() as es:
        es.enter_context(nc.allow_non_contiguous_dma(reason="tiny parameter transposes"))
        ident = sb(es, nc, "ident", [128, 128], BF16)
        identf = sb(es, nc, "identf", [128, 128], F32)
        maskf = sb(es, nc, "maskf", [128, 128], F32)
        onesf = sb(es, nc, "onesf", [128, 128], F32)
        mbd = sb(es, nc, "mbd", [128, 32], F32)
        pw = sb(es, nc, "pw", [128, 4], F32)
        cwT = sb(es, nc, "cwT", [128, 4, 5], F32)
        cbT = sb(es, nc, "cbT", [128, 4], F32)
        cdiag = sb(es, nc, "cdiag", [128, 4, 5, 128], BF16)
        wl = sb(es, nc, "wl", [128, 3, 4, 4], F32)
        bdf = sb(es, nc, "bdf", [128, 3, 4, 128], F32)
        bd = sb(es, nc, "bd", [128, 3, 4, 128], BF16)
        bdT = sb(es, nc, "bdT", [128, 3, 4, 128], BF16)
        gwb = sb(es, nc, "gwb", [128, 12, 8], BF16)
        maskb = sb(es, nc, "maskb", [128, 2, 128], BF16)
        lh = sb(es, nc, "lh", [128, 8, 2, 4], BF16, 8)
        gw = sb(es, nc, "gw", [128, 12, 8], F32)
        Gf = sb(es, nc, "Gf", [128, 2, 4, 8], BF16)
        gbb = sb(es, nc, "gbb", [128, 8], F32)
        xmf = sb(es, nc, "xmf", [128, 2, 4, 516], F32, 2)
        xmb = sb(es, nc, "xmb", [128, 3, 4, 516], BF16, 3)
        xcT = sb(es, nc, "xcT", [128, 4, 4, 512], BF16, 4)
        gsb = sb(es, nc, "gsb", [128, 8, 8], F32, 8)
        lfn = sb(es, nc, "lfn", [128, 8, 8], F32, 8)
        ebt = sb(es, nc, "ebt", [128, 8, 12], F32, 8)
        qs = sb(es, nc, "qs", [128, 4, 4, 128], BF16, 4)
        ks = sb(es, nc, "ks", [128, 6, 4, 128], BF16, 6)
        vext = sb(es, nc, "vext", [128, 6, 4, 130], BF16, 6)
        qkT = sb(es, nc, "qkT", [128, 5, 2, 4, 128], BF16, 5)
        Sm = sb(es, nc, "Sm", [128, 3, 4, 128], BF16, 3)
        Cst = sb(es, nc, "Cst", [128, 4, 129], F32)
        Cbf = sb(es, nc, "Cbf", [128, 4, 130], BF16)
        den = sb(es, nc, "den", [128, 3, 8], F32, 3)
        hd = sb(es, nc, "hd", [128, 4, 4, 128], F32, 4)
        P = ps(es, nc, "P", [128, 8, 512], F32, 8)
        rP = Rot(8)
        if bwd:
            mhg = sb(es, nc, "mhg", [128, 512], F32)
            skT = sb(es, nc, "skT", [128, 4], F32)
            sdiag = sb(es, nc, "sdiag", [128, 4, 128], BF16)
            wout = sb(es, nc, "wout", [128, 8, D], BF16)
            hfl = sb(es, nc, "hfl", [128, 4, 512], F32, 4)
            sgl = sb(es, nc, "sgl", [128, 4, 512], BF16, 4)
            ycT = sb(es, nc, "ycT", [128, 4, 8, 128], BF16, 4)
            xr = sb(es, nc, "xr", [128, 4, D], F32, 4)
            bst = sb(es, nc, "bst", [128, 4, 4, 6], F32, 4)
            mv = sb(es, nc, "mv", [128, 4, 4, 2], F32, 4)
            rs = sb(es, nc, "rs", [128, 4, 4], F32, 4)
            hn = sb(es, nc, "hn", [128, 4, 512], F32, 4)
            ym = sb(es, nc, "ym", [128, 4, 512], BF16, 4)

        conv_items = []
        for (cwg, cwu, cdst) in conv_jobs:
            conv_items.extend(convert_items(S, es, nc, cwg, cwu, cdst))
        conv_per_step = 1 if conv_items else 0
        make_ident(S, ident)
        make_ident(S, identf)
        S.op("pool", lambda e: e.memset(pw.t[:], -0.5), writes=[pw.b[0]])
        S.op("pool", lambda e: e.memset(onesf.t[:], 1.0), writes=[onesf.b[0]])
        S.op("pool", lambda e: e.memset(maskf.t[:], 1.0), writes=[maskf.b[0]])
        S.op("pool", lambda e: e.affine_select(out=maskf.t[:], in_=maskf.t[:], pattern=[[-1 if bwd else 1, 128]],
                                               compare_op=ALU.is_ge, fill=0.0, base=0,
                                               channel_multiplier=1 if bwd else -1),
             reads=[maskf.b[0]], writes=[maskf.b[0]])
        S.op("pool", lambda e: e.memset(mbd.t[:], 1.0), writes=[mbd.b[0]])
        S.op("pool", lambda e: e.affine_select(out=mbd.t[:], in_=mbd.t[:], pattern=[[-4, 32]], compare_op=ALU.is_ge,
                                               fill=0.0, base=0, channel_multiplier=1),
             reads=[mbd.b[0]], writes=[mbd.b[0]])
        S.op("pool", lambda e: e.affine_select(out=mbd.t[:], in_=mbd.t[:], pattern=[[4, 32]], compare_op=ALU.is_ge,
                                               fill=0.0, base=3, channel_multiplier=-1),
             reads=[mbd.b[0]], writes=[mbd.b[0]])
        for cc in range(4):
            S.dma("sync", cwT.t[:, cc, :], conv_w[:, cc * 128:(cc + 1) * 128].rearrange("j p -> p j"), writes=[cwT.b[0]])
        S.dma("sync", cbT.t[:], conv_b.rearrange("(cc p) -> p cc", p=128), writes=[cbT.b[0]])
        for i, w in enumerate((w_q, w_k, w_v)):
            S.dma("sync", wl.t[:, i, :, :], w.rearrange("(hh g) i o -> (g i) hh o", hh=4), writes=[wl.b[0]])
        S.dma("sync", gw.t[:], gate_w.rearrange("(r p) n -> p r n", p=128), writes=[gw.b[0]])
        S.dma("sync", gbb.t[:], gate_b.partition_broadcast(128), writes=[gbb.b[0]])
        S.op("dve", lambda e: e.tensor_scalar(out=wl.t[:, 1], in0=wl.t[:, 1], scalar1=128.0 ** -0.5, scalar2=None, op0=ALU.mult),
             reads=[wl.b[0]], writes=[wl.b[0]])
        for i in range(3):
            for hh in range(4):
                S.op("dve", lambda e, i=i, hh=hh: e.tensor_tensor(
                    out=bdf.t[:, i, hh, :].rearrange("p (g o) -> p g o", o=4),
                    in0=mbd.t[:].unsqueeze(2).broadcast_to([128, 32, 4]),
                    in1=wl.t[:, i, hh:hh + 1, :].broadcast_to([128, 32, 4]), op=ALU.mult),
                     reads=[mbd.b[0], wl.b[0]], writes=[bdf.b[0]])
        S.op("pool", lambda e: e.tensor_copy(out=bd.t[:], in_=bdf.t[:]), reads=[bdf.b[0]], writes=[bd.b[0]])
        S.op("pool", lambda e: e.tensor_copy(out=gwb.t[:], in_=gw.t[:]), reads=[gw.b[0]], writes=[gwb.b[0]])
        S.op("pool", lambda e: e.tensor_copy(out=maskb.t[:, 0, :], in_=maskf.t[:]), reads=[maskf.b[0]], writes=[maskb.b[0]])
        S.op("pool", lambda e: e.memset(maskb.t[:, 1, :], 1.0), writes=[maskb.b[0]])
        for i in range(3):
            for hh in range(4):
                k = rP()
                pv = P.t[:, k, :].bitcast(BF16)
                S.op("pe", lambda e, i=i, hh=hh, pv=pv: e.transpose(out=pv[:, 0:128], in_=bd.t[:, i, hh, :], identity=ident.t[:]),
                     reads=[bd.b[0], ident.b[0]], writes=[P.b[k]])
                S.op("act", lambda e, i=i, hh=hh, pv=pv: e.activation(out=bdT.t[:, i, hh, :], in_=pv[:, 0:128], func=AF.Copy),
                     reads=[P.b[k]], writes=[bdT.b[0]])
        for cc in range(4):
            for j in range(5):
                S.op("dve", lambda e, cc=cc, j=j: e.tensor_scalar(out=cdiag.t[:, cc, j, :], in0=identf.t[:],
                                                                  scalar1=cwT.t[:, cc, j:j + 1], scalar2=None, op0=ALU.mult),
                     reads=[identf.b[0], cwT.b[0]], writes=[cdiag.b[0]])
        for cc in range(4):
            k = rP()
            S.op("pe", lambda e, cc=cc, k=k: e.matmul(out=P.t[:, k, 0:8], lhsT=bdT.t[:, 0, cc, :], rhs=gwb.t[:, cc, :],
                                                      start=True, stop=False), reads=[bdT.b[0], gwb.b[0]], writes=[P.b[k]])
            S.op("pe", lambda e, cc=cc, k=k: e.matmul(out=P.t[:, k, 0:8], lhsT=bdT.t[:, 1, cc, :], rhs=gwb.t[:, 4 + cc, :],
                                                      start=False, stop=True), reads=[bdT.b[0], gwb.b[0]], writes=[P.b[k]])
            S.op("pe", lambda e, cc=cc, k=k: e.matmul(out=P.t[:, k, 8:16], lhsT=bdT.t[:, 2, cc, :], rhs=gwb.t[:, 8 + cc, :],
                                                      start=True, stop=True), reads=[bdT.b[0], gwb.b[0]], writes=[P.b[k]])
            S.op("dve", lambda e, cc=cc, k=k: e.tensor_copy(out=Gf.t[:, :, cc, :], in_=P.t[:, k, 0:16].rearrange("p (a n) -> p a n", a=2)),
                 reads=[P.b[k]], writes=[Gf.b[0]])
        S.op("pool", lambda e: e.memset(Cst.t[:], 0.0), writes=[Cst.b[0]])
        S.op("pool", lambda e: e.memset(Cbf.t[:], 0.0), writes=[Cbf.b[0]])
        for i in range(6):
            S.op("pool", lambda e, i=i: e.memset(vext.t[:, i, :, 128:130], 1.0), writes=[vext.b[i]])
        if bwd:
            S.dma("sync", mhg.t[:], mh_norm.rearrange("h d -> (h d)").partition_broadcast(128), writes=[mhg.b[0]])
            S.dma("sync", skT.t[:], skip.rearrange("(cc p) -> p cc", p=128), writes=[skT.b[0]])
            for cc in range(4):
                S.op("dve", lambda e, cc=cc: e.tensor_scalar(out=sdiag.t[:, cc, :], in0=identf.t[:], scalar1=skT.t[:, cc:cc + 1],
                                                             scalar2=None, op0=ALU.mult),
                     reads=[identf.b[0], skT.b[0]], writes=[sdiag.b[0]])
            wsrc = w_out.rearrange("(kc p) n -> p kc n", p=128)
            S.dma("pool", wout.t[:, 0:4, :], wsrc[:, 0:4, :], writes=[wout.b[0]])
            S.dma("pool", wout.t[:, 4:8, :], wsrc[:, 4:8, :], writes=[wout.b[0]])

        xmv = xmT_pad.rearrange("(cc p) t -> p cc t", p=128)
        tiles_slot = {}
        rA, rxm, rxf = Rot(4), Rot(3), Rot(2)

        def stageA(ti):
            tok0 = ti * 512
            f = rxf()
            a = rxm()
            S.dma("sync", xmf.t[:, f], xmv[:, :, tok0:tok0 + 516], writes=[xmf.b[f]])
            S.op("pool", lambda e: e.tensor_copy(out=xmb.t[:, a], in_=xmf.t[:, f]), reads=[xmf.b[f]], writes=[xmb.b[a]])
            sl = rA()
            for cc in range(4):
                k = rP()
                for j in range(5):
                    S.op("pe", lambda e, cc=cc, j=j, k=k: e.matmul(out=P.t[:, k, :], lhsT=cdiag.t[:, cc, j, :],
                                                                   rhs=xmb.t[:, a, cc, j:j + 512], start=(j == 0), stop=(j == 4)),
                         reads=[cdiag.b[0], xmb.b[a]], writes=[P.b[k]])
                S.op("act", lambda e, cc=cc, k=k: e.activation(out=xcT.t[:, sl, cc, :], in_=P.t[:, k, :], func=AF.Silu,
                                                               bias=cbT.t[:, cc:cc + 1]),
                     reads=[P.b[k], cbT.b[0]], writes=[xcT.b[sl]])
            tiles_slot[ti] = (a, sl)

        rG, rQ, rK, rT, rS, rH, rD = Rot(8), Rot(4), Rot(6), Rot(5), Rot(3), Rot(4), Rot(4)
        sg_, sq_, sk_, st_, ss_, sh_, sd_ = {}, {}, {}, {}, {}, {}, {}

        def S1(c):
            ti, m = divmod(c, 4)
            a, sl = tiles_slot[ti]
            b = rG()
            sg_[c] = b
            kg = rP()
            for i in range(8):
                cc = i % 4
                lhsT = (xcT.t[:, sl, cc, m * 128:(m + 1) * 128] if i < 4 else xmb.t[:, a, cc, 2 + m * 128:2 + (m + 1) * 128])
                S.op("pe", lambda e, i=i, cc=cc, lhsT=lhsT: e.matmul(out=P.t[:, kg, 0:8], lhsT=lhsT, rhs=Gf.t[:, i // 4, cc, :],
                                                                     start=(i == 0), stop=(i == 7)),
                     reads=[xcT.b[sl], xmb.b[a], Gf.b[0]], writes=[P.b[kg]])
            S.op("dve", lambda e: e.tensor_tensor(out=gsb.t[:, b, :], in0=P.t[:, kg, 0:8], in1=gbb.t[:], op=ALU.add),
                 reads=[P.b[kg], gbb.b[0]], writes=[gsb.b[b]])
            S.op("act", lambda e: e.activation(out=lfn.t[:, b, 0:4], in_=gsb.t[:, b, 4:8], func=AF.Exp, scale=-1.0),
                 reads=[gsb.b[b]], writes=[lfn.b[b]])
            S.op("act", lambda e: e.activation(out=lfn.t[:, b, 4:8], in_=lfn.t[:, b, 0:4], func=AF.Ln, bias=1.0),
                 reads=[lfn.b[b]], writes=[lfn.b[b]])

        def S2(c):
            b = sg_[c]
            kg = rP()
            S.op("dve", lambda e: e.tensor_copy(out=lh.t[:, b, 0, :], in_=lfn.t[:, b, 4:8]), reads=[lfn.b[b]], writes=[lh.b[b]])
            S.op("dve", lambda e: e.tensor_tensor(out=lh.t[:, b, 1, :], in0=lfn.t[:, b, 4:8], in1=lh.t[:, b, 0, :], op=ALU.subtract),
                 reads=[lfn.b[b], lh.b[b]], writes=[lh.b[b]])
            for (o0, mi) in ((8, 0), (12, 1)):
                for part in range(2):
                    S.op("pe", lambda e, o0=o0, mi=mi, part=part: e.matmul(
                        out=P.t[:, kg, o0:o0 + 4], lhsT=maskb.t[:, mi, :], rhs=lh.t[:, b, part, :],
                        start=(part == 0), stop=(part == 1)), reads=[maskb.b[0], lh.b[b]], writes=[P.b[kg]])
            S.op("act", lambda e: e.activation(out=ebt.t[:, b, 0:8], in_=P.t[:, kg, 8:16], func=AF.Exp, scale=-1.0),
                 reads=[P.b[kg]], writes=[ebt.b[b]])
            S.op("dve", lambda e: e.tensor_tensor(out=gsb.t[:, b, 4:8], in0=P.t[:, kg, 8:12], in1=gsb.t[:, b, 0:4], op=ALU.add),
                 reads=[P.b[kg], gsb.b[b]], writes=[gsb.b[b]])
            S.op("act", lambda e: e.activation(out=ebt.t[:, b, 8:12], in_=gsb.t[:, b, 4:8], func=AF.Exp),
                 reads=[gsb.b[b]], writes=[ebt.b[b]])

        def S3(c):
            ti, m = divmod(c, 4)
            a, sl = tiles_slot[ti]
            b = sg_[c]
            q_, k_ = rQ(), rK()
            sq_[c], sk_[c] = q_, k_
            kq, kk, kv = rP(), rP(), rP()
            for (kx, wi, src) in ((kq, 0, "xc"), (kk, 1, "xc"), (kv, 2, "xm")):
                for hh in range(4):
                    lhsT = (xcT.t[:, sl, hh, m * 128:(m + 1) * 128] if src == "xc"
                            else xmb.t[:, a, hh, 2 + m * 128:2 + (m + 1) * 128])
                    S.op("pe", lambda e, kx=kx, wi=wi, hh=hh, lhsT=lhsT: e.matmul(
                        out=P.t[:, kx, hh * 128:(hh + 1) * 128], lhsT=lhsT, rhs=bd.t[:, wi, hh, :], start=True, stop=True),
                         reads=[xcT.b[sl], xmb.b[a], bd.b[0]], writes=[P.b[kx]])
            S.op("dve", lambda e: e.tensor_tensor(
                out=qs.t[:, q_], in0=P.t[:, kq, :].rearrange("p (h d) -> p h d", h=4),
                in1=ebt.t[:, b, 0:4].unsqueeze(2).broadcast_to([128, 4, 128]), op=ALU.mult),
                 reads=[P.b[kq], ebt.b[b]], writes=[qs.b[q_]])
            S.op("dve", lambda e: e.tensor_tensor(
                out=ks.t[:, k_], in0=P.t[:, kk, :].rearrange("p (h d) -> p h d", h=4),
                in1=ebt.t[:, b, 8:12].unsqueeze(2).broadcast_to([128, 4, 128]), op=ALU.mult),
                 reads=[P.b[kk], ebt.b[b]], writes=[ks.b[k_]])
            S.op("act", lambda e: e.activation(out=vext.t[:, k_, :, 0:128], in_=P.t[:, kv, :].rearrange("p (h d) -> p h d", h=4),
                                               func=AF.Copy), reads=[P.b[kv]], writes=[vext.b[k_]])

        def S4(c):
            q_, k_ = sq_[c], sk_[c]
            t_ = rT()
            st_[c] = t_
            kt = rP()
            ptv = P.t[:, kt, :].bitcast(BF16)
            for i, (src, sl_) in enumerate(((qs, q_), (ks, k_))):
                for hh in range(4):
                    S.op("pe", lambda e, i=i, hh=hh, src=src, sl_=sl_: e.transpose(
                        out=ptv[:, (i * 4 + hh) * 128:(i * 4 + hh + 1) * 128], in_=src.t[:, sl_, hh, :], identity=ident.t[:]),
                         reads=[src.b[sl_], ident.b[0]], writes=[P.b[kt]])
            S.op("act", lambda e: e.activation(out=qkT.t[:, t_].rearrange("p a h t -> p (a h t)"), in_=ptv, func=AF.Copy),
                 reads=[P.b[kt]], writes=[qkT.b[t_]])

        def S5(c):
            t_ = st_[c]
            k1 = rP()
            for hh in range(4):
                S.op("pe", lambda e, hh=hh: e.matmul(out=P.t[:, k1, hh * 128:(hh + 1) * 128], lhsT=qkT.t[:, t_, 1, hh, :],
                                                     rhs=qkT.t[:, t_, 0, hh, :], start=True, stop=True),
                     reads=[qkT.b[t_]], writes=[P.b[k1]])
            s = rS()
            ss_[c] = s
            S.op("dve", lambda e: e.tensor_tensor(out=Sm.t[:, s], in0=P.t[:, k1, :].rearrange("p (h j) -> p h j", h=4),
                                                  in1=maskf.t[:].unsqueeze(1).broadcast_to([128, 4, 128]), op=ALU.mult),
                 reads=[P.b[k1], maskf.b[0]], writes=[Sm.b[s]])

        def S6(c):
            b, k_, t_, s = sg_[c], sk_[c], st_[c], ss_[c]
            kd = [rP(), rP()]
            kn = [rP(), rP()]
            for hh in range(4):
                hp, h2 = divmod(hh, 2)
                S.op("pe", lambda e, hh=hh, hp=hp, h2=h2: e.matmul(out=P.t[:, kd[hp], h2 * 256:h2 * 256 + 129],
                                                                   lhsT=ks.t[:, k_, hh, :], rhs=vext.t[:, k_, hh, 0:129],
                                                                   start=True, stop=True),
                     reads=[ks.b[k_], vext.b[k_]], writes=[P.b[kd[hp]]])
            for hh in range(4):
                hp, h2 = divmod(hh, 2)
                o = P.t[:, kn[hp], h2 * 256:h2 * 256 + 129]
                S.op("pe", lambda e, hh=hh, o=o: e.matmul(out=o, lhsT=Sm.t[:, s, hh, :], rhs=vext.t[:, k_, hh, 0:129],
                                                          start=True, stop=False),
                     reads=[Sm.b[s], vext.b[k_]], writes=[P.b[kn[hp]]])
                S.op("pe", lambda e, hh=hh, o=o: e.matmul(out=o, lhsT=qkT.t[:, t_, 0, hh, :], rhs=Cbf.t[:, hh, 0:129],
                                                          start=False, stop=True),
                     reads=[qkT.b[t_], Cbf.b[0]], writes=[P.b[kn[hp]]])
            for hp in range(2):
                S.op("dve", lambda e, hp=hp: e.tensor_tensor(
                    out=Cst.t[:, hp * 2:hp * 2 + 2, :], in0=P.t[:, kd[hp], :].rearrange("p (h x) -> p h x", h=2)[:, :, 0:129],
                    in1=Cst.t[:, hp * 2:hp * 2 + 2, :], op=ALU.add), reads=[P.b[kd[hp]], Cst.b[0]], writes=[Cst.b[0]])
            S.op("dve", lambda e: e.tensor_tensor(out=Cst.t[:], in0=Cst.t[:],
                                                  in1=ebt.t[:, b, 4:8].unsqueeze(2).broadcast_to([128, 4, 129]), op=ALU.mult),
                 reads=[Cst.b[0], ebt.b[b]], writes=[Cst.b[0]])
            S.op("pool", lambda e: e.tensor_copy(out=Cbf.t[:, :, 0:129], in_=Cst.t[:]), reads=[Cst.b[0]], writes=[Cbf.b[0]])
            dn = s
            for hp in range(2):
                S.op("dve", lambda e, hp=hp: e.tensor_copy(
                    out=den.t[:, dn, hp * 2:hp * 2 + 2].unsqueeze(2),
                    in_=P.t[:, kn[hp], :].rearrange("p (h x) -> p h x", h=2)[:, :, 128:129]),
                     reads=[P.b[kn[hp]]], writes=[den.b[dn]])
            S.op("dve", lambda e: e.scalar_tensor_tensor(out=den.t[:, dn, 4:8], in0=den.t[:, dn, 0:4], scalar=-1.0,
                                                         in1=den.t[:, dn, 0:4], op0=ALU.mult, op1=ALU.max),
                 reads=[den.b[dn]], writes=[den.b[dn]])
            S.op("dve", lambda e: e.tensor_scalar(out=den.t[:, dn, 0:4], in0=den.t[:, dn, 4:8], scalar1=1.0, scalar2=None,
                                                  op0=ALU.max), reads=[den.b[dn]], writes=[den.b[dn]])
            S.op("dve", lambda e: e.reciprocal(out=den.t[:, dn, 4:8], in_=den.t[:, dn, 0:4]), reads=[den.b[dn]], writes=[den.b[dn]])
            h = rH()
            sh_[c] = h
            for hp in range(2):
                S.op("dve", lambda e, hp=hp: e.tensor_tensor(
                    out=hd.t[:, h, hp * 2:hp * 2 + 2, :],
                    in0=P.t[:, kn[hp], :].rearrange("p (h x) -> p h x", h=2)[:, :, 0:128],
                    in1=den.t[:, dn, 4 + hp * 2:6 + hp * 2].unsqueeze(2).broadcast_to([128, 2, 128]), op=ALU.mult),
                     reads=[P.b[kn[hp]], den.b[dn]], writes=[hd.b[h]])
            if not bwd:
                S.dma("sync", hf[c * 128:(c + 1) * 128, :], hd.t[:, h].rearrange("p h d -> p (h d)"), reads=[hd.b[h]])

        def D1(c):
            ti, m = divmod(c, 4)
            a, sl = tiles_slot[ti]
            h = sh_[c]
            d = rD()
            sd_[c] = d
            t0 = c * 128
            S.dma("sync", hfl.t[:, d, :], hf[t0:t0 + 128, :], writes=[hfl.b[d]])
            S.dma("sync", sgl.t[:, d, :], sog[t0:t0 + 128, :], writes=[sgl.b[d]])
            S.dma("sync", ycT.t[:, d, 0:4, :], ysguT.rearrange("(h p) t -> p h t", p=128)[:, :, t0:t0 + 128], writes=[ycT.b[d]])
            S.dma("sync", xr.t[:, d, :], xin[t0:t0 + 128, :], writes=[xr.b[d]])
            S.op("pool", lambda e: e.tensor_tensor(out=hfl.t[:, d, :], in0=hfl.t[:, d, :],
                                                   in1=hd.t[:, h].rearrange("p h d -> p (h d)"), op=ALU.add),
                 reads=[hfl.b[d], hd.b[h]], writes=[hfl.b[d]])
            for hh in range(4):
                S.op("dve", lambda e, hh=hh: e.bn_stats(out=bst.t[:, d, hh, :], in_=hfl.t[:, d, hh * 128:(hh + 1) * 128]),
                     reads=[hfl.b[d]], writes=[bst.b[d]])
            for hh in range(4):
                S.op("dve", lambda e, hh=hh: e.bn_aggr(out=mv.t[:, d, hh, :], in_=bst.t[:, d, hh, :]),
                     reads=[bst.b[d]], writes=[mv.b[d]])
            rstd_ops(S, rs.t[:, d, :], mv.t[:, d, :, 1], pw.t[:], mv.b[d], rs.b[d])

        def D1b(c):
            ti, m = divmod(c, 4)
            a, sl = tiles_slot[ti]
            d = sd_[c]
            for hh in range(4):
                S.op("dve", lambda e, hh=hh: e.tensor_scalar(
                    out=hn.t[:, d, hh * 128:(hh + 1) * 128], in0=hfl.t[:, d, hh * 128:(hh + 1) * 128],
                    scalar1=mv.t[:, d, hh, 0:1], scalar2=rs.t[:, d, hh:hh + 1], op0=ALU.subtract, op1=ALU.mult),
                     reads=[hfl.b[d], mv.b[d], rs.b[d]], writes=[hn.b[d]])
            S.op("dve", lambda e: e.tensor_tensor(out=hn.t[:, d, :], in0=hn.t[:, d, :], in1=mhg.t[:], op=ALU.mult),
                 reads=[hn.b[d], mhg.b[0]], writes=[hn.b[d]])
            kx = rP()
            for hh in range(4):
                S.op("pe", lambda e, hh=hh: e.matmul(out=P.t[:, kx, hh * 128:(hh + 1) * 128],
                                                     lhsT=xcT.t[:, sl, hh, m * 128:(m + 1) * 128], rhs=sdiag.t[:, hh, :],
                                                     start=True, stop=True), reads=[xcT.b[sl], sdiag.b[0]], writes=[P.b[kx]])
            S.op("dve", lambda e: e.tensor_tensor(out=hn.t[:, d, :], in0=P.t[:, kx, :], in1=hn.t[:, d, :], op=ALU.add),
                 reads=[P.b[kx], hn.b[d]], writes=[hn.b[d]])
            S.op("pool", lambda e: e.tensor_tensor(out=ym.t[:, d, :], in0=hn.t[:, d, :], in1=sgl.t[:, d, :], op=ALU.mult),
                 reads=[hn.b[d], sgl.b[d]], writes=[ym.b[d]])

        def D2(c):
            d = sd_[c]
            kt = rP()
            ptv = P.t[:, kt, :].bitcast(BF16)
            for hh in range(4):
                S.op("pe", lambda e, hh=hh: e.transpose(out=ptv[:, hh * 128:(hh + 1) * 128], in_=ym.t[:, d, hh * 128:(hh + 1) * 128],
                                                        identity=ident.t[:]), reads=[ym.b[d], ident.b[0]], writes=[P.b[kt]])
            S.op("act", lambda e: e.activation(out=ycT.t[:, d, 4:8, :].rearrange("p h t -> p (h t)"), in_=ptv[:, 0:512], func=AF.Copy),
                 reads=[P.b[kt]], writes=[ycT.b[d]])

        def D3(c):
            d = sd_[c]
            t0 = c * 128
            for half in range(2):
                ko = rP()
                for cc in range(8):
                    S.op("pe", lambda e, cc=cc, half=half, ko=ko: e.matmul(
                        out=P.t[:, ko, :], lhsT=ycT.t[:, d, cc, :], rhs=wout.t[:, cc, half * 512:(half + 1) * 512],
                        start=(cc == 0), stop=(cc == 7)), reads=[ycT.b[d], wout.b[0]], writes=[P.b[ko]])
                S.op("dve", lambda e, half=half, ko=ko: e.tensor_tensor(
                    out=xr.t[:, d, half * 512:(half + 1) * 512], in0=P.t[:, ko, :], in1=xr.t[:, d, half * 512:(half + 1) * 512],
                    op=ALU.add), reads=[P.b[ko], xr.b[d]], writes=[xr.b[d]])
            S.dma("pool", xout[t0:t0 + 128, :], xr.t[:, d, :], reads=[xr.b[d]])

        order = list(range(NCH))
        if bwd:
            order.reverse()
        done_tiles = set()
        stages = [(S6, 0), (S5, 1), (S4, 2), (S3, 3), (S2, 4), (S1, 5)]
        if bwd:
            stages = [(S6, 0), (D1, -1), (D1b, -2), (D2, -3), (D3, -4), (S5, 1), (S4, 2), (S3, 3), (S2, 4), (S1, 5)]
        for i in range(-5, NCH + 4):
            for la in (5, 7):
                if 0 <= i + la < NCH and order[i + la] // 4 not in done_tiles:
                    stageA(order[i + la] // 4)
                    done_tiles.add(order[i + la] // 4)
            for fn, off in stages:
                if 0 <= i + off < NCH:
                    fn(order[i + off])
            if conv_items and i >= 0:
                for _ in range(conv_per_step):
                    if conv_items:
                        conv_items.pop(0)()
        while conv_items:
            conv_items.pop(0)()
        S.emit()


DEPTH = 2
SEQ = 8192
NCORES = 4

PARAM_SHAPES = {
    'ffn1_norm': (DEPTH, D), 'ffn1_w_gate': (DEPTH, D, HID), 'ffn1_w_up': (DEPTH, D, HID), 'ffn1_w_down': (DEPTH, HID, D),
    'mix_norm': (DEPTH, D), 'w_in': (DEPTH, D, 2048), 'sgu_norm': (DEPTH, 4, 128), 'sgu_w': (DEPTH, 4, 128, 128),
    'sgu_b': (DEPTH, 4, 128), 'conv_w': (DEPTH, 5, 512), 'conv_b': (DEPTH, 512), 'w_q': (DEPTH, 128, 4, 4),
    'w_k': (DEPTH, 128, 4, 4), 'w_v': (DEPTH, 128, 4, 4), 'gate_w_fwd': (DEPTH, 1536, 8), 'gate_b_fwd': (DEPTH, 8),
    'gate_w_bwd': (DEPTH, 1536, 8), 'gate_b_bwd': (DEPTH, 8), 'mh_norm': (DEPTH, 4, 128), 'mlstm_skip': (DEPTH, 512),
    'w_out': (DEPTH, D, D), 'ffn2_norm': (DEPTH, D), 'ffn2_w_gate': (DEPTH, D, HID), 'ffn2_w_up': (DEPTH, D, HID),
    'ffn2_w_down': (DEPTH, HID, D), 'final_norm': (D,),
}


def build_program(T=SEQ, depth=DEPTH):
    nc = bass.Bass("TRN2", target_bir_lowering=False)
    x = nc.dram_tensor("x", [T, D], F32, kind="ExternalInput").ap()
    p = {k: nc.dram_tensor(k, list(s), F32, kind="ExternalInput").ap() for k, s in PARAM_SHAPES.items()}
    y = nc.dram_tensor("y", [T, D], F32, kind="ExternalOutput").ap()
    xres = nc.dram_tensor("xres", [T, D], F32, kind="Internal").ap()
    wguA = nc.dram_tensor("wguA", [NJ, 128, 2, 8, 128], BF16, kind="Internal").ap()
    wguB = nc.dram_tensor("wguB", [NJ, 128, 2, 8, 128], BF16, kind="Internal").ap()
    xmT = nc.dram_tensor("xmT", [512, T + 4], F32, kind="Internal").ap()
    sog = nc.dram_tensor("sog", [T, 512], BF16, kind="Internal").ap()
    ysguT = nc.dram_tensor("ysguT", [512, T], BF16, kind="Internal").ap()
    hf = nc.dram_tensor("hf", [T, 512], F32, kind="Internal").ap()
    C = Ctx(nc)
    for l in range(depth):
        last = (l == depth - 1)
        if l == 0:
            convert_wgu_phase(C, p['ffn1_w_gate'][l], p['ffn1_w_up'][l], wguA)
        ffn_phase(C, T, x if l == 0 else xres, xres, p['ffn1_norm'][l], wguA, p['ffn1_w_down'][l])
        mixa_phase(C, T, xres, p['mix_norm'][l], p['w_in'][l], p['sgu_norm'][l], p['sgu_w'][l], p['sgu_b'][l],
                   xmT, sog, ysguT, AF.Gelu_apprx_tanh)
        mixb_phase(C, T, False, xmT, hf, p['conv_w'][l], p['conv_b'][l], p['w_q'][l], p['w_k'][l], p['w_v'][l],
                   p['gate_w_fwd'][l], p['gate_b_fwd'][l])
        mixb_phase(C, T, True, xmT, hf, p['conv_w'][l], p['conv_b'][l], p['w_q'][l], p['w_k'][l], p['w_v'][l],
                   p['gate_w_bwd'][l], p['gate_b_bwd'][l], sog=sog, ysguT=ysguT, mh_norm=p['mh_norm'][l],
                   skip=p['mlstm_skip'][l], w_out=p['w_out'][l], xin=xres, xout=xres,
                   conv_jobs=[(p['ffn2_w_gate'][l], p['ffn2_w_up'][l], wguB)] +
                   ([] if last else [(p['ffn1_w_gate'][l + 1], p['ffn1_w_up'][l + 1], wguA)]))
        ffn_phase(C, T, xres, y if last else xres, p['ffn2_norm'][l], wguB, p['ffn2_w_down'][l],
                  final_g=p['final_norm'] if last else None)
    return nc


def kernel(**inputs):
    x = np.ascontiguousarray(np.asarray(inputs['x'], dtype=np.float32))
    B = x.shape[0]
    params = {k: np.ascontiguousarray(np.asarray(inputs[k], dtype=np.float32)) for k in PARAM_SHAPES}
    nc = build_program()
    in_maps = [dict(params, x=x[b]) for b in range(B)]
    res = run_bass_kernel_spmd(nc, in_maps, core_ids=list(range(B)))
    return np.stack([np.asarray(res.results[b]["y"], dtype=np.float32) for b in range(B)], axis=0)
```

```python
import contextlib
import numpy as np
import ml_dtypes
import concourse.bass as bass
import concourse.mybir as mybir
from concourse.bass_utils import run_bass_kernel_spmd

F32 = mybir.dt.float32
BF16 = mybir.dt.bfloat16
AF = mybir.ActivationFunctionType
ALU = mybir.AluOpType

D = 1024
HID = 2816
NJ = HID // 128
EPS = 1e-6
ENGS = ("sync", "act", "dve", "pool", "pe")


class Buf:
    __slots__ = ("w", "r")

    def __init__(self):
        self.w = None
        self.r = []


class Op:
    __slots__ = ("eng", "fn", "deps", "ev", "dma")


class DmaPool:
    def __init__(self, nc, eng, n):
        self.slots = [[nc.alloc_semaphore(name=f"dq_{eng}_{i}"), 0, None] for i in range(n)]
        self.i = 0


class Sched:
    def __init__(self, nc, pools, tag):
        self.nc = nc
        self.pools = pools
        self.lists = {e: [] for e in ENGS}
        self.esem = {e: nc.alloc_semaphore(name=f"e_{tag}_{e}") for e in ENGS if e != "sync"}
        self.ecnt = {e: 0 for e in ENGS}

    def _deps(self, reads, writes):
        deps = []
        for b in reads:
            if b.w is not None:
                deps.append(b.w)
        for b in writes:
            if b.w is not None:
                deps.append(b.w)
            deps.extend(b.r)
        return deps

    def _commit(self, o, reads, writes):
        for b in reads:
            b.r.append(o)
        for b in writes:
            b.w = o
            b.r = []
        self.lists[o.eng].append(o)

    def op(self, eng, fn, reads=(), writes=()):
        o = Op()
        o.eng, o.fn, o.dma = eng, fn, False
        o.deps = self._deps(reads, writes)
        self.ecnt[eng] += 1
        o.ev = (self.esem[eng], self.ecnt[eng])
        self._commit(o, reads, writes)
        return o

    def dma(self, eng, out, in_, reads=(), writes=(), **kw):
        o = Op()
        o.eng, o.dma = eng, True
        o.fn = lambda e: e.dma_start(out=out, in_=in_, **kw)
        o.deps = self._deps(reads, writes)
        pool = self.pools[eng]
        slot = pool.slots[pool.i % len(pool.slots)]
        pool.i += 1
        if slot[2] is not None:
            o.deps.append(slot[2])
        slot[1] += 16
        slot[2] = o
        o.ev = (slot[0], slot[1])
        self._commit(o, reads, writes)
        return o

    def emit(self):
        nc = self.nc
        finals = [(s, c) for e, s in self.esem.items() for c in [self.ecnt[e]] if c > 0]
        for p in self.pools.values():
            for s, c, last in p.slots:
                if c > 0:
                    finals.append((s, c))
        names = {"sync": "sync", "act": "scalar", "dve": "vector", "pool": "gpsimd", "pe": "tensor"}
        with nc.Block() as blk:
            for eng in ENGS:
                def body(e, eng=eng):
                    waited = {}
                    for o in self.lists[eng]:
                        need = {}
                        for d in o.deps:
                            if d.eng == "pe" and eng == "pe" and not d.dma:
                                continue
                            s, v = d.ev
                            k = id(s)
                            if need.get(k, (None, 0))[1] < v:
                                need[k] = (s, v)
                        for k, (s, v) in need.items():
                            if waited.get(k, 0) >= v:
                                continue
                            e.wait_ge(s, v)
                            waited[k] = v
                        ins = o.fn(e)
                        ins.then_inc(o.ev[0], 16 if o.dma else 1)
                    for s, v in finals:
                        if waited.get(id(s), 0) < v:
                            e.wait_ge(s, v)
                getattr(blk, names[eng])(body)


class Ctx:
    def __init__(self, nc):
        self.nc = nc
        self.pools = {"sync": DmaPool(nc, "sync", 24), "pool": DmaPool(nc, "pool", 12), "act": DmaPool(nc, "act", 6)}
        self.nphase = 0

    def sched(self):
        self.nphase += 1
        return Sched(self.nc, self.pools, f"p{self.nphase}")


class Tile:
    def __init__(self, t, nslots=1):
        self.t = t
        self.b = [Buf() for _ in range(nslots)]


_UID = [0]


def _uname(name):
    _UID[0] += 1
    return f"t{_UID[0]}_{name}"


def sb(es, nc, name, shape, dt, nslots=1):
    t = es.enter_context(nc.sbuf_tensor(_uname(name), [128 if shape[0] is None else shape[0]] + list(shape[1:]), dt))
    return Tile(t, nslots)


def ps(es, nc, name, shape, dt, nslots=1):
    t = es.enter_context(nc.psum_tensor(_uname(name), list(shape), dt))
    return Tile(t, nslots)


def convert_wgu_phase(C, wg, wu, wgu):
    nc = C.nc
    S = C.sched()
    with contextlib.ExitStack() as es:
        wst = sb(es, nc, "wst", [128, 2, 2, 8, 256], F32, 2)
        wbf = sb(es, nc, "wbf", [128, 2, 2, 2, 8, 128], BF16, 2)
        engs = ("dve", "pool", "act", "dve")
        for jp in range(NJ // 2):
            sl = jp % 2
            for a, w in enumerate((wg, wu)):
                src = w.rearrange("(kc p) n -> p kc n", p=128)[:, :, jp * 256:(jp + 1) * 256]
                S.dma("sync", wst.t[:, sl, a, :, :], src, writes=[wst.b[sl]])
            i = 0
            for jj in range(2):
                for a in range(2):
                    eng = engs[i]; i += 1
                    o = wbf.t[:, sl, jj, a, :, :]
                    src = wst.t[:, sl, a, :, jj * 128:(jj + 1) * 128]
                    if eng == "act":
                        S.op("act", lambda e, o=o, src=src: e.activation(out=o, in_=src, func=AF.Copy),
                             reads=[wst.b[sl]], writes=[wbf.b[sl]])
                    else:
                        S.op(eng, lambda e, o=o, src=src: e.tensor_copy(out=o, in_=src),
                             reads=[wst.b[sl]], writes=[wbf.b[sl]])
            S.dma("sync", wgu[jp * 2:jp * 2 + 2].rearrange("j p a k c -> p j (a k c)"),
                  wbf.t[:, sl].rearrange("p j a k c -> p j (a k c)"), reads=[wbf.b[sl]])
        S.emit()


def convert_items(S, es, nc, wg, wu, wgu):
    wst = sb(es, nc, "cwst", [128, 1, 2, 8, 128], F32, 1)
    wbf = sb(es, nc, "cwbf", [128, 1, 2, 8, 128], BF16, 1)
    r = Rot(1)
    items = []
    for j in range(NJ):
        def it(j=j):
            sl = r()
            for a, w in enumerate((wg, wu)):
                src = w.rearrange("(kc p) n -> p kc n", p=128)[:, :, j * 128:(j + 1) * 128]
                S.dma("sync", wst.t[:, sl, a, :, :], src, writes=[wst.b[sl]])
            S.op("act", lambda e: e.activation(out=wbf.t[:, sl, 0], in_=wst.t[:, sl, 0], func=AF.Copy),
                 reads=[wst.b[sl]], writes=[wbf.b[sl]])
            S.op("act", lambda e: e.activation(out=wbf.t[:, sl, 1], in_=wst.t[:, sl, 1], func=AF.Copy),
                 reads=[wst.b[sl]], writes=[wbf.b[sl]])
            S.dma("sync", wgu[j].rearrange("p a k c -> p (a k c)"), wbf.t[:, sl].rearrange("p a k c -> p (a k c)"),
                  reads=[wbf.b[sl]])
        items.append(it)
    return items


def ffn_phase(C, T, xin, xout, gvec, wgu, wd_f32, final_g=None):
    nc = C.nc
    S = C.sched()
    NG = T // 1024
    with contextlib.ExitStack() as es:
        ident = sb(es, nc, "ident", [128, 128], BF16)
        gb = sb(es, nc, "gb", [128, D], F32)
        gfb = sb(es, nc, "gfb", [128, D], F32)
        pw = sb(es, nc, "pw", [128, 1], F32)
        wd = sb(es, nc, "wd", [128, NJ, D], BF16)
        wgus = sb(es, nc, "wgus", [128, 3, 2, 2, 8, 128], BF16, 3)
        xprep = sb(es, nc, "xprep", [128, 4, D], F32, 4)
        xres = sb(es, nc, "xres", [128, 3, D], F32, 3)
        xn = sb(es, nc, "xn", [128, 2, D], BF16, 2)
        stat = sb(es, nc, "stat", [128, 8, 4], F32, 8)
        xnT = sb(es, nc, "xnT", [128, 2, 8, 1024], BF16, 2)
        aT = sb(es, nc, "aT", [128, NJ, 1024], BF16, 1)
        sg = sb(es, nc, "sg", [128, 2, 512], BF16, 2)
        junk = sb(es, nc, "junk", [128, D], BF16, 1)
        pg = ps(es, nc, "pg", [128, 2, 512], F32, 2)
        pu = ps(es, nc, "pu", [128, 2, 512], F32, 2)
        pt = ps(es, nc, "pt", [128, 2, 1024], BF16, 2)
        po = ps(es, nc, "po", [128, 2, 512], F32, 2)

        S.op("pool", lambda e: e.memset(ident.t[:], 0.0), writes=[ident.b[0]])
        S.op("pool", lambda e: e.affine_select(out=ident.t[:], in_=ident.t[:], pattern=[[-1, 128]],
                                               compare_op=ALU.not_equal, fill=1.0, base=0, channel_multiplier=1),
             reads=[ident.b[0]], writes=[ident.b[0]])
        S.op("pool", lambda e: e.memset(pw.t[:], -0.5), writes=[pw.b[0]])
        S.dma("sync", gb.t[:], gvec.partition_broadcast(128), writes=[gb.b[0]])
        if final_g is not None:
            S.dma("sync", gfb.t[:], final_g.partition_broadcast(128), writes=[gfb.b[0]])
        wdsrc = wd_f32.rearrange("(j p) n -> p j n", p=128)
        for j0 in range(0, NJ, 2):
            S.dma("pool", wd.t[:, j0:j0 + 2, :], wdsrc[:, j0:j0 + 2, :], writes=[wd.b[0]])

        cnt = {"xp": 0, "xn": 0, "st": 0, "pt": 0, "wg": 0, "g": 0, "sg": 0, "po": 0, "xr": 0}

        def prep_items(g):
            fronts, backs = [], []
            slot = g % 2
            for s in range(8):
                def it(s=s):
                    i = cnt["xp"]; cnt["xp"] += 1
                    k = i % 4
                    tok0 = (g * 8 + s) * 128
                    xp = xprep.t[:, k, :]
                    S.dma("sync", xp, xin[tok0:tok0 + 128, :], writes=[xprep.b[k]])
                    q = cnt["st"] % 8; cnt["st"] += 1
                    ss = stat.t[:, q, 0:1]
                    rstd = stat.t[:, q, 1:2]
                    S.op("dve", lambda e: e.scalar_tensor_tensor(out=junk.t[:], in0=xp, scalar=1.0, in1=xp,
                                                                 op0=ALU.mult, op1=ALU.mult, accum_out=ss),
                         reads=[xprep.b[k]], writes=[junk.b[0], stat.b[q]])
                    S.op("pool", lambda e: e.tensor_scalar(out=rstd, in0=ss, scalar1=1.0 / D, scalar2=EPS,
                                                           op0=ALU.mult, op1=ALU.add),
                         reads=[stat.b[q]], writes=[stat.b[q]])
                    S.op("pool", lambda e: e.tensor_tensor(out=rstd, in0=rstd, in1=pw.t[:], op=ALU.pow),
                         reads=[stat.b[q], pw.b[0]], writes=[stat.b[q]])
                    n = cnt["xn"] % 2; cnt["xn"] += 1
                    S.op("dve", lambda e: e.scalar_tensor_tensor(out=xn.t[:, n, :], in0=xp, scalar=rstd, in1=gb.t[:],
                                                                 op0=ALU.mult, op1=ALU.mult),
                         reads=[xprep.b[k], stat.b[q], gb.b[0]], writes=[xn.b[n]])
                    return n

                def bk(s=s, n=None):
                    p = cnt["pt"] % 2; cnt["pt"] += 1
                    for kc in range(8):
                        S.op("pe", lambda e, kc=kc: e.transpose(out=pt.t[:, p, kc * 128:(kc + 1) * 128],
                                                                in_=xn.t[:, n, kc * 128:(kc + 1) * 128],
                                                                identity=ident.t[:]),
                             reads=[xn.b[n], ident.b[0]], writes=[pt.b[p]])
                    S.op("act", lambda e: e.activation(
                        out=xnT.t[:, slot, :, s * 128:(s + 1) * 128],
                        in_=pt.t[:, p, :].rearrange("p (k t) -> p k t", k=8), func=AF.Copy),
                         reads=[pt.b[p]], writes=[xnT.b[slot]])
                fronts.append(it)
                backs.append(bk)
            nsl = {}

            def mk_f(i):
                def f():
                    nsl[i] = fronts[i]()
                return f

            def mk_b(i):
                def f():
                    backs[i](n=nsl[i])
                return f
            order = [mk_f(0)]
            for i in range(1, 8):
                order.append(mk_f(i))
                order.append(mk_b(i - 1))
            order.append(mk_b(7))
            return order

        def gateup(g, extra):
            slot = g % 2
            for j in range(NJ):
                if j % 2 == 0:
                    w = cnt["wg"] % 3; cnt["wg"] += 1
                    S.dma("sync", wgus.t[:, w].rearrange("p j a k c -> p j (a k c)"),
                          wgu[j:j + 2].rearrange("j p a k c -> p j (a k c)"), writes=[wgus.b[w]])
                    wcur = w
                jj = j % 2
                for half in range(2):
                    q = cnt["g"] % 2; cnt["g"] += 1
                    for (pp, a) in ((pg, 0), (pu, 1)):
                        for kc in range(8):
                            S.op("pe", lambda e, pp=pp, a=a, kc=kc, q=q, half=half, wcur=wcur, jj=jj: e.matmul(
                                out=pp.t[:, q, :], lhsT=wgus.t[:, wcur, jj, a, kc, :],
                                rhs=xnT.t[:, slot, kc, half * 512:(half + 1) * 512],
                                start=(kc == 0), stop=(kc == 7)),
                                 reads=[wgus.b[wcur], xnT.b[slot]], writes=[pp.b[q]])
                    r = cnt["sg"] % 2; cnt["sg"] += 1
                    S.op("act", lambda e, q=q, r=r: e.activation(out=sg.t[:, r, :], in_=pg.t[:, q, :], func=AF.Silu),
                         reads=[pg.b[q]], writes=[sg.b[r]])
                    S.op("dve", lambda e, q=q, r=r, j=j, half=half: e.tensor_tensor(
                        out=aT.t[:, j, half * 512:(half + 1) * 512], in0=pu.t[:, q, :], in1=sg.t[:, r, :], op=ALU.mult),
                         reads=[pu.b[q], sg.b[r]], writes=[aT.b[0]])
                    if extra:
                        extra.pop(0)()

        def down(g, extra):
            for m in range(8):
                tok0 = (g * 8 + m) * 128
                x = cnt["xr"] % 3; cnt["xr"] += 1
                S.dma("sync", xres.t[:, x, :], xin[tok0:tok0 + 128, :], writes=[xres.b[x]])
                for n in range(2):
                    q = cnt["po"] % 2; cnt["po"] += 1
                    for j in range(NJ):
                        S.op("pe", lambda e, j=j, q=q, m=m, n=n: e.matmul(
                            out=po.t[:, q, :], lhsT=aT.t[:, j, m * 128:(m + 1) * 128],
                            rhs=wd.t[:, j, n * 512:(n + 1) * 512], start=(j == 0), stop=(j == NJ - 1)),
                             reads=[aT.b[0], wd.b[0]], writes=[po.b[q]])
                    S.op("dve", lambda e, q=q, x=x, n=n: e.scalar_tensor_tensor(
                        out=xres.t[:, x, n * 512:(n + 1) * 512], in0=po.t[:, q, :], scalar=0.5,
                        in1=xres.t[:, x, n * 512:(n + 1) * 512], op0=ALU.mult, op1=ALU.add),
                         reads=[po.b[q], xres.b[x]], writes=[xres.b[x]])
                if final_g is not None:
                    qs = cnt["st"] % 8; cnt["st"] += 1
                    ss = stat.t[:, qs, 0:1]
                    rstd = stat.t[:, qs, 1:2]
                    xr = xres.t[:, x, :]
                    S.op("dve", lambda e, xr=xr, ss=ss: e.scalar_tensor_tensor(
                        out=junk.t[:], in0=xr, scalar=1.0, in1=xr, op0=ALU.mult, op1=ALU.mult, accum_out=ss),
                         reads=[xres.b[x]], writes=[junk.b[0], stat.b[qs]])
                    S.op("pool", lambda e, ss=ss, rstd=rstd: e.tensor_scalar(
                        out=rstd, in0=ss, scalar1=1.0 / D, scalar2=EPS, op0=ALU.mult, op1=ALU.add),
                         reads=[stat.b[qs]], writes=[stat.b[qs]])
                    S.op("pool", lambda e, rstd=rstd: e.tensor_tensor(out=rstd, in0=rstd, in1=pw.t[:], op=ALU.pow),
                         reads=[stat.b[qs], pw.b[0]], writes=[stat.b[qs]])
                    S.op("dve", lambda e, xr=xr, rstd=rstd: e.scalar_tensor_tensor(
                        out=xr, in0=xr, scalar=rstd, in1=gfb.t[:], op0=ALU.mult, op1=ALU.mult),
                         reads=[xres.b[x], stat.b[qs], gfb.b[0]], writes=[xres.b[x]])
                S.dma("pool", xout[tok0:tok0 + 128, :], xres.t[:, x, :], reads=[xres.b[x]])
                if extra:
                    extra.pop(0)()

        for it in prep_items(0):
            it()
        for g in range(NG):
            nxt = prep_items(g + 1) if g + 1 < NG else []
            gateup(g, nxt)
            down(g, nxt)
            while nxt:
                nxt.pop(0)()
        S.emit()


class Rot:
    def __init__(self, n):
        self.n, self.i = n, 0

    def __call__(self):
        k = self.i % self.n
        self.i += 1
        return k


def make_ident(S, tile, dt_is_f32=False):
    S.op("pool", lambda e: e.memset(tile.t[:], 0.0), writes=[tile.b[0]])
    S.op("pool", lambda e: e.affine_select(out=tile.t[:], in_=tile.t[:], pattern=[[-1, 128]],
                                           compare_op=ALU.not_equal, fill=1.0, base=0, channel_multiplier=1),
         reads=[tile.b[0]], writes=[tile.b[0]])


def rstd_ops(S, out, in_, pwt, b_in, b_out, scale=1.0):
    S.op("pool", lambda e: e.tensor_scalar(out=out, in0=in_, scalar1=scale, scalar2=EPS, op0=ALU.mult, op1=ALU.add),
         reads=[b_in], writes=[b_out])
    S.op("pool", lambda e: e.tensor_tensor(out=out, in0=out, in1=pwt, op=ALU.pow), reads=[b_out], writes=[b_out])


GELU_C = 0.7978845608028654


def mixa_phase(C, T, xin, gvec, w_in, sgu_norm, sgu_w, sgu_b, xmT_pad, sog, ysguT, gelu_fn):
    nc = C.nc
    S = C.sched()
    NT = T // 512
    with contextlib.ExitStack() as es:
        es.enter_context(nc.allow_non_contiguous_dma(reason="tiny parameter transposes"))
        ident = sb(es, nc, "ident", [128, 128], BF16)
        identf = sb(es, nc, "identf", [128, 128], F32)
        gb = sb(es, nc, "gb", [128, D], F32)
        pw = sb(es, nc, "pw", [128, 4], F32)
        win = sb(es, nc, "win", [128, 8, 2048], BF16)
        wsf = sb(es, nc, "wsf", [128, 4, 128], F32)
        wsT = sb(es, nc, "wsT", [128, 4, 128], BF16)
        wsb = sb(es, nc, "wsb", [128, 4, 128], BF16)
        gT = sb(es, nc, "gT", [128, 4], F32)
        bsb = sb(es, nc, "bsb", [128, 4, 128], F32)
        zt = sb(es, nc, "zt", [128, 4, 2], F32)
        xprep = sb(es, nc, "xprep", [128, 4, D], F32, 4)
        junk = sb(es, nc, "junk", [128, D], BF16)
        stat = sb(es, nc, "stat", [128, 8, 4], F32, 8)
        xn = sb(es, nc, "xn", [128, 2, D], BF16, 2)
        xnT = sb(es, nc, "xnT", [128, 2, 8, 512], BF16, 2)
        guT = sb(es, nc, "guT", [128, 2, 4, 512], F32, 2)
        xms = sb(es, nc, "xms", [128, 2, 4, 512], F32, 2)
        gv = sb(es, nc, "gv", [128, 4, 512], F32, 4)
        bst = sb(es, nc, "bst", [128, 4, 4, 6], F32, 4)
        mv = sb(es, nc, "mv", [128, 4, 4, 2], F32, 4)
        rs = sb(es, nc, "rs", [128, 4, 4], F32, 4)
        vhn = sb(es, nc, "vhn", [128, 2, 4, 4, 128], BF16, 2)
        sgs = sb(es, nc, "sgs", [128, 2, 512], BF16, 2)
        tmp = sb(es, nc, "tmp", [128, 2, 512], F32, 2)
        ys = sb(es, nc, "ys", [128, 2, 4, 512], BF16, 2)
        gtmp = sb(es, nc, "gtmp", [128, 2, 512], F32, 2)
        P = ps(es, nc, "P", [128, 8, 512], F32, 8)
        rP = Rot(8)

        make_ident(S, ident)
        make_ident(S, identf)
        S.op("pool", lambda e: e.memset(pw.t[:], -0.5), writes=[pw.b[0]])
        S.op("pool", lambda e: e.memset(zt.t[:], 0.0), writes=[zt.b[0]])
        S.dma("sync", gb.t[:], gvec.partition_broadcast(128), writes=[gb.b[0]])
        S.dma("pool", win.t[:, 0:4, :], w_in.rearrange("(kc p) n -> p kc n", p=128)[:, 0:4, :], writes=[win.b[0]])
        S.dma("pool", win.t[:, 4:8, :], w_in.rearrange("(kc p) n -> p kc n", p=128)[:, 4:8, :], writes=[win.b[0]])
        S.dma("sync", wsf.t[:], sgu_w.rearrange("h p q -> p h q"), writes=[wsf.b[0]])
        S.dma("sync", gT.t[:], sgu_norm.rearrange("h d -> d h"), writes=[gT.b[0]])
        S.dma("sync", bsb.t[:].rearrange("p h q -> p (h q)"), sgu_b.rearrange("h q -> (h q)").partition_broadcast(128),
              writes=[bsb.b[0]])
        xmv = xmT_pad.rearrange("(cc p) t -> p cc t", p=128)
        S.dma("sync", xmv[:, :, 0:2], zt.t[:], reads=[zt.b[0]])
        S.dma("sync", xmv[:, :, T + 2:T + 4], zt.t[:], reads=[zt.b[0]])
        S.op("pool", lambda e: e.tensor_copy(out=wsb.t[:], in_=wsf.t[:]), reads=[wsf.b[0]], writes=[wsb.b[0]])
        for hh in range(4):
            k = rP()
            pv = P.t[:, k, :].bitcast(BF16)
            S.op("pe", lambda e, hh=hh, pv=pv: e.transpose(out=pv[:, 0:128], in_=wsb.t[:, hh, :], identity=ident.t[:]),
                 reads=[wsb.b[0], ident.b[0]], writes=[P.b[k]])
            S.op("dve", lambda e, hh=hh, pv=pv: e.tensor_copy(out=wsT.t[:, hh, :], in_=pv[:, 0:128]),
                 reads=[P.b[k]], writes=[wsT.b[0]])

        rxp, rst, rxn = Rot(4), Rot(8), Rot(2)

        def gelu(out, in_ap, in_buf, out_buf, n):
            if gelu_fn is not None:
                S.op("act", lambda e: e.activation(out=out, in_=in_ap, func=gelu_fn), reads=[in_buf], writes=[out_buf])
                return
            k = rgt()
            t = gtmp.t[:, k, 0:n]
            S.op("act", lambda e: e.activation(out=t, in_=in_ap, func=AF.Square), reads=[in_buf], writes=[gtmp.b[k]])
            S.op("pool", lambda e: e.tensor_scalar(out=t, in0=t, scalar1=0.044715, scalar2=1.0, op0=ALU.mult, op1=ALU.add),
                 reads=[gtmp.b[k]], writes=[gtmp.b[k]])
            S.op("dve", lambda e: e.tensor_tensor(out=t, in0=in_ap, in1=t, op=ALU.mult), reads=[in_buf, gtmp.b[k]],
                 writes=[gtmp.b[k]])
            S.op("act", lambda e: e.activation(out=t, in_=t, func=AF.Sigmoid, scale=2.0 * GELU_C),
                 reads=[gtmp.b[k]], writes=[gtmp.b[k]])
            S.op("dve", lambda e: e.tensor_tensor(out=out, in0=in_ap, in1=t, op=ALU.mult), reads=[in_buf, gtmp.b[k]],
                 writes=[out_buf])
        rgt = Rot(2)

        def prep(ti):
            slot = ti % 2
            info = []
            for s in range(4):
                tok0 = ti * 512 + s * 128
                k = rxp()
                xp = xprep.t[:, k, :]
                S.dma("sync", xp, xin[tok0:tok0 + 128, :], writes=[xprep.b[k]])
                q = rst()
                ss, rstd = stat.t[:, q, 0:1], stat.t[:, q, 1:2]
                S.op("dve", lambda e, xp=xp, ss=ss: e.scalar_tensor_tensor(
                    out=junk.t[:], in0=xp, scalar=1.0, in1=xp, op0=ALU.mult, op1=ALU.mult, accum_out=ss),
                     reads=[xprep.b[k]], writes=[junk.b[0], stat.b[q]])
                rstd_ops(S, rstd, ss, pw.t[:, 0:1], stat.b[q], stat.b[q], 1.0 / D)
                info.append((k, xp, q, rstd))
            for s in range(4):
                k, xp, q, rstd = info[s]
                n = rxn()
                S.op("dve", lambda e, xp=xp, rstd=rstd, n=n: e.scalar_tensor_tensor(
                    out=xn.t[:, n, :], in0=xp, scalar=rstd, in1=gb.t[:], op0=ALU.mult, op1=ALU.mult),
                     reads=[xprep.b[k], stat.b[q], gb.b[0]], writes=[xn.b[n]])
                p = rP()
                ptv = P.t[:, p, :].bitcast(BF16)
                for kc in range(8):
                    S.op("pe", lambda e, kc=kc, n=n, ptv=ptv: e.transpose(
                        out=ptv[:, kc * 128:(kc + 1) * 128], in_=xn.t[:, n, kc * 128:(kc + 1) * 128], identity=ident.t[:]),
                         reads=[xn.b[n], ident.b[0]], writes=[P.b[p]])
                S.op("act", lambda e, s=s, ptv=ptv: e.activation(
                    out=xnT.t[:, slot, :, s * 128:(s + 1) * 128], in_=ptv.rearrange("p (k t) -> p k t", k=8), func=AF.Copy),
                     reads=[P.b[p]], writes=[xnT.b[slot]])

        def body(ti):
            slot = ti % 2
            tok0 = ti * 512
            for cc in range(4):
                k = rP()
                for kc in range(8):
                    S.op("pe", lambda e, cc=cc, kc=kc, k=k: e.matmul(
                        out=P.t[:, k, :], lhsT=win.t[:, kc, cc * 128:(cc + 1) * 128], rhs=xnT.t[:, slot, kc, :],
                        start=(kc == 0), stop=(kc == 7)), reads=[win.b[0], xnT.b[slot]], writes=[P.b[k]])
                gelu(guT.t[:, slot, cc, :], P.t[:, k, :], P.b[k], guT.b[slot], 512)
            for m in range(4):
                k = rP()
                for kc in range(8):
                    S.op("pe", lambda e, m=m, kc=kc, k=k: e.matmul(
                        out=P.t[:, k, :], lhsT=xnT.t[:, slot, kc, m * 128:(m + 1) * 128], rhs=win.t[:, kc, 512:1024],
                        start=(kc == 0), stop=(kc == 7)), reads=[win.b[0], xnT.b[slot]], writes=[P.b[k]])
                v = m
                gelu(gv.t[:, v, :], P.t[:, k, :], P.b[k], gv.b[v], 512)
                for hh in range(4):
                    S.op("dve", lambda e, hh=hh, v=v: e.bn_stats(out=bst.t[:, v, hh, :], in_=gv.t[:, v, hh * 128:(hh + 1) * 128]),
                         reads=[gv.b[v]], writes=[bst.b[v]])
                for hh in range(4):
                    S.op("dve", lambda e, hh=hh, v=v: e.bn_aggr(out=mv.t[:, v, hh, :], in_=bst.t[:, v, hh, :]),
                         reads=[bst.b[v]], writes=[mv.b[v]])
                rstd_ops(S, rs.t[:, v, :], mv.t[:, v, :, 1], pw.t[:], mv.b[v], rs.b[v])
            for cc in range(4):
                k = rP()
                for kc in range(8):
                    S.op("pe", lambda e, cc=cc, kc=kc, k=k: e.matmul(
                        out=P.t[:, k, :], lhsT=win.t[:, kc, 1024 + cc * 128:1024 + (cc + 1) * 128],
                        rhs=xnT.t[:, slot, kc, :], start=(kc == 0), stop=(kc == 7)),
                         reads=[win.b[0], xnT.b[slot]], writes=[P.b[k]])
                S.op("dve", lambda e, cc=cc, k=k: e.tensor_copy(out=xms.t[:, slot, cc, :], in_=P.t[:, k, :]),
                     reads=[P.b[k]], writes=[xms.b[slot]])
            S.dma("sync", xmv[:, :, 2 + tok0:2 + tok0 + 512], xms.t[:, slot], reads=[xms.b[slot]])
            for m in range(4):
                v = m
                k = rP()
                for kc in range(8):
                    S.op("pe", lambda e, m=m, kc=kc, k=k: e.matmul(
                        out=P.t[:, k, :], lhsT=xnT.t[:, slot, kc, m * 128:(m + 1) * 128], rhs=win.t[:, kc, 1536:2048],
                        start=(kc == 0), stop=(kc == 7)), reads=[win.b[0], xnT.b[slot]], writes=[P.b[k]])
                S.op("act", lambda e, k=k, v=v: e.activation(out=sgs.t[:, v % 2, :], in_=P.t[:, k, :], func=AF.Sigmoid),
                     reads=[P.b[k]], writes=[sgs.b[v % 2]])
                S.dma("sync", sog[tok0 + m * 128:tok0 + (m + 1) * 128, :], sgs.t[:, v % 2, :], reads=[sgs.b[v % 2]])
            for m in range(4):
                v = m
                for hh in range(4):
                    S.op("dve", lambda e, hh=hh, v=v, m=m: e.tensor_scalar(
                        out=vhn.t[:, slot, m, hh, :], in0=gv.t[:, v, hh * 128:(hh + 1) * 128],
                        scalar1=mv.t[:, v, hh, 0:1], scalar2=rs.t[:, v, hh:hh + 1], op0=ALU.subtract, op1=ALU.mult),
                         reads=[gv.b[v], mv.b[v], rs.b[v]], writes=[vhn.b[slot]])

        def body_sgu(ti):
            slot = ti % 2
            tok0 = ti * 512
            for hh in range(4):
                k = rP()
                for m in range(4):
                    S.op("pe", lambda e, hh=hh, m=m, k=k: e.matmul(
                        out=P.t[:, k, m * 128:(m + 1) * 128], lhsT=vhn.t[:, slot, m, hh, :], rhs=wsT.t[:, hh, :],
                        start=True, stop=True), reads=[vhn.b[slot], wsT.b[0]], writes=[P.b[k]])
                tq = hh % 2
                S.op("dve", lambda e, hh=hh, k=k, tq=tq: e.scalar_tensor_tensor(
                    out=tmp.t[:, tq, :].rearrange("p (m q) -> p m q", m=4), in0=P.t[:, k, :].rearrange("p (m q) -> p m q", m=4),
                    scalar=gT.t[:, hh:hh + 1], in1=bsb.t[:, hh:hh + 1, :].broadcast_to([128, 4, 128]),
                    op0=ALU.mult, op1=ALU.add), reads=[P.b[k], gT.b[0], bsb.b[0]], writes=[tmp.b[tq]])
                S.op("pool", lambda e, hh=hh, tq=tq: e.tensor_tensor(
                    out=ys.t[:, slot, hh, :], in0=tmp.t[:, tq, :], in1=guT.t[:, slot, hh, :], op=ALU.mult),
                     reads=[tmp.b[tq], guT.b[slot]], writes=[ys.b[slot]])
            S.dma("sync", ysguT.rearrange("(h p) t -> p h t", p=128)[:, :, tok0:tok0 + 512], ys.t[:, slot],
                  reads=[ys.b[slot]])

        prep(0)
        for ti in range(NT):
            if ti + 1 < NT:
                prep(ti + 1)
            body(ti)
            if ti >= 1:
                body_sgu(ti - 1)
        body_sgu(NT - 1)
        S.emit()


def mixb_phase(C, T, bwd, xmT_pad, hf, conv_w, conv_b, w_q, w_k, w_v, gate_w, gate_b,
               sog=None, ysguT=None, mh_norm=None, skip=None, w_out=None, xin=None, xout=None, dbg=None, conv_jobs=()):
    nc = C.nc
    S = C.sched()
    NCH = T // 128
    NT = T // 512
    with contextlib.ExitStack() as es:
        es.enter_context(nc.allow_non_contiguous_dma(reason="tiny parameter transposes"))
        ident = sb(es, nc, "ident", [128, 128], BF16)
        identf = sb(es, nc, "identf", [128, 128], F32)
        maskf = sb(es, nc, "maskf", [128, 128], F32)
        onesf = sb(es, nc, "onesf", [128, 128], F32)
        mbd = sb(es, nc, "mbd", [128, 32], F32)
        pw = sb(es, nc, "pw", [128, 4], F32)
        cwT = sb(es, nc, "cwT", [128, 4, 5], F32)
        cbT = sb(es, nc, "cbT", [128, 4], F32)
        cdiag = sb(es, nc, "cdiag", [128, 4, 5, 128], BF16)
        wl = sb(es, nc, "wl", [128, 3, 4, 4], F32)
        bdf = sb(es, nc, "bdf", [128, 3, 4, 128], F32)
        bd = sb(es, nc, "bd", [128, 3, 4, 128], BF16)
        bdT = sb(es, nc, "bdT", [128, 3, 4, 128], BF16)
        gwb = sb(es, nc, "gwb", [128, 12, 8], BF16)
        maskb = sb(es, nc, "maskb", [128, 2, 128], BF16)
        lh = sb(es, nc, "lh", [128, 8, 2, 4], BF16, 8)
        gw = sb(es, nc, "gw", [128, 12, 8], F32)
        Gf = sb(es, nc, "Gf", [128, 2, 4, 8], BF16)
        gbb = sb(es, nc, "gbb", [128, 8], F32)
        xmf = sb(es, nc, "xmf", [128, 2, 4, 516], F32, 2)
        xmb = sb(es, nc, "xmb", [128, 3, 4, 516], BF16, 3)
        xcT = sb(es, nc, "xcT", [128, 4, 4, 512], BF16, 4)
        gsb = sb(es, nc, "gsb", [128, 8, 8], F32, 8)
        lfn = sb(es, nc, "lfn", [128, 8, 8], F32, 8)
        ebt = sb(es, nc, "ebt", [128, 8, 12], F32, 8)
        qs = sb(es, nc, "qs", [128, 4, 4, 128], BF16, 4)
        ks = sb(es, nc, "ks", [128, 6, 4, 128], BF16, 6)
        vext = sb(es, nc, "vext", [128, 6, 4, 130], BF16, 6)
        qkT = sb(es, nc, "qkT", [128, 5, 2, 4, 128], BF16, 5)
        Sm = sb(es, nc, "Sm", [128, 3, 4, 128], BF16, 3)
        Cst = sb(es, nc, "Cst", [128, 4, 129], F32)
        Cbf = sb(es, nc, "Cbf", [128, 4, 130], BF16)
        den = sb(es, nc, "den", [128, 3, 8], F32, 3)
        hd = sb(es, nc, "hd", [128, 4, 4, 128], F32, 4)
        P = ps(es, nc, "P", [128, 8, 512], F32, 8)
        rP = Rot(8)
        if bwd:
            mhg = sb(es, nc, "mhg", [128, 512], F32)
            skT = sb(es, nc, "skT", [128, 4], F32)
            sdiag = sb(es, nc, "sdiag", [128, 4, 128], BF16)
            wout = sb(es, nc, "wout", [128, 8, D], BF16)
            hfl = sb(es, nc, "hfl", [128, 4, 512], F32, 4)
            sgl = sb(es, nc, "sgl", [128, 4, 512], BF16, 4)
            ycT = sb(es, nc, "ycT", [128, 4, 8, 128], BF16, 4)
            xr = sb(es, nc, "xr", [128, 4, D], F32, 4)
            bst = sb(es, nc, "bst", [128, 4, 4, 6], F32, 4)
            mv = sb(es, nc, "mv", [128, 4, 4, 2], F32, 4)
            rs = sb(es, nc, "rs", [128, 4, 4], F32, 4)
            hn = sb(es, nc, "hn", [128, 4, 512], F32, 4)
            ym = sb(es, nc, "ym", [128, 4, 512], BF16, 4)

        conv_items = []
        for (cwg, cwu, cdst) in conv_jobs:
            conv_items.extend(convert_items(S, es, nc, cwg, cwu, cdst))
        conv_per_step = 1 if conv_items else 0
        make_ident(S, ident)
        make_ident(S, identf)
        S.op("pool", lambda e: e.memset(pw.t[:], -0.5), writes=[pw.b[0]])
        S.op("pool", lambda e: e.memset(onesf.t[:], 1.0), writes=[onesf.b[0]])
        S.op("pool", lambda e: e.memset(maskf.t[:], 1.0), writes=[maskf.b[0]])
        S.op("pool", lambda e: e.affine_select(out=maskf.t[:], in_=maskf.t[:], pattern=[[-1 if bwd else 1, 128]],
                                               compare_op=ALU.is_ge, fill=0.0, base=0,
                                               channel_multiplier=1 if bwd else -1),
             reads=[maskf.b[0]], writes=[maskf.b[0]])
        S.op("pool", lambda e: e.memset(mbd.t[:], 1.0), writes=[mbd.b[0]])
        S.op("pool", lambda e: e.affine_select(out=mbd.t[:], in_=mbd.t[:], pattern=[[-4, 32]], compare_op=ALU.is_ge,
                                               fill=0.0, base=0, channel_multiplier=1),
             reads=[mbd.b[0]], writes=[mbd.b[0]])
        S.op("pool", lambda e: e.affine_select(out=mbd.t[:], in_=mbd.t[:], pattern=[[4, 32]], compare_op=ALU.is_ge,
                                               fill=0.0, base=3, channel_multiplier=-1),
             reads=[mbd.b[0]], writes=[mbd.b[0]])
        for cc in range(4):
            S.dma("sync", cwT.t[:, cc, :], conv_w[:, cc * 128:(cc + 1) * 128].rearrange("j p -> p j"), writes=[cwT.b[0]])
        S.dma("sync", cbT.t[:], conv_b.rearrange("(cc p) -> p cc", p=128), writes=[cbT.b[0]])
        for i, w in enumerate((w_q, w_k, w_v)):
            S.dma("sync", wl.t[:, i, :, :], w.rearrange("(hh g) i o -> (g i) hh o", hh=4), writes=[wl.b[0]])
        S.dma("sync", gw.t[:], gate_w.rearrange("(r p) n -> p r n", p=128), writes=[gw.b[0]])
        S.dma("sync", gbb.t[:], gate_b.partition_broadcast(128), writes=[gbb.b[0]])
        S.op("dve", lambda e: e.tensor_scalar(out=wl.t[:, 1], in0=wl.t[:, 1], scalar1=128.0 ** -0.5, scalar2=None, op0=ALU.mult),
             reads=[wl.b[0]], writes=[wl.b[0]])
        for i in range(3):
            for hh in range(4):
                S.op("dve", lambda e, i=i, hh=hh: e.tensor_tensor(
                    out=bdf.t[:, i, hh, :].rearrange("p (g o) -> p g o", o=4),
                    in0=mbd.t[:].unsqueeze(2).broadcast_to([128, 32, 4]),
                    in1=wl.t[:, i, hh:hh + 1, :].broadcast_to([128, 32, 4]), op=ALU.mult),
                     reads=[mbd.b[0], wl.b[0]], writes=[bdf.b[0]])
        S.op("pool", lambda e: e.tensor_copy(out=bd.t[:], in_=bdf.t[:]), reads=[bdf.b[0]], writes=[bd.b[0]])
        S.op("pool", lambda e: e.tensor_copy(out=gwb.t[:], in_=gw.t[:]), reads=[gw.b[0]], writes=[gwb.b[0]])
        S.op("pool", lambda e: e.tensor_copy(out=maskb.t[:, 0, :], in_=maskf.t[:]), reads=[maskf.b[0]], writes=[maskb.b[0]])
        S.op("pool", lambda e: e.memset(maskb.t[:, 1, :], 1.0), writes=[maskb.b[0]])
        for i in range(3):
            for hh in range(4):
                k = rP()
                pv = P.t[:, k, :].bitcast(BF16)
                S.op("pe", lambda e, i=i, hh=hh, pv=pv: e.transpose(out=pv[:, 0:128], in_=bd.t[:, i, hh, :], identity=ident.t[:]),
                     reads=[bd.b[0], ident.b[0]], writes=[P.b[k]])
                S.op("act", lambda e, i=i, hh=hh, pv=pv: e.activation(out=bdT.t[:, i, hh, :], in_=pv[:, 0:128], func=AF.Copy),
                     reads=[P.b[k]], writes=[bdT.b[0]])
        for cc in range(4):
            for j in range(5):
                S.op("dve", lambda e, cc=cc, j=j: e.tensor_scalar(out=cdiag.t[:, cc, j, :], in0=identf.t[:],
                                                                  scalar1=cwT.t[:, cc, j:j + 1], scalar2=None, op0=ALU.mult),
                     reads=[identf.b[0], cwT.b[0]], writes=[cdiag.b[0]])
        for cc in range(4):
            k = rP()
            S.op("pe", lambda e, cc=cc, k=k: e.matmul(out=P.t[:, k, 0:8], lhsT=bdT.t[:, 0, cc, :], rhs=gwb.t[:, cc, :],
                                                      start=True, stop=False), reads=[bdT.b[0], gwb.b[0]], writes=[P.b[k]])
            S.op("pe", lambda e, cc=cc, k=k: e.matmul(out=P.t[:, k, 0:8], lhsT=bdT.t[:, 1, cc, :], rhs=gwb.t[:, 4 + cc, :],
                                                      start=False, stop=True), reads=[bdT.b[0], gwb.b[0]], writes=[P.b[k]])
            S.op("pe", lambda e, cc=cc, k=k: e.matmul(out=P.t[:, k, 8:16], lhsT=bdT.t[:, 2, cc, :], rhs=gwb.t[:, 8 + cc, :],
                                                      start=True, stop=True), reads=[bdT.b[0], gwb.b[0]], writes=[P.b[k]])
            S.op("dve", lambda e, cc=cc, k=k: e.tensor_copy(out=Gf.t[:, :, cc, :], in_=P.t[:, k, 0:16].rearrange("p (a n) -> p a n", a=2)),
                 reads=[P.b[k]], writes=[Gf.b[0]])
        S.op("pool", lambda e: e.memset(Cst.t[:], 0.0), writes=[Cst.b[0]])
        S.op("pool", lambda e: e.memset(Cbf.t[:], 0.0), writes=[Cbf.b[0]])
        for i in range(6):
            S.op("pool", lambda e, i=i: e.memset(vext.t[:, i, :, 128:130], 1.0), writes=[vext.b[i]])
        if bwd:
            S.dma("sync", mhg.t[:], mh_norm.rearrange("h d -> (h d)").partition_broadcast(128), writes=[mhg.b[0]])
            S.dma("sync", skT.t[:], skip.rearrange("(cc p) -> p cc", p=128), writes=[skT.b[0]])
            for cc in range(4):
                S.op("dve", lambda e, cc=cc: e.tensor_scalar(out=sdiag.t[:, cc, :], in0=identf.t[:], scalar1=skT.t[:, cc:cc + 1],
                                                             scalar2=None, op0=ALU.mult),
                     reads=[identf.b[0], skT.b[0]], writes=[sdiag.b[0]])
            wsrc = w_out.rearrange("(kc p) n -> p kc n", p=128)
            S.dma("pool", wout.t[:, 0:4, :], wsrc[:, 0:4, :], writes=[wout.b[0]])
            S.dma("pool", wout.t[:, 4:8, :], wsrc[:, 4:8, :], writes=[wout.b[0]])

        xmv = xmT_pad.rearrange("(cc p) t -> p cc t", p=128)
        tiles_slot = {}
        rA, rxm, rxf = Rot(4), Rot(3), Rot(2)

        def stageA(ti):
            tok0 = ti * 512
            f = rxf()
            a = rxm()
            S.dma("sync", xmf.t[:, f], xmv[:, :, tok0:tok0 + 516], writes=[xmf.b[f]])
            S.op("pool", lambda e: e.tensor_copy(out=xmb.t[:, a], in_=xmf.t[:, f]), reads=[xmf.b[f]], writes=[xmb.b[a]])
            sl = rA()
            for cc in range(4):
                k = rP()
                for j in range(5):
                    S.op("pe", lambda e, cc=cc, j=j, k=k: e.matmul(out=P.t[:, k, :], lhsT=cdiag.t[:, cc, j, :],
                                                                   rhs=xmb.t[:, a, cc, j:j + 512], start=(j == 0), stop=(j == 4)),
                         reads=[cdiag.b[0], xmb.b[a]], writes=[P.b[k]])
                S.op("act", lambda e, cc=cc, k=k: e.activation(out=xcT.t[:, sl, cc, :], in_=P.t[:, k, :], func=AF.Silu,
                                                               bias=cbT.t[:, cc:cc + 1]),
                     reads=[P.b[k], cbT.b[0]], writes=[xcT.b[sl]])
            tiles_slot[ti] = (a, sl)

        rG, rQ, rK, rT, rS, rH, rD = Rot(8), Rot(4), Rot(6), Rot(5), Rot(3), Rot(4), Rot(4)
        sg_, sq_, sk_, st_, ss_, sh_, sd_ = {}, {}, {}, {}, {}, {}, {}

        def S1(c):
            ti, m = divmod(c, 4)
            a, sl = tiles_slot[ti]
            b = rG()
            sg_[c] = b
            kg = rP()
            for i in range(8):
                cc = i % 4
                lhsT = (xcT.t[:, sl, cc, m * 128:(m + 1) * 128] if i < 4 else xmb.t[:, a, cc, 2 + m * 128:2 + (m + 1) * 128])
                S.op("pe", lambda e, i=i, cc=cc, lhsT=lhsT: e.matmul(out=P.t[:, kg, 0:8], lhsT=lhsT, rhs=Gf.t[:, i // 4, cc, :],
                                                                     start=(i == 0), stop=(i == 7)),
                     reads=[xcT.b[sl], xmb.b[a], Gf.b[0]], writes=[P.b[kg]])
            S.op("dve", lambda e: e.tensor_tensor(out=gsb.t[:, b, :], in0=P.t[:, kg, 0:8], in1=gbb.t[:], op=ALU.add),
                 reads=[P.b[kg], gbb.b[0]], writes=[gsb.b[b]])
            S.op("act", lambda e: e.activation(out=lfn.t[:, b, 0:4], in_=gsb.t[:, b, 4:8], func=AF.Exp, scale=-1.0),
                 reads=[gsb.b[b]], writes=[lfn.b[b]])
            S.op("act", lambda e: e.activation(out=lfn.t[:, b, 4:8], in_=lfn.t[:, b, 0:4], func=AF.Ln, bias=1.0),
                 reads=[lfn.b[b]], writes=[lfn.b[b]])

        def S2(c):
            b = sg_[c]
            kg = rP()
            S.op("dve", lambda e: e.tensor_copy(out=lh.t[:, b, 0, :], in_=lfn.t[:, b, 4:8]), reads=[lfn.b[b]], writes=[lh.b[b]])
            S.op("dve", lambda e: e.tensor_tensor(out=lh.t[:, b, 1, :], in0=lfn.t[:, b, 4:8], in1=lh.t[:, b, 0, :], op=ALU.subtract),
                 reads=[lfn.b[b], lh.b[b]], writes=[lh.b[b]])
            for (o0, mi) in ((8, 0), (12, 1)):
                for part in range(2):
                    S.op("pe", lambda e, o0=o0, mi=mi, part=part: e.matmul(
                        out=P.t[:, kg, o0:o0 + 4], lhsT=maskb.t[:, mi, :], rhs=lh.t[:, b, part, :],
                        start=(part == 0), stop=(part == 1)), reads=[maskb.b[0], lh.b[b]], writes=[P.b[kg]])
            S.op("act", lambda e: e.activation(out=ebt.t[:, b, 0:8], in_=P.t[:, kg, 8:16], func=AF.Exp, scale=-1.0),
                 reads=[P.b[kg]], writes=[ebt.b[b]])
            S.op("dve", lambda e: e.tensor_tensor(out=gsb.t[:, b, 4:8], in0=P.t[:, kg, 8:12], in1=gsb.t[:, b, 0:4], op=ALU.add),
                 reads=[P.b[kg], gsb.b[b]], writes=[gsb.b[b]])
            S.op("act", lambda e: e.activation(out=ebt.t[:, b, 8:12], in_=gsb.t[:, b, 4:8], func=AF.Exp),
                 reads=[gsb.b[b]], writes=[ebt.b[b]])

        def S3(c):
            ti, m = divmod(c, 4)
            a, sl = tiles_slot[ti]
            b = sg_[c]
            q_, k_ = rQ(), rK()
            sq_[c], sk_[c] = q_, k_
            kq, kk, kv = rP(), rP(), rP()
            for (kx, wi, src) in ((kq, 0, "xc"), (kk, 1, "xc"), (kv, 2, "xm")):
                for hh in range(4):
                    lhsT = (xcT.t[:, sl, hh, m * 128:(m + 1) * 128] if src == "xc"
                            else xmb.t[:, a, hh, 2 + m * 128:2 + (m + 1) * 128])
                    S.op("pe", lambda e, kx=kx, wi=wi, hh=hh, lhsT=lhsT: e.matmul(
                        out=P.t[:, kx, hh * 128:(hh + 1) * 128], lhsT=lhsT, rhs=bd.t[:, wi, hh, :], start=True, stop=True),
                         reads=[xcT.b[sl], xmb.b[a], bd.b[0]], writes=[P.b[kx]])
            S.op("dve", lambda e: e.tensor_tensor(
                out=qs.t[:, q_], in0=P.t[:, kq, :].rearrange("p (h d) -> p h d", h=4),
                in1=ebt.t[:, b, 0:4].unsqueeze(2).broadcast_to([128, 4, 128]), op=ALU.mult),
                 reads=[P.b[kq], ebt.b[b]], writes=[qs.b[q_]])
            S.op("dve", lambda e: e.tensor_tensor(
                out=ks.t[:, k_], in0=P.t[:, kk, :].rearrange("p (h d) -> p h d", h=4),
                in1=ebt.t[:, b, 8:12].unsqueeze(2).broadcast_to([128, 4, 128]), op=ALU.mult),
                 reads=[P.b[kk], ebt.b[b]], writes=[ks.b[k_]])
            S.op("act", lambda e: e.activation(out=vext.t[:, k_, :, 0:128], in_=P.t[:, kv, :].rearrange("p (h d) -> p h d", h=4),
                                               func=AF.Copy), reads=[P.b[kv]], writes=[vext.b[k_]])

        def S4(c):
            q_, k_ = sq_[c], sk_[c]
            t_ = rT()
            st_[c] = t_
            kt = rP()
            ptv = P.t[:, kt, :].bitcast(BF16)
            for i, (src, sl_) in enumerate(((qs, q_), (ks, k_))):
                for hh in range(4):
                    S.op("pe", lambda e, i=i, hh=hh, src=src, sl_=sl_: e.transpose(
                        out=ptv[:, (i * 4 + hh) * 128:(i * 4 + hh + 1) * 128], in_=src.t[:, sl_, hh, :], identity=ident.t[:]),
                         reads=[src.b[sl_], ident.b[0]], writes=[P.b[kt]])
            S.op("act", lambda e: e.activation(out=qkT.t[:, t_].rearrange("p a h t -> p (a h t)"), in_=ptv, func=AF.Copy),
                 reads=[P.b[kt]], writes=[qkT.b[t_]])

        def S5(c):
            t_ = st_[c]
            k1 = rP()
            for hh in range(4):
                S.op("pe", lambda e, hh=hh: e.matmul(out=P.t[:, k1, hh * 128:(hh + 1) * 128], lhsT=qkT.t[:, t_, 1, hh, :],
                                                     rhs=qkT.t[:, t_, 0, hh, :], start=True, stop=True),
                     reads=[qkT.b[t_]], writes=[P.b[k1]])
            s = rS()
            ss_[c] = s
            S.op("dve", lambda e: e.tensor_tensor(out=Sm.t[:, s], in0=P.t[:, k1, :].rearrange("p (h j) -> p h j", h=4),
                                                  in1=maskf.t[:].unsqueeze(1).broadcast_to([128, 4, 128]), op=ALU.mult),
                 reads=[P.b[k1], maskf.b[0]], writes=[Sm.b[s]])

        def S6(c):
            b, k_, t_, s = sg_[c], sk_[c], st_[c], ss_[c]
            kd = [rP(), rP()]
            kn = [rP(), rP()]
            for hh in range(4):
                hp, h2 = divmod(hh, 2)
                S.op("pe", lambda e, hh=hh, hp=hp, h2=h2: e.matmul(out=P.t[:, kd[hp], h2 * 256:h2 * 256 + 129],
                                                                   lhsT=ks.t[:, k_, hh, :], rhs=vext.t[:, k_, hh, 0:129],
                                                                   start=True, stop=True),
                     reads=[ks.b[k_], vext.b[k_]], writes=[P.b[kd[hp]]])
            for hh in range(4):
                hp, h2 = divmod(hh, 2)
                o = P.t[:, kn[hp], h2 * 256:h2 * 256 + 129]
                S.op("pe", lambda e, hh=hh, o=o: e.matmul(out=o, lhsT=Sm.t[:, s, hh, :], rhs=vext.t[:, k_, hh, 0:129],
                                                          start=True, stop=False),
                     reads=[Sm.b[s], vext.b[k_]], writes=[P.b[kn[hp]]])
                S.op("pe", lambda e, hh=hh, o=o: e.matmul(out=o, lhsT=qkT.t[:, t_, 0, hh, :], rhs=Cbf.t[:, hh, 0:129],
                                                          start=False, stop=True),
                     reads=[qkT.b[t_], Cbf.b[0]], writes=[P.b[kn[hp]]])
            for hp in range(2):
                S.op("dve", lambda e, hp=hp: e.tensor_tensor(
                    out=Cst.t[:, hp * 2:hp * 2 + 2, :], in0=P.t[:, kd[hp], :].rearrange("p (h x) -> p h x", h=2)[:, :, 0:129],
                    in1=Cst.t[:, hp * 2:hp * 2 + 2, :], op=ALU.add), reads=[P.b[kd[hp]], Cst.b[0]], writes=[Cst.b[0]])
            S.op("dve", lambda e: e.tensor_tensor(out=Cbf.t[:, :, 0:129], in0=Cst.t[:],
                                                  in1=ebt.t[:, b, 4:8].unsqueeze(2).broadcast_to([128, 4, 129]), op=ALU.mult),
                 reads=[Cst.b[0], ebt.b[b]], writes=[Cbf.b[0]])
            S.op("dve", lambda e: e.tensor_tensor(out=Cst.t[:], in0=Cst.t[:],
                                                  in1=ebt.t[:, b, 4:8].unsqueeze(2).broadcast_to([128, 4, 129]), op=ALU.mult),
                 reads=[Cst.b[0], ebt.b[b]], writes=[Cst.b[0]])
            dn = s
            for hp in range(2):
                S.op("dve", lambda e, hp=hp: e.tensor_copy(
                    out=den.t[:, dn, hp * 2:hp * 2 + 2].unsqueeze(2),
                    in_=P.t[:, kn[hp], :].rearrange("p (h x) -> p h x", h=2)[:, :, 128:129]),
                     reads=[P.b[kn[hp]]], writes=[den.b[dn]])
            S.op("dve", lambda e: e.scalar_tensor_tensor(out=den.t[:, dn, 4:8], in0=den.t[:, dn, 0:4], scalar=-1.0,
                                                         in1=den.t[:, dn, 0:4], op0=ALU.mult, op1=ALU.max),
                 reads=[den.b[dn]], writes=[den.b[dn]])
            S.op("dve", lambda e: e.tensor_scalar(out=den.t[:, dn, 0:4], in0=den.t[:, dn, 4:8], scalar1=1.0, scalar2=None,
                                                  op0=ALU.max), reads=[den.b[dn]], writes=[den.b[dn]])
            S.op("dve", lambda e: e.reciprocal(out=den.t[:, dn, 4:8], in_=den.t[:, dn, 0:4]), reads=[den.b[dn]], writes=[den.b[dn]])
            h = rH()
            sh_[c] = h
            for hp in range(2):
                S.op("dve", lambda e, hp=hp: e.tensor_tensor(
                    out=hd.t[:, h, hp * 2:hp * 2 + 2, :],
                    in0=P.t[:, kn[hp], :].rearrange("p (h x) -> p h x", h=2)[:, :, 0:128],
                    in1=den.t[:, dn, 4 + hp * 2:6 + hp * 2].unsqueeze(2).broadcast_to([128, 2, 128]), op=ALU.mult),
                     reads=[P.b[kn[hp]], den.b[dn]], writes=[hd.b[h]])
            if not bwd:
                S.dma("sync", hf[c * 128:(c + 1) * 128, :], hd.t[:, h].rearrange("p h d -> p (h d)"), reads=[hd.b[h]])

        def D1(c):
            ti, m = divmod(c, 4)
            a, sl = tiles_slot[ti]
            h = sh_[c]
            d = rD()
            sd_[c] = d
            t0 = c * 128
            S.dma("sync", hfl.t[:, d, :], hf[t0:t0 + 128, :], writes=[hfl.b[d]])
            S.dma("sync", sgl.t[:, d, :], sog[t0:t0 + 128, :], writes=[sgl.b[d]])
            S.dma("sync", ycT.t[:, d, 0:4, :], ysguT.rearrange("(h p) t -> p h t", p=128)[:, :, t0:t0 + 128], writes=[ycT.b[d]])
            S.dma("sync", xr.t[:, d, :], xin[t0:t0 + 128, :], writes=[xr.b[d]])
            S.op("pool", lambda e: e.tensor_tensor(out=hfl.t[:, d, :], in0=hfl.t[:, d, :],
                                                   in1=hd.t[:, h].rearrange("p h d -> p (h d)"), op=ALU.add),
                 reads=[hfl.b[d], hd.b[h]], writes=[hfl.b[d]])
            for hh in range(4):
                S.op("dve", lambda e, hh=hh: e.bn_stats(out=bst.t[:, d, hh, :], in_=hfl.t[:, d, hh * 128:(hh + 1) * 128]),
                     reads=[hfl.b[d]], writes=[bst.b[d]])
            for hh in range(4):
                S.op("dve", lambda e, hh=hh: e.bn_aggr(out=mv.t[:, d, hh, :], in_=bst.t[:, d, hh, :]),
                     reads=[bst.b[d]], writes=[mv.b[d]])
            rstd_ops(S, rs.t[:, d, :], mv.t[:, d, :, 1], pw.t[:], mv.b[d], rs.b[d])

        def D1b(c):
            ti, m = divmod(c, 4)
            a, sl = tiles_slot[ti]
            d = sd_[c]
            for hh in range(4):
                S.op("dve", lambda e, hh=hh: e.tensor_scalar(
                    out=hn.t[:, d, hh * 128:(hh + 1) * 128], in0=hfl.t[:, d, hh * 128:(hh + 1) * 128],
                    scalar1=mv.t[:, d, hh, 0:1], scalar2=rs.t[:, d, hh:hh + 1], op0=ALU.subtract, op1=ALU.mult),
                     reads=[hfl.b[d], mv.b[d], rs.b[d]], writes=[hn.b[d]])
            S.op("dve", lambda e: e.tensor_tensor(out=hn.t[:, d, :], in0=hn.t[:, d, :], in1=mhg.t[:], op=ALU.mult),
                 reads=[hn.b[d], mhg.b[0]], writes=[hn.b[d]])
            kx = rP()
            for hh in range(4):
                S.op("pe", lambda e, hh=hh: e.matmul(out=P.t[:, kx, hh * 128:(hh + 1) * 128],
                                                     lhsT=xcT.t[:, sl, hh, m * 128:(m + 1) * 128], rhs=sdiag.t[:, hh, :],
                                                     start=True, stop=True), reads=[xcT.b[sl], sdiag.b[0]], writes=[P.b[kx]])
            S.op("dve", lambda e: e.tensor_tensor(out=hn.t[:, d, :], in0=P.t[:, kx, :], in1=hn.t[:, d, :], op=ALU.add),
                 reads=[P.b[kx], hn.b[d]], writes=[hn.b[d]])
            S.op("pool", lambda e: e.tensor_tensor(out=ym.t[:, d, :], in0=hn.t[:, d, :], in1=sgl.t[:, d, :], op=ALU.mult),
                 reads=[hn.b[d], sgl.b[d]], writes=[ym.b[d]])

        def D2(c):
            d = sd_[c]
            kt = rP()
            ptv = P.t[:, kt, :].bitcast(BF16)
            for hh in range(4):
                S.op("pe", lambda e, hh=hh: e.transpose(out=ptv[:, hh * 128:(hh + 1) * 128], in_=ym.t[:, d, hh * 128:(hh + 1) * 128],
                                                        identity=ident.t[:]), reads=[ym.b[d], ident.b[0]], writes=[P.b[kt]])
            S.op("act", lambda e: e.activation(out=ycT.t[:, d, 4:8, :].rearrange("p h t -> p (h t)"), in_=ptv[:, 0:512], func=AF.Copy),
                 reads=[P.b[kt]], writes=[ycT.b[d]])

        def D3(c):
            d = sd_[c]
            t0 = c * 128
            for half in range(2):
                ko = rP()
                for cc in range(8):
                    S.op("pe", lambda e, cc=cc, half=half, ko=ko: e.matmul(
                        out=P.t[:, ko, :], lhsT=ycT.t[:, d, cc, :], rhs=wout.t[:, cc, half * 512:(half + 1) * 512],
                        start=(cc == 0), stop=(cc == 7)), reads=[ycT.b[d], wout.b[0]], writes=[P.b[ko]])
                S.op("dve", lambda e, half=half, ko=ko: e.tensor_tensor(
                    out=xr.t[:, d, half * 512:(half + 1) * 512], in0=P.t[:, ko, :], in1=xr.t[:, d, half * 512:(half + 1) * 512],
                    op=ALU.add), reads=[P.b[ko], xr.b[d]], writes=[xr.b[d]])
            S.dma("pool", xout[t0:t0 + 128, :], xr.t[:, d, :], reads=[xr.b[d]])

        order = list(range(NCH))
        if bwd:
            order.reverse()
        done_tiles = set()
        stages = [(S6, 0), (S5, 1), (S4, 2), (S3, 3), (S2, 4), (S1, 5)]
        if bwd:
            stages = [(S6, 0), (D1, -1), (D1b, -2), (D2, -3), (D3, -4), (S5, 1), (S4, 2), (S3, 3), (S2, 4), (S1, 5)]
        for i in range(-5, NCH + 4):
            for la in (5, 7):
                if 0 <= i + la < NCH and order[i + la] // 4 not in done_tiles:
                    stageA(order[i + la] // 4)
                    done_tiles.add(order[i + la] // 4)
            for fn, off in stages:
                if 0 <= i + off < NCH:
                    fn(order[i + off])
            if conv_items and i >= 0:
                for _ in range(conv_per_step):
                    if conv_items:
                        conv_items.pop(0)()
        while conv_items:
            conv_items.pop(0)()
        S.emit()


DEPTH = 2
SEQ = 8192
NCORES = 4

PARAM_SHAPES = {
    'ffn1_norm': (DEPTH, D), 'ffn1_w_gate': (DEPTH, D, HID), 'ffn1_w_up': (DEPTH, D, HID), 'ffn1_w_down': (DEPTH, HID, D),
    'mix_norm': (DEPTH, D), 'w_in': (DEPTH, D, 2048), 'sgu_norm': (DEPTH, 4, 128), 'sgu_w': (DEPTH, 4, 128, 128),
    'sgu_b': (DEPTH, 4, 128), 'conv_w': (DEPTH, 5, 512), 'conv_b': (DEPTH, 512), 'w_q': (DEPTH, 128, 4, 4),
    'w_k': (DEPTH, 128, 4, 4), 'w_v': (DEPTH, 128, 4, 4), 'gate_w_fwd': (DEPTH, 1536, 8), 'gate_b_fwd': (DEPTH, 8),
    'gate_w_bwd': (DEPTH, 1536, 8), 'gate_b_bwd': (DEPTH, 8), 'mh_norm': (DEPTH, 4, 128), 'mlstm_skip': (DEPTH, 512),
    'w_out': (DEPTH, D, D), 'ffn2_norm': (DEPTH, D), 'ffn2_w_gate': (DEPTH, D, HID), 'ffn2_w_up': (DEPTH, D, HID),
    'ffn2_w_down': (DEPTH, HID, D), 'final_norm': (D,),
}


def build_program(T=SEQ, depth=DEPTH):
    nc = bass.Bass("TRN2", target_bir_lowering=False)
    x = nc.dram_tensor("x", [T, D], F32, kind="ExternalInput").ap()
    p = {k: nc.dram_tensor(k, list(s), F32, kind="ExternalInput").ap() for k, s in PARAM_SHAPES.items()}
    y = nc.dram_tensor("y", [T, D], F32, kind="ExternalOutput").ap()
    xres = nc.dram_tensor("xres", [T, D], F32, kind="Internal").ap()
    wguA = nc.dram_tensor("wguA", [NJ, 128, 2, 8, 128], BF16, kind="Internal").ap()
    wguB = nc.dram_tensor("wguB", [NJ, 128, 2, 8, 128], BF16, kind="Internal").ap()
    xmT = nc.dram_tensor("xmT", [512, T + 4], F32, kind="Internal").ap()
    sog = nc.dram_tensor("sog", [T, 512], BF16, kind="Internal").ap()
    ysguT = nc.dram_tensor("ysguT", [512, T], BF16, kind="Internal").ap()
    hf = nc.dram_tensor("hf", [T, 512], F32, kind="Internal").ap()
    C = Ctx(nc)
    for l in range(depth):
        last = (l == depth - 1)
        if l == 0:
            convert_wgu_phase(C, p['ffn1_w_gate'][l], p['ffn1_w_up'][l], wguA)
        ffn_phase(C, T, x if l == 0 else xres, xres, p['ffn1_norm'][l], wguA, p['ffn1_w_down'][l])
        mixa_phase(C, T, xres, p['mix_norm'][l], p['w_in'][l], p['sgu_norm'][l], p['sgu_w'][l], p['sgu_b'][l],
                   xmT, sog, ysguT, AF.Gelu_apprx_tanh)
        mixb_phase(C, T, False, xmT, hf, p['conv_w'][l], p['conv_b'][l], p['w_q'][l], p['w_k'][l], p['w_v'][l],
                   p['gate_w_fwd'][l], p['gate_b_fwd'][l])
        mixb_phase(C, T, True, xmT, hf, p['conv_w'][l], p['conv_b'][l], p['w_q'][l], p['w_k'][l], p['w_v'][l],
                   p['gate_w_bwd'][l], p['gate_b_bwd'][l], sog=sog, ysguT=ysguT, mh_norm=p['mh_norm'][l],
                   skip=p['mlstm_skip'][l], w_out=p['w_out'][l], xin=xres, xout=xres,
                   conv_jobs=[(p['ffn2_w_gate'][l], p['ffn2_w_up'][l], wguB)] +
                   ([] if last else [(p['ffn1_w_gate'][l + 1], p['ffn1_w_up'][l + 1], wguA)]))
        ffn_phase(C, T, xres, y if last else xres, p['ffn2_norm'][l], wguB, p['ffn2_w_down'][l],
                  final_g=p['final_norm'] if last else None)
    return nc


def kernel(**inputs):
    x = np.ascontiguousarray(np.asarray(inputs['x'], dtype=np.float32))
    B = x.shape[0]
    params = {k: np.ascontiguousarray(np.asarray(inputs[k], dtype=np.float32)) for k in PARAM_SHAPES}
    nc = build_program()
    in_maps = [dict(params, x=x[b]) for b in range(B)]
    res = run_bass_kernel_spmd(nc, in_maps, core_ids=list(range(B)))
    return np.stack([np.asarray(res.results[b]["y"], dtype=np.float32) for b in range(B)], axis=0)
```

```python
import contextlib
import numpy as np
import ml_dtypes
import concourse.bass as bass
import concourse.mybir as mybir
from concourse.bass_utils import run_bass_kernel_spmd

F32 = mybir.dt.float32
BF16 = mybir.dt.bfloat16
AF = mybir.ActivationFunctionType
ALU = mybir.AluOpType

D = 1024
HID = 2816
NJ = HID // 128
EPS = 1e-6
ENGS = ("sync", "act", "dve", "pool", "pe")


class Buf:
    __slots__ = ("w", "r")

    def __init__(self):
        self.w = None
        self.r = []


class Op:
    __slots__ = ("eng", "fn", "deps", "ev", "dma")


class DmaPool:
    def __init__(self, nc, eng, n):
        self.slots = [[nc.alloc_semaphore(name=f"dq_{eng}_{i}"), 0, None] for i in range(n)]
        self.i = 0


class Sched:
    def __init__(self, nc, pools, tag):
        self.nc = nc
        self.pools = pools
        self.lists = {e: [] for e in ENGS}
        self.esem = {e: nc.alloc_semaphore(name=f"e_{tag}_{e}") for e in ENGS if e != "sync"}
        self.ecnt = {e: 0 for e in ENGS}

    def _deps(self, reads, writes, eng=None, dma=False, disjoint=False):
        deps = []
        for b in reads:
            if b.w is not None:
                deps.append(b.w)
        for b in writes:
            w = b.w
            if w is not None:
                skip = (eng is not None and not dma and not w.dma and w.eng == eng and b not in reads
                        and (disjoint or len(b.r) > 0))
                if not skip:
                    deps.append(w)
            deps.extend(b.r)
        return deps

    def _commit(self, o, reads, writes):
        for b in reads:
            b.r.append(o)
        for b in writes:
            b.w = o
            b.r = []
        self.lists[o.eng].append(o)

    def op(self, eng, fn, reads=(), writes=(), disjoint=False):
        o = Op()
        o.eng, o.fn, o.dma = eng, fn, False
        o.deps = self._deps(reads, writes, eng, False, disjoint)
        self.ecnt[eng] += 1
        o.ev = (self.esem[eng], self.ecnt[eng])
        self._commit(o, reads, writes)
        return o

    def dma(self, eng, out, in_, reads=(), writes=(), **kw):
        o = Op()
        o.eng, o.dma = eng, True
        o.fn = lambda e: e.dma_start(out=out, in_=in_, **kw)
        o.deps = self._deps(reads, writes)
        pool = self.pools[eng]
        slot = pool.slots[pool.i % len(pool.slots)]
        pool.i += 1
        if slot[2] is not None:
            o.deps.append(slot[2])
        slot[1] += 16
        slot[2] = o
        o.ev = (slot[0], slot[1])
        self._commit(o, reads, writes)
        return o

    def emit(self):
        nc = self.nc
        finals = [(s, c) for e, s in self.esem.items() for c in [self.ecnt[e]] if c > 0]
        for p in self.pools.values():
            for s, c, last in p.slots:
                if c > 0:
                    finals.append((s, c))
        names = {"sync": "sync", "act": "scalar", "dve": "vector", "pool": "gpsimd", "pe": "tensor"}
        with nc.Block() as blk:
            for eng in ENGS:
                def body(e, eng=eng):
                    waited = {}
                    for o in self.lists[eng]:
                        need = {}
                        for d in o.deps:
                            if d.eng == "pe" and eng == "pe" and not d.dma:
                                continue
                            s, v = d.ev
                            k = id(s)
                            if need.get(k, (None, 0))[1] < v:
                                need[k] = (s, v)
                        for k, (s, v) in need.items():
                            if waited.get(k, 0) >= v:
                                continue
                            e.wait_ge(s, v)
                            waited[k] = v
                        ins = o.fn(e)
                        ins.then_inc(o.ev[0], 16 if o.dma else 1)
                    for s, v in finals:
                        if waited.get(id(s), 0) < v:
                            e.wait_ge(s, v)
                getattr(blk, names[eng])(body)


class Ctx:
    def __init__(self, nc):
        self.nc = nc
        self.pools = {"sync": DmaPool(nc, "sync", 24), "pool": DmaPool(nc, "pool", 12), "act": DmaPool(nc, "act", 6)}
        self.nphase = 0

    def sched(self):
        self.nphase += 1
        return Sched(self.nc, self.pools, f"p{self.nphase}")


class Tile:
    def __init__(self, t, nslots=1):
        self.t = t
        self.b = [Buf() for _ in range(nslots)]


_UID = [0]


def _uname(name):
    _UID[0] += 1
    return f"t{_UID[0]}_{name}"


def sb(es, nc, name, shape, dt, nslots=1):
    t = es.enter_context(nc.sbuf_tensor(_uname(name), [128 if shape[0] is None else shape[0]] + list(shape[1:]), dt))
    return Tile(t, nslots)


def ps(es, nc, name, shape, dt, nslots=1):
    t = es.enter_context(nc.psum_tensor(_uname(name), list(shape), dt))
    return Tile(t, nslots)


def convert_wgu_phase(C, wg, wu, wgu):
    nc = C.nc
    S = C.sched()
    with contextlib.ExitStack() as es:
        wst = sb(es, nc, "wst", [128, 2, 2, 8, 256], F32, 2)
        wbf = sb(es, nc, "wbf", [128, 2, 2, 2, 8, 128], BF16, 2)
        engs = ("dve", "pool", "act", "dve")
        for jp in range(NJ // 2):
            sl = jp % 2
            for a, w in enumerate((wg, wu)):
                src = w.rearrange("(kc p) n -> p kc n", p=128)[:, :, jp * 256:(jp + 1) * 256]
                S.dma("sync", wst.t[:, sl, a, :, :], src, writes=[wst.b[sl]])
            i = 0
            for jj in range(2):
                for a in range(2):
                    eng = engs[i]; i += 1
                    o = wbf.t[:, sl, jj, a, :, :]
                    src = wst.t[:, sl, a, :, jj * 128:(jj + 1) * 128]
                    if eng == "act":
                        S.op("act", lambda e, o=o, src=src: e.activation(out=o, in_=src, func=AF.Copy),
                             reads=[wst.b[sl]], writes=[wbf.b[sl]])
                    else:
                        S.op(eng, lambda e, o=o, src=src: e.tensor_copy(out=o, in_=src),
                             reads=[wst.b[sl]], writes=[wbf.b[sl]])
            S.dma("sync", wgu[jp * 2:jp * 2 + 2].rearrange("j p a k c -> p j (a k c)"),
                  wbf.t[:, sl].rearrange("p j a k c -> p j (a k c)"), reads=[wbf.b[sl]])
        S.emit()


def convert_items(S, es, nc, wg, wu, wgu):
    wst = sb(es, nc, "cwst", [128, 1, 2, 8, 128], F32, 1)
    wbf = sb(es, nc, "cwbf", [128, 1, 2, 8, 128], BF16, 1)
    r = Rot(1)
    items = []
    for j in range(NJ):
        def it(j=j):
            sl = r()
            for a, w in enumerate((wg, wu)):
                src = w.rearrange("(kc p) n -> p kc n", p=128)[:, :, j * 128:(j + 1) * 128]
                S.dma("sync", wst.t[:, sl, a, :, :], src, writes=[wst.b[sl]])
            S.op("act", lambda e: e.activation(out=wbf.t[:, sl, 0], in_=wst.t[:, sl, 0], func=AF.Copy),
                 reads=[wst.b[sl]], writes=[wbf.b[sl]])
            S.op("act", lambda e: e.activation(out=wbf.t[:, sl, 1], in_=wst.t[:, sl, 1], func=AF.Copy),
                 reads=[wst.b[sl]], writes=[wbf.b[sl]])
            S.dma("sync", wgu[j].rearrange("p a k c -> p (a k c)"), wbf.t[:, sl].rearrange("p a k c -> p (a k c)"),
                  reads=[wbf.b[sl]])
        items.append(it)
    return items


def ffn_phase(C, T, xin, xout, gvec, wgu, wd_f32, final_g=None):
    nc = C.nc
    S = C.sched()
    NG = T // 1024
    with contextlib.ExitStack() as es:
        ident = sb(es, nc, "ident", [128, 128], BF16)
        gb = sb(es, nc, "gb", [128, D], F32)
        gfb = sb(es, nc, "gfb", [128, D], F32)
        pw = sb(es, nc, "pw", [128, 1], F32)
        wd = sb(es, nc, "wd", [128, NJ, D], BF16)
        wgus = sb(es, nc, "wgus", [128, 3, 2, 2, 8, 128], BF16, 3)
        xprep = sb(es, nc, "xprep", [128, 4, D], F32, 4)
        xres = sb(es, nc, "xres", [128, 3, D], F32, 3)
        xn = sb(es, nc, "xn", [128, 2, D], BF16, 2)
        stat = sb(es, nc, "stat", [128, 8, 4], F32, 8)
        xnT = sb(es, nc, "xnT", [128, 2, 8, 1024], BF16, 2)
        aT = sb(es, nc, "aT", [128, NJ, 1024], BF16, 1)
        sg = sb(es, nc, "sg", [128, 2, 512], BF16, 2)
        junk = sb(es, nc, "junk", [128, D], BF16, 1)
        pg = ps(es, nc, "pg", [128, 2, 512], F32, 2)
        pu = ps(es, nc, "pu", [128, 2, 512], F32, 2)
        pt = ps(es, nc, "pt", [128, 2, 1024], BF16, 2)
        po = ps(es, nc, "po", [128, 2, 512], F32, 2)

        S.op("pool", lambda e: e.memset(ident.t[:], 0.0), writes=[ident.b[0]])
        S.op("pool", lambda e: e.affine_select(out=ident.t[:], in_=ident.t[:], pattern=[[-1, 128]],
                                               compare_op=ALU.not_equal, fill=1.0, base=0, channel_multiplier=1),
             reads=[ident.b[0]], writes=[ident.b[0]])
        S.op("pool", lambda e: e.memset(pw.t[:], -0.5), writes=[pw.b[0]])
        S.dma("sync", gb.t[:], gvec.partition_broadcast(128), writes=[gb.b[0]])
        if final_g is not None:
            S.dma("sync", gfb.t[:], final_g.partition_broadcast(128), writes=[gfb.b[0]])
        wdsrc = wd_f32.rearrange("(j p) n -> p j n", p=128)
        for j0 in range(0, NJ, 2):
            S.dma("pool", wd.t[:, j0:j0 + 2, :], wdsrc[:, j0:j0 + 2, :], writes=[wd.b[0]])

        cnt = {"xp": 0, "xn": 0, "st": 0, "pt": 0, "wg": 0, "g": 0, "sg": 0, "po": 0, "xr": 0}

        def prep_items(g):
            fronts, backs = [], []
            slot = g % 2
            for s in range(8):
                def it(s=s):
                    i = cnt["xp"]; cnt["xp"] += 1
                    k = i % 4
                    tok0 = (g * 8 + s) * 128
                    xp = xprep.t[:, k, :]
                    S.dma("sync", xp, xin[tok0:tok0 + 128, :], writes=[xprep.b[k]])
                    q = cnt["st"] % 8; cnt["st"] += 1
                    ss = stat.t[:, q, 0:1]
                    rstd = stat.t[:, q, 1:2]
                    S.op("dve", lambda e: e.scalar_tensor_tensor(out=junk.t[:], in0=xp, scalar=1.0, in1=xp,
                                                                 op0=ALU.mult, op1=ALU.mult, accum_out=ss),
                         reads=[xprep.b[k]], writes=[junk.b[0], stat.b[q]])
                    S.op("pool", lambda e: e.tensor_scalar(out=rstd, in0=ss, scalar1=1.0 / D, scalar2=EPS,
                                                           op0=ALU.mult, op1=ALU.add),
                         reads=[stat.b[q]], writes=[stat.b[q]])
                    S.op("pool", lambda e: e.tensor_tensor(out=rstd, in0=rstd, in1=pw.t[:], op=ALU.pow),
                         reads=[stat.b[q], pw.b[0]], writes=[stat.b[q]])
                    n = cnt["xn"] % 2; cnt["xn"] += 1
                    S.op("dve", lambda e: e.scalar_tensor_tensor(out=xn.t[:, n, :], in0=xp, scalar=rstd, in1=gb.t[:],
                                                                 op0=ALU.mult, op1=ALU.mult),
                         reads=[xprep.b[k], stat.b[q], gb.b[0]], writes=[xn.b[n]])
                    return n

                def bk(s=s, n=None):
                    p = cnt["pt"] % 2; cnt["pt"] += 1
                    for kc in range(8):
                        S.op("pe", lambda e, kc=kc: e.transpose(out=pt.t[:, p, kc * 128:(kc + 1) * 128],
                                                                in_=xn.t[:, n, kc * 128:(kc + 1) * 128],
                                                                identity=ident.t[:]),
                             reads=[xn.b[n], ident.b[0]], writes=[pt.b[p]])
                    S.op("act", lambda e: e.activation(
                        out=xnT.t[:, slot, :, s * 128:(s + 1) * 128],
                        in_=pt.t[:, p, :].rearrange("p (k t) -> p k t", k=8), func=AF.Copy),
                         reads=[pt.b[p]], writes=[xnT.b[slot]], disjoint=True)
                fronts.append(it)
                backs.append(bk)
            nsl = {}

            def mk_f(i):
                def f():
                    nsl[i] = fronts[i]()
                return f

            def mk_b(i):
                def f():
                    backs[i](n=nsl[i])
                return f
            order = [mk_f(0)]
            for i in range(1, 8):
                order.append(mk_f(i))
                order.append(mk_b(i - 1))
            order.append(mk_b(7))
            return order

        def gateup(g, extra):
            slot = g % 2
            for j in range(NJ):
                if j % 2 == 0:
                    w = cnt["wg"] % 3; cnt["wg"] += 1
                    S.dma("sync", wgus.t[:, w].rearrange("p j a k c -> p j (a k c)"),
                          wgu[j:j + 2].rearrange("j p a k c -> p j (a k c)"), writes=[wgus.b[w]])
                    wcur = w
                jj = j % 2
                for half in range(2):
                    q = cnt["g"] % 2; cnt["g"] += 1
                    for (pp, a) in ((pg, 0), (pu, 1)):
                        for kc in range(8):
                            S.op("pe", lambda e, pp=pp, a=a, kc=kc, q=q, half=half, wcur=wcur, jj=jj: e.matmul(
                                out=pp.t[:, q, :], lhsT=wgus.t[:, wcur, jj, a, kc, :],
                                rhs=xnT.t[:, slot, kc, half * 512:(half + 1) * 512],
                                start=(kc == 0), stop=(kc == 7)),
                                 reads=[wgus.b[wcur], xnT.b[slot]], writes=[pp.b[q]])
                    r = cnt["sg"] % 2; cnt["sg"] += 1
                    S.op("act", lambda e, q=q, r=r: e.activation(out=sg.t[:, r, :], in_=pg.t[:, q, :], func=AF.Silu),
                         reads=[pg.b[q]], writes=[sg.b[r]])
                    S.op("dve", lambda e, q=q, r=r, j=j, half=half: e.tensor_tensor(
                        out=aT.t[:, j, half * 512:(half + 1) * 512], in0=pu.t[:, q, :], in1=sg.t[:, r, :], op=ALU.mult),
                         reads=[pu.b[q], sg.b[r]], writes=[aT.b[0]], disjoint=True)
                    if extra:
                        extra.pop(0)()

        def down(g, extra):
            for m in range(8):
                tok0 = (g * 8 + m) * 128
                x = cnt["xr"] % 3; cnt["xr"] += 1
                S.dma("sync", xres.t[:, x, :], xin[tok0:tok0 + 128, :], writes=[xres.b[x]])
                for n in range(2):
                    q = cnt["po"] % 2; cnt["po"] += 1
                    for j in range(NJ):
                        S.op("pe", lambda e, j=j, q=q, m=m, n=n: e.matmul(
                            out=po.t[:, q, :], lhsT=aT.t[:, j, m * 128:(m + 1) * 128],
                            rhs=wd.t[:, j, n * 512:(n + 1) * 512], start=(j == 0), stop=(j == NJ - 1)),
                             reads=[aT.b[0], wd.b[0]], writes=[po.b[q]])
                    S.op("dve", lambda e, q=q, x=x, n=n: e.scalar_tensor_tensor(
                        out=xres.t[:, x, n * 512:(n + 1) * 512], in0=po.t[:, q, :], scalar=0.5,
                        in1=xres.t[:, x, n * 512:(n + 1) * 512], op0=ALU.mult, op1=ALU.add),
                         reads=[po.b[q], xres.b[x]], writes=[xres.b[x]])
                if final_g is not None:
                    qs = cnt["st"] % 8; cnt["st"] += 1
                    ss = stat.t[:, qs, 0:1]
                    rstd = stat.t[:, qs, 1:2]
                    xr = xres.t[:, x, :]
                    S.op("dve", lambda e, xr=xr, ss=ss: e.scalar_tensor_tensor(
                        out=junk.t[:], in0=xr, scalar=1.0, in1=xr, op0=ALU.mult, op1=ALU.mult, accum_out=ss),
                         reads=[xres.b[x]], writes=[junk.b[0], stat.b[qs]])
                    S.op("pool", lambda e, ss=ss, rstd=rstd: e.tensor_scalar(
                        out=rstd, in0=ss, scalar1=1.0 / D, scalar2=EPS, op0=ALU.mult, op1=ALU.add),
                         reads=[stat.b[qs]], writes=[stat.b[qs]])
                    S.op("pool", lambda e, rstd=rstd: e.tensor_tensor(out=rstd, in0=rstd, in1=pw.t[:], op=ALU.pow),
                         reads=[stat.b[qs], pw.b[0]], writes=[stat.b[qs]])
                    S.op("dve", lambda e, xr=xr, rstd=rstd: e.scalar_tensor_tensor(
                        out=xr, in0=xr, scalar=rstd, in1=gfb.t[:], op0=ALU.mult, op1=ALU.mult),
                         reads=[xres.b[x], stat.b[qs], gfb.b[0]], writes=[xres.b[x]])
                S.dma("pool", xout[tok0:tok0 + 128, :], xres.t[:, x, :], reads=[xres.b[x]])
                if extra:
                    extra.pop(0)()

        for it in prep_items(0):
            it()
        for g in range(NG):
            nxt = prep_items(g + 1) if g + 1 < NG else []
            gateup(g, nxt)
            down(g, nxt)
            while nxt:
                nxt.pop(0)()
        S.emit()


class Rot:
    def __init__(self, n):
        self.n, self.i = n, 0

    def __call__(self):
        k = self.i % self.n
        self.i += 1
        return k


def make_ident(S, tile, dt_is_f32=False):
    S.op("pool", lambda e: e.memset(tile.t[:], 0.0), writes=[tile.b[0]])
    S.op("pool", lambda e: e.affine_select(out=tile.t[:], in_=tile.t[:], pattern=[[-1, 128]],
                                           compare_op=ALU.not_equal, fill=1.0, base=0, channel_multiplier=1),
         reads=[tile.b[0]], writes=[tile.b[0]])


def rstd_ops(S, out, in_, pwt, b_in, b_out, scale=1.0):
    S.op("pool", lambda e: e.tensor_scalar(out=out, in0=in_, scalar1=scale, scalar2=EPS, op0=ALU.mult, op1=ALU.add),
         reads=[b_in], writes=[b_out])
    S.op("pool", lambda e: e.tensor_tensor(out=out, in0=out, in1=pwt, op=ALU.pow), reads=[b_out], writes=[b_out])


GELU_C = 0.7978845608028654


def mixa_phase(C, T, xin, gvec, w_in, sgu_norm, sgu_w, sgu_b, xmT_pad, sog, ysguT, gelu_fn):
    nc = C.nc
    S = C.sched()
    NT = T // 512
    with contextlib.ExitStack() as es:
        es.enter_context(nc.allow_non_contiguous_dma(reason="tiny parameter transposes"))
        ident = sb(es, nc, "ident", [128, 128], BF16)
        identf = sb(es, nc, "identf", [128, 128], F32)
        gb = sb(es, nc, "gb", [128, D], F32)
        pw = sb(es, nc, "pw", [128, 4], F32)
        win = sb(es, nc, "win", [128, 8, 2048], BF16)
        wsf = sb(es, nc, "wsf", [128, 4, 128], F32)
        wsT = sb(es, nc, "wsT", [128, 4, 128], BF16)
        wsb = sb(es, nc, "wsb", [128, 4, 128], BF16)
        gT = sb(es, nc, "gT", [128, 4], F32)
        bsb = sb(es, nc, "bsb", [128, 4, 128], F32)
        zt = sb(es, nc, "zt", [128, 4, 2], F32)
        xprep = sb(es, nc, "xprep", [128, 4, D], F32, 4)
        junk = sb(es, nc, "junk", [128, D], BF16)
        stat = sb(es, nc, "stat", [128, 8, 4], F32, 8)
        xn = sb(es, nc, "xn", [128, 2, D], BF16, 2)
        xnT = sb(es, nc, "xnT", [128, 2, 8, 512], BF16, 2)
        guT = sb(es, nc, "guT", [128, 2, 4, 512], F32, 2)
        xms = sb(es, nc, "xms", [128, 2, 4, 512], F32, 2)
        gv = sb(es, nc, "gv", [128, 4, 512], F32, 4)
        bst = sb(es, nc, "bst", [128, 4, 4, 6], F32, 4)
        mv = sb(es, nc, "mv", [128, 4, 4, 2], F32, 4)
        rs = sb(es, nc, "rs", [128, 4, 4], F32, 4)
        vhn = sb(es, nc, "vhn", [128, 2, 4, 4, 128], BF16, 2)
        sgs = sb(es, nc, "sgs", [128, 2, 512], BF16, 2)
        tmp = sb(es, nc, "tmp", [128, 2, 512], F32, 2)
        ys = sb(es, nc, "ys", [128, 2, 4, 512], BF16, 2)
        gtmp = sb(es, nc, "gtmp", [128, 2, 512], F32, 2)
        P = ps(es, nc, "P", [128, 8, 512], F32, 8)
        rP = Rot(8)

        make_ident(S, ident)
        make_ident(S, identf)
        S.op("pool", lambda e: e.memset(pw.t[:], -0.5), writes=[pw.b[0]])
        S.op("pool", lambda e: e.memset(zt.t[:], 0.0), writes=[zt.b[0]])
        S.dma("sync", gb.t[:], gvec.partition_broadcast(128), writes=[gb.b[0]])
        S.dma("pool", win.t[:, 0:4, :], w_in.rearrange("(kc p) n -> p kc n", p=128)[:, 0:4, :], writes=[win.b[0]])
        S.dma("pool", win.t[:, 4:8, :], w_in.rearrange("(kc p) n -> p kc n", p=128)[:, 4:8, :], writes=[win.b[0]])
        S.dma("sync", wsf.t[:], sgu_w.rearrange("h p q -> p h q"), writes=[wsf.b[0]])
        S.dma("sync", gT.t[:], sgu_norm.rearrange("h d -> d h"), writes=[gT.b[0]])
        S.dma("sync", bsb.t[:].rearrange("p h q -> p (h q)"), sgu_b.rearrange("h q -> (h q)").partition_broadcast(128),
              writes=[bsb.b[0]])
        xmv = xmT_pad.rearrange("(cc p) t -> p cc t", p=128)
        S.dma("sync", xmv[:, :, 0:2], zt.t[:], reads=[zt.b[0]])
        S.dma("sync", xmv[:, :, T + 2:T + 4], zt.t[:], reads=[zt.b[0]])
        S.op("pool", lambda e: e.tensor_copy(out=wsb.t[:], in_=wsf.t[:]), reads=[wsf.b[0]], writes=[wsb.b[0]])
        for hh in range(4):
            k = rP()
            pv = P.t[:, k, :].bitcast(BF16)
            S.op("pe", lambda e, hh=hh, pv=pv: e.transpose(out=pv[:, 0:128], in_=wsb.t[:, hh, :], identity=ident.t[:]),
                 reads=[wsb.b[0], ident.b[0]], writes=[P.b[k]])
            S.op("dve", lambda e, hh=hh, pv=pv: e.tensor_copy(out=wsT.t[:, hh, :], in_=pv[:, 0:128]),
                 reads=[P.b[k]], writes=[wsT.b[0]], disjoint=True)

        rxp, rst, rxn = Rot(4), Rot(8), Rot(2)

        def gelu(out, in_ap, in_buf, out_buf, n):
            if gelu_fn is not None:
                S.op("act", lambda e: e.activation(out=out, in_=in_ap, func=gelu_fn), reads=[in_buf], writes=[out_buf])
                return
            k = rgt()
            t = gtmp.t[:, k, 0:n]
            S.op("act", lambda e: e.activation(out=t, in_=in_ap, func=AF.Square), reads=[in_buf], writes=[gtmp.b[k]])
            S.op("pool", lambda e: e.tensor_scalar(out=t, in0=t, scalar1=0.044715, scalar2=1.0, op0=ALU.mult, op1=ALU.add),
                 reads=[gtmp.b[k]], writes=[gtmp.b[k]])
            S.op("dve", lambda e: e.tensor_tensor(out=t, in0=in_ap, in1=t, op=ALU.mult), reads=[in_buf, gtmp.b[k]],
                 writes=[gtmp.b[k]])
            S.op("act", lambda e: e.activation(out=t, in_=t, func=AF.Sigmoid, scale=2.0 * GELU_C),
                 reads=[gtmp.b[k]], writes=[gtmp.b[k]])
            S.op("dve", lambda e: e.tensor_tensor(out=out, in0=in_ap, in1=t, op=ALU.mult), reads=[in_buf, gtmp.b[k]],
                 writes=[out_buf])
        rgt = Rot(2)

        def prep(ti):
            slot = ti % 2
            info = []
            for s in range(4):
                tok0 = ti * 512 + s * 128
                k = rxp()
                xp = xprep.t[:, k, :]
                S.dma("sync", xp, xin[tok0:tok0 + 128, :], writes=[xprep.b[k]])
                q = rst()
                ss, rstd = stat.t[:, q, 0:1], stat.t[:, q, 1:2]
                S.op("dve", lambda e, xp=xp, ss=ss: e.scalar_tensor_tensor(
                    out=junk.t[:], in0=xp, scalar=1.0, in1=xp, op0=ALU.mult, op1=ALU.mult, accum_out=ss),
                     reads=[xprep.b[k]], writes=[junk.b[0], stat.b[q]])
                rstd_ops(S, rstd, ss, pw.t[:, 0:1], stat.b[q], stat.b[q], 1.0 / D)
                info.append((k, xp, q, rstd))
            for s in range(4):
                k, xp, q, rstd = info[s]
                n = rxn()
                S.op("dve", lambda e, xp=xp, rstd=rstd, n=n: e.scalar_tensor_tensor(
                    out=xn.t[:, n, :], in0=xp, scalar=rstd, in1=gb.t[:], op0=ALU.mult, op1=ALU.mult),
                     reads=[xprep.b[k], stat.b[q], gb.b[0]], writes=[xn.b[n]])
                p = rP()
                ptv = P.t[:, p, :].bitcast(BF16)
                for kc in range(8):
                    S.op("pe", lambda e, kc=kc, n=n, ptv=ptv: e.transpose(
                        out=ptv[:, kc * 128:(kc + 1) * 128], in_=xn.t[:, n, kc * 128:(kc + 1) * 128], identity=ident.t[:]),
                         reads=[xn.b[n], ident.b[0]], writes=[P.b[p]])
                S.op("act", lambda e, s=s, ptv=ptv: e.activation(
                    out=xnT.t[:, slot, :, s * 128:(s + 1) * 128], in_=ptv.rearrange("p (k t) -> p k t", k=8), func=AF.Copy),
                     reads=[P.b[p]], writes=[xnT.b[slot]], disjoint=True)

        def body(ti):
            slot = ti % 2
            tok0 = ti * 512
            for cc in range(4):
                k = rP()
                for kc in range(8):
                    S.op("pe", lambda e, cc=cc, kc=kc, k=k: e.matmul(
                        out=P.t[:, k, :], lhsT=win.t[:, kc, cc * 128:(cc + 1) * 128], rhs=xnT.t[:, slot, kc, :],
                        start=(kc == 0), stop=(kc == 7)), reads=[win.b[0], xnT.b[slot]], writes=[P.b[k]])
                gelu(guT.t[:, slot, cc, :], P.t[:, k, :], P.b[k], guT.b[slot], 512)
            for m in range(4):
                k = rP()
                for kc in range(8):
                    S.op("pe", lambda e, m=m, kc=kc, k=k: e.matmul(
                        out=P.t[:, k, :], lhsT=xnT.t[:, slot, kc, m * 128:(m + 1) * 128], rhs=win.t[:, kc, 512:1024],
                        start=(kc == 0), stop=(kc == 7)), reads=[win.b[0], xnT.b[slot]], writes=[P.b[k]])
                v = m
                gelu(gv.t[:, v, :], P.t[:, k, :], P.b[k], gv.b[v], 512)
                for hh in range(4):
                    S.op("dve", lambda e, hh=hh, v=v: e.bn_stats(out=bst.t[:, v, hh, :], in_=gv.t[:, v, hh * 128:(hh + 1) * 128]),
                         reads=[gv.b[v]], writes=[bst.b[v]], disjoint=True)
                for hh in range(4):
                    S.op("dve", lambda e, hh=hh, v=v: e.bn_aggr(out=mv.t[:, v, hh, :], in_=bst.t[:, v, hh, :]),
                         reads=[bst.b[v]], writes=[mv.b[v]], disjoint=True)
                rstd_ops(S, rs.t[:, v, :], mv.t[:, v, :, 1], pw.t[:], mv.b[v], rs.b[v])
            for cc in range(4):
                k = rP()
                for kc in range(8):
                    S.op("pe", lambda e, cc=cc, kc=kc, k=k: e.matmul(
                        out=P.t[:, k, :], lhsT=win.t[:, kc, 1024 + cc * 128:1024 + (cc + 1) * 128],
                        rhs=xnT.t[:, slot, kc, :], start=(kc == 0), stop=(kc == 7)),
                         reads=[win.b[0], xnT.b[slot]], writes=[P.b[k]])
                S.op("dve", lambda e, cc=cc, k=k: e.tensor_copy(out=xms.t[:, slot, cc, :], in_=P.t[:, k, :]),
                     reads=[P.b[k]], writes=[xms.b[slot]], disjoint=True)
            S.dma("sync", xmv[:, :, 2 + tok0:2 + tok0 + 512], xms.t[:, slot], reads=[xms.b[slot]])
            for m in range(4):
                v = m
                k = rP()
                for kc in range(8):
                    S.op("pe", lambda e, m=m, kc=kc, k=k: e.matmul(
                        out=P.t[:, k, :], lhsT=xnT.t[:, slot, kc, m * 128:(m + 1) * 128], rhs=win.t[:, kc, 1536:2048],
                        start=(kc == 0), stop=(kc == 7)), reads=[win.b[0], xnT.b[slot]], writes=[P.b[k]])
                S.op("act", lambda e, k=k, v=v: e.activation(out=sgs.t[:, v % 2, :], in_=P.t[:, k, :], func=AF.Sigmoid),
                     reads=[P.b[k]], writes=[sgs.b[v % 2]])
                S.dma("sync", sog[tok0 + m * 128:tok0 + (m + 1) * 128, :], sgs.t[:, v % 2, :], reads=[sgs.b[v % 2]])
            for m in range(4):
                v = m
                for hh in range(4):
                    S.op("dve", lambda e, hh=hh, v=v, m=m: e.tensor_scalar(
                        out=vhn.t[:, slot, m, hh, :], in0=gv.t[:, v, hh * 128:(hh + 1) * 128],
                        scalar1=mv.t[:, v, hh, 0:1], scalar2=rs.t[:, v, hh:hh + 1], op0=ALU.subtract, op1=ALU.mult),
                         reads=[gv.b[v], mv.b[v], rs.b[v]], writes=[vhn.b[slot]], disjoint=True)

        def body_sgu(ti):
            slot = ti % 2
            tok0 = ti * 512
            for hh in range(4):
                k = rP()
                for m in range(4):
                    S.op("pe", lambda e, hh=hh, m=m, k=k: e.matmul(
                        out=P.t[:, k, m * 128:(m + 1) * 128], lhsT=vhn.t[:, slot, m, hh, :], rhs=wsT.t[:, hh, :],
                        start=True, stop=True), reads=[vhn.b[slot], wsT.b[0]], writes=[P.b[k]])
                tq = hh % 2
                S.op("dve", lambda e, hh=hh, k=k, tq=tq: e.scalar_tensor_tensor(
                    out=tmp.t[:, tq, :].rearrange("p (m q) -> p m q", m=4), in0=P.t[:, k, :].rearrange("p (m q) -> p m q", m=4),
                    scalar=gT.t[:, hh:hh + 1], in1=bsb.t[:, hh:hh + 1, :].broadcast_to([128, 4, 128]),
                    op0=ALU.mult, op1=ALU.add), reads=[P.b[k], gT.b[0], bsb.b[0]], writes=[tmp.b[tq]])
                S.op("pool", lambda e, hh=hh, tq=tq: e.tensor_tensor(
                    out=ys.t[:, slot, hh, :], in0=tmp.t[:, tq, :], in1=guT.t[:, slot, hh, :], op=ALU.mult),
                     reads=[tmp.b[tq], guT.b[slot]], writes=[ys.b[slot]], disjoint=True)
            S.dma("sync", ysguT.rearrange("(h p) t -> p h t", p=128)[:, :, tok0:tok0 + 512], ys.t[:, slot],
                  reads=[ys.b[slot]])

        prep(0)
        for ti in range(NT):
            if ti + 1 < NT:
                prep(ti + 1)
            body(ti)
            if ti >= 1:
                body_sgu(ti - 1)
        body_sgu(NT - 1)
        S.emit()


def mixb_phase(C, T, bwd, xmT_pad, hf, conv_w, conv_b, w_q, w_k, w_v, gate_w, gate_b,
               sog=None, ysguT=None, mh_norm=None, skip=None, w_out=None, xin=None, xout=None, dbg=None, conv_jobs=()):
    nc = C.nc
    S = C.sched()
    NCH = T // 128
    NT = T // 512
    with contextlib.ExitStack() as es:
        es.enter_context(nc.allow_non_contiguous_dma(reason="tiny parameter transposes"))
        ident = sb(es, nc, "ident", [128, 128], BF16)
        identf = sb(es, nc, "identf", [128, 128], F32)
        maskf = sb(es, nc, "maskf", [128, 128], F32)
        onesf = sb(es, nc, "onesf", [128, 128], F32)
        mbd = sb(es, nc, "mbd", [128, 32], F32)
        pw = sb(es, nc, "pw", [128, 4], F32)
        cwT = sb(es, nc, "cwT", [128, 4, 5], F32)
        cbT = sb(es, nc, "cbT", [128, 4], F32)
        cdiag = sb(es, nc, "cdiag", [128, 4, 5, 128], BF16)
        wl = sb(es, nc, "wl", [128, 3, 4, 4], F32)
        bdf = sb(es, nc, "bdf", [128, 3, 4, 128], F32)
        bd = sb(es, nc, "bd", [128, 3, 4, 128], BF16)
        bdT = sb(es, nc, "bdT", [128, 3, 4, 128], BF16)
        gwb = sb(es, nc, "gwb", [128, 12, 8], BF16)
        maskb = sb(es, nc, "maskb", [128, 2, 128], BF16)
        lh = sb(es, nc, "lh", [128, 8, 2, 4], BF16, 8)
        gw = sb(es, nc, "gw", [128, 12, 8], F32)
        Gf = sb(es, nc, "Gf", [128, 2, 4, 8], BF16)
        gbb = sb(es, nc, "gbb", [128, 8], F32)
        xmf = sb(es, nc, "xmf", [128, 2, 4, 516], F32, 2)
        xmb = sb(es, nc, "xmb", [128, 3, 4, 516], BF16, 3)
        xcT = sb(es, nc, "xcT", [128, 4, 4, 512], BF16, 4)
        gsb = sb(es, nc, "gsb", [128, 8, 8], F32, 8)
        lfn = sb(es, nc, "lfn", [128, 8, 8], F32, 8)
        ebt = sb(es, nc, "ebt", [128, 8, 12], F32, 8)
        qs = sb(es, nc, "qs", [128, 4, 4, 128], BF16, 4)
        ks = sb(es, nc, "ks", [128, 6, 4, 128], BF16, 6)
        vext = sb(es, nc, "vext", [128, 6, 4, 130], BF16, 6)
        qkT = sb(es, nc, "qkT", [128, 5, 2, 4, 128], BF16, 5)
        Sm = sb(es, nc, "Sm", [128, 3, 4, 128], BF16, 3)
        Cst = sb(es, nc, "Cst", [128, 4, 129], F32)
        Cbf = sb(es, nc, "Cbf", [128, 4, 130], BF16)
        den = sb(es, nc, "den", [128, 3, 8], F32, 3)
        hd = sb(es, nc, "hd", [128, 4, 4, 128], F32, 4)
        P = ps(es, nc, "P", [128, 8, 512], F32, 8)
        rP = Rot(8)
        if bwd:
            mhg = sb(es, nc, "mhg", [128, 512], F32)
            skT = sb(es, nc, "skT", [128, 4], F32)
            sdiag = sb(es, nc, "sdiag", [128, 4, 128], BF16)
            wout = sb(es, nc, "wout", [128, 8, D], BF16)
            hfl = sb(es, nc, "hfl", [128, 4, 512], F32, 4)
            sgl = sb(es, nc, "sgl", [128, 4, 512], BF16, 4)
            ycT = sb(es, nc, "ycT", [128, 4, 8, 128], BF16, 4)
            xr = sb(es, nc, "xr", [128, 4, D], F32, 4)
            bst = sb(es, nc, "bst", [128, 4, 4, 6], F32, 4)
            mv = sb(es, nc, "mv", [128, 4, 4, 2], F32, 4)
            rs = sb(es, nc, "rs", [128, 4, 4], F32, 4)
            hn = sb(es, nc, "hn", [128, 4, 512], F32, 4)
            ym = sb(es, nc, "ym", [128, 4, 512], BF16, 4)

        conv_items = []
        for (cwg, cwu, cdst) in conv_jobs:
            conv_items.extend(convert_items(S, es, nc, cwg, cwu, cdst))
        conv_per_step = 1 if conv_items else 0
        make_ident(S, ident)
        make_ident(S, identf)
        S.op("pool", lambda e: e.memset(pw.t[:], -0.5), writes=[pw.b[0]])
        S.op("pool", lambda e: e.memset(onesf.t[:], 1.0), writes=[onesf.b[0]])
        S.op("pool", lambda e: e.memset(maskf.t[:], 1.0), writes=[maskf.b[0]])
        S.op("pool", lambda e: e.affine_select(out=maskf.t[:], in_=maskf.t[:], pattern=[[-1 if bwd else 1, 128]],
                                               compare_op=ALU.is_ge, fill=0.0, base=0,
                                               channel_multiplier=1 if bwd else -1),
             reads=[maskf.b[0]], writes=[maskf.b[0]])
        S.op("pool", lambda e: e.memset(mbd.t[:], 1.0), writes=[mbd.b[0]])
        S.op("pool", lambda e: e.affine_select(out=mbd.t[:], in_=mbd.t[:], pattern=[[-4, 32]], compare_op=ALU.is_ge,
                                               fill=0.0, base=0, channel_multiplier=1),
             reads=[mbd.b[0]], writes=[mbd.b[0]])
        S.op("pool", lambda e: e.affine_select(out=mbd.t[:], in_=mbd.t[:], pattern=[[4, 32]], compare_op=ALU.is_ge,
                                               fill=0.0, base=3, channel_multiplier=-1),
             reads=[mbd.b[0]], writes=[mbd.b[0]])
        for cc in range(4):
            S.dma("sync", cwT.t[:, cc, :], conv_w[:, cc * 128:(cc + 1) * 128].rearrange("j p -> p j"), writes=[cwT.b[0]])
        S.dma("sync", cbT.t[:], conv_b.rearrange("(cc p) -> p cc", p=128), writes=[cbT.b[0]])
        for i, w in enumerate((w_q, w_k, w_v)):
            S.dma("sync", wl.t[:, i, :, :], w.rearrange("(hh g) i o -> (g i) hh o", hh=4), writes=[wl.b[0]])
        S.dma("sync", gw.t[:], gate_w.rearrange("(r p) n -> p r n", p=128), writes=[gw.b[0]])
        S.dma("sync", gbb.t[:], gate_b.partition_broadcast(128), writes=[gbb.b[0]])
        S.op("dve", lambda e: e.tensor_scalar(out=wl.t[:, 1], in0=wl.t[:, 1], scalar1=128.0 ** -0.5, scalar2=None, op0=ALU.mult),
             reads=[wl.b[0]], writes=[wl.b[0]])
        for i in range(3):
            for hh in range(4):
                S.op("dve", lambda e, i=i, hh=hh: e.tensor_tensor(
                    out=bdf.t[:, i, hh, :].rearrange("p (g o) -> p g o", o=4),
                    in0=mbd.t[:].unsqueeze(2).broadcast_to([128, 32, 4]),
                    in1=wl.t[:, i, hh:hh + 1, :].broadcast_to([128, 32, 4]), op=ALU.mult),
                     reads=[mbd.b[0], wl.b[0]], writes=[bdf.b[0]], disjoint=True)
        S.op("pool", lambda e: e.tensor_copy(out=bd.t[:], in_=bdf.t[:]), reads=[bdf.b[0]], writes=[bd.b[0]])
        S.op("pool", lambda e: e.tensor_copy(out=gwb.t[:], in_=gw.t[:]), reads=[gw.b[0]], writes=[gwb.b[0]])
        S.op("pool", lambda e: e.tensor_copy(out=maskb.t[:, 0, :], in_=maskf.t[:]), reads=[maskf.b[0]], writes=[maskb.b[0]])
        S.op("pool", lambda e: e.memset(maskb.t[:, 1, :], 1.0), writes=[maskb.b[0]])
        for i in range(3):
            for hh in range(4):
                k = rP()
                pv = P.t[:, k, :].bitcast(BF16)
                S.op("pe", lambda e, i=i, hh=hh, pv=pv: e.transpose(out=pv[:, 0:128], in_=bd.t[:, i, hh, :], identity=ident.t[:]),
                     reads=[bd.b[0], ident.b[0]], writes=[P.b[k]])
                S.op("act", lambda e, i=i, hh=hh, pv=pv: e.activation(out=bdT.t[:, i, hh, :], in_=pv[:, 0:128], func=AF.Copy),
                     reads=[P.b[k]], writes=[bdT.b[0]], disjoint=True)
        for cc in range(4):
            for j in range(5):
                S.op("dve", lambda e, cc=cc, j=j: e.tensor_scalar(out=cdiag.t[:, cc, j, :], in0=identf.t[:],
                                                                  scalar1=cwT.t[:, cc, j:j + 1], scalar2=None, op0=ALU.mult),
                     reads=[identf.b[0], cwT.b[0]], writes=[cdiag.b[0]], disjoint=True)
        for cc in range(4):
            k = rP()
            S.op("pe", lambda e, cc=cc, k=k: e.matmul(out=P.t[:, k, 0:8], lhsT=bdT.t[:, 0, cc, :], rhs=gwb.t[:, cc, :],
                                                      start=True, stop=False), reads=[bdT.b[0], gwb.b[0]], writes=[P.b[k]])
            S.op("pe", lambda e, cc=cc, k=k: e.matmul(out=P.t[:, k, 0:8], lhsT=bdT.t[:, 1, cc, :], rhs=gwb.t[:, 4 + cc, :],
                                                      start=False, stop=True), reads=[bdT.b[0], gwb.b[0]], writes=[P.b[k]])
            S.op("pe", lambda e, cc=cc, k=k: e.matmul(out=P.t[:, k, 8:16], lhsT=bdT.t[:, 2, cc, :], rhs=gwb.t[:, 8 + cc, :],
                                                      start=True, stop=True), reads=[bdT.b[0], gwb.b[0]], writes=[P.b[k]])
            S.op("dve", lambda e, cc=cc, k=k: e.tensor_copy(out=Gf.t[:, :, cc, :], in_=P.t[:, k, 0:16].rearrange("p (a n) -> p a n", a=2)),
                 reads=[P.b[k]], writes=[Gf.b[0]], disjoint=True)
        S.op("pool", lambda e: e.memset(Cst.t[:], 0.0), writes=[Cst.b[0]])
        S.op("pool", lambda e: e.memset(Cbf.t[:], 0.0), writes=[Cbf.b[0]])
        for i in range(6):
            S.op("pool", lambda e, i=i: e.memset(vext.t[:, i, :, 128:130], 1.0), writes=[vext.b[i]])
        if bwd:
            S.dma("sync", mhg.t[:], mh_norm.rearrange("h d -> (h d)").partition_broadcast(128), writes=[mhg.b[0]])
            S.dma("sync", skT.t[:], skip.rearrange("(cc p) -> p cc", p=128), writes=[skT.b[0]])
            for cc in range(4):
                S.op("dve", lambda e, cc=cc: e.tensor_scalar(out=sdiag.t[:, cc, :], in0=identf.t[:], scalar1=skT.t[:, cc:cc + 1],
                                                             scalar2=None, op0=ALU.mult),
                     reads=[identf.b[0], skT.b[0]], writes=[sdiag.b[0]], disjoint=True)
            wsrc = w_out.rearrange("(kc p) n -> p kc n", p=128)
            S.dma("pool", wout.t[:, 0:4, :], wsrc[:, 0:4, :], writes=[wout.b[0]])
            S.dma("pool", wout.t[:, 4:8, :], wsrc[:, 4:8, :], writes=[wout.b[0]])

        xmv = xmT_pad.rearrange("(cc p) t -> p cc t", p=128)
        tiles_slot = {}
        rA, rxm, rxf = Rot(4), Rot(3), Rot(2)

        def stageA(ti):
            tok0 = ti * 512
            f = rxf()
            a = rxm()
            S.dma("sync", xmf.t[:, f], xmv[:, :, tok0:tok0 + 516], writes=[xmf.b[f]])
            S.op("pool", lambda e: e.tensor_copy(out=xmb.t[:, a], in_=xmf.t[:, f]), reads=[xmf.b[f]], writes=[xmb.b[a]])
            sl = rA()
            for cc in range(4):
                k = rP()
                for j in range(5):
                    S.op("pe", lambda e, cc=cc, j=j, k=k: e.matmul(out=P.t[:, k, :], lhsT=cdiag.t[:, cc, j, :],
                                                                   rhs=xmb.t[:, a, cc, j:j + 512], start=(j == 0), stop=(j == 4)),
                         reads=[cdiag.b[0], xmb.b[a]], writes=[P.b[k]])
                S.op("act", lambda e, cc=cc, k=k: e.activation(out=xcT.t[:, sl, cc, :], in_=P.t[:, k, :], func=AF.Silu,
                                                               bias=cbT.t[:, cc:cc + 1]),
                     reads=[P.b[k], cbT.b[0]], writes=[xcT.b[sl]], disjoint=True)
            tiles_slot[ti] = (a, sl)

        rG, rQ, rK, rT, rS, rH, rD = Rot(8), Rot(4), Rot(6), Rot(5), Rot(3), Rot(4), Rot(4)
        sg_, sq_, sk_, st_, ss_, sh_, sd_ = {}, {}, {}, {}, {}, {}, {}

        def S1(c):
            ti, m = divmod(c, 4)
            a, sl = tiles_slot[ti]
            b = rG()
            sg_[c] = b
            kg = rP()
            for i in range(8):
                cc = i % 4
                lhsT = (xcT.t[:, sl, cc, m * 128:(m + 1) * 128] if i < 4 else xmb.t[:, a, cc, 2 + m * 128:2 + (m + 1) * 128])
                S.op("pe", lambda e, i=i, cc=cc, lhsT=lhsT: e.matmul(out=P.t[:, kg, 0:8], lhsT=lhsT, rhs=Gf.t[:, i // 4, cc, :],
                                                                     start=(i == 0), stop=(i == 7)),
                     reads=[xcT.b[sl], xmb.b[a], Gf.b[0]], writes=[P.b[kg]])
            S.op("dve", lambda e: e.tensor_tensor(out=gsb.t[:, b, :], in0=P.t[:, kg, 0:8], in1=gbb.t[:], op=ALU.add),
                 reads=[P.b[kg], gbb.b[0]], writes=[gsb.b[b]])
            S.op("act", lambda e: e.activation(out=lfn.t[:, b, 0:4], in_=gsb.t[:, b, 4:8], func=AF.Exp, scale=-1.0),
                 reads=[gsb.b[b]], writes=[lfn.b[b]])
            S.op("act", lambda e: e.activation(out=lfn.t[:, b, 4:8], in_=lfn.t[:, b, 0:4], func=AF.Ln, bias=1.0),
                 reads=[lfn.b[b]], writes=[lfn.b[b]])

        def S2(c):
            b = sg_[c]
            kg = rP()
            S.op("dve", lambda e: e.tensor_copy(out=lh.t[:, b, 0, :], in_=lfn.t[:, b, 4:8]), reads=[lfn.b[b]], writes=[lh.b[b]])
            S.op("dve", lambda e: e.tensor_tensor(out=lh.t[:, b, 1, :], in0=lfn.t[:, b, 4:8], in1=lh.t[:, b, 0, :], op=ALU.subtract),
                 reads=[lfn.b[b], lh.b[b]], writes=[lh.b[b]])
            for (o0, mi) in ((8, 0), (12, 1)):
                for part in range(2):
                    S.op("pe", lambda e, o0=o0, mi=mi, part=part: e.matmul(
                        out=P.t[:, kg, o0:o0 + 4], lhsT=maskb.t[:, mi, :], rhs=lh.t[:, b, part, :],
                        start=(part == 0), stop=(part == 1)), reads=[maskb.b[0], lh.b[b]], writes=[P.b[kg]])
            S.op("act", lambda e: e.activation(out=ebt.t[:, b, 0:8], in_=P.t[:, kg, 8:16], func=AF.Exp, scale=-1.0),
                 reads=[P.b[kg]], writes=[ebt.b[b]])
            S.op("dve", lambda e: e.tensor_tensor(out=gsb.t[:, b, 4:8], in0=P.t[:, kg, 8:12], in1=gsb.t[:, b, 0:4], op=ALU.add),
                 reads=[P.b[kg], gsb.b[b]], writes=[gsb.b[b]])
            S.op("act", lambda e: e.activation(out=ebt.t[:, b, 8:12], in_=gsb.t[:, b, 4:8], func=AF.Exp),
                 reads=[gsb.b[b]], writes=[ebt.b[b]])

        def S3(c):
            ti, m = divmod(c, 4)
            a, sl = tiles_slot[ti]
            b = sg_[c]
            q_, k_ = rQ(), rK()
            sq_[c], sk_[c] = q_, k_
            kq, kk, kv = rP(), rP(), rP()
            for (kx, wi, src) in ((kq, 0, "xc"), (kk, 1, "xc"), (kv, 2, "xm")):
                for hh in range(4):
                    lhsT = (xcT.t[:, sl, hh, m * 128:(m + 1) * 128] if src == "xc"
                            else xmb.t[:, a, hh, 2 + m * 128:2 + (m + 1) * 128])
                    S.op("pe", lambda e, kx=kx, wi=wi, hh=hh, lhsT=lhsT: e.matmul(
                        out=P.t[:, kx, hh * 128:(hh + 1) * 128], lhsT=lhsT, rhs=bd.t[:, wi, hh, :], start=True, stop=True),
                         reads=[xcT.b[sl], xmb.b[a], bd.b[0]], writes=[P.b[kx]])
            S.op("dve", lambda e: e.tensor_tensor(
                out=qs.t[:, q_], in0=P.t[:, kq, :].rearrange("p (h d) -> p h d", h=4),
                in1=ebt.t[:, b, 0:4].unsqueeze(2).broadcast_to([128, 4, 128]), op=ALU.mult),
                 reads=[P.b[kq], ebt.b[b]], writes=[qs.b[q_]])
            S.op("dve", lambda e: e.tensor_tensor(
                out=ks.t[:, k_], in0=P.t[:, kk, :].rearrange("p (h d) -> p h d", h=4),
                in1=ebt.t[:, b, 8:12].unsqueeze(2).broadcast_to([128, 4, 128]), op=ALU.mult),
                 reads=[P.b[kk], ebt.b[b]], writes=[ks.b[k_]])
            S.op("act", lambda e: e.activation(out=vext.t[:, k_, :, 0:128], in_=P.t[:, kv, :].rearrange("p (h d) -> p h d", h=4),
                                               func=AF.Copy), reads=[P.b[kv]], writes=[vext.b[k_]])

        def S4(c):
            q_, k_ = sq_[c], sk_[c]
            t_ = rT()
            st_[c] = t_
            kt = rP()
            ptv = P.t[:, kt, :].bitcast(BF16)
            for i, (src, sl_) in enumerate(((qs, q_), (ks, k_))):
                for hh in range(4):
                    S.op("pe", lambda e, i=i, hh=hh, src=src, sl_=sl_: e.transpose(
                        out=ptv[:, (i * 4 + hh) * 128:(i * 4 + hh + 1) * 128], in_=src.t[:, sl_, hh, :], identity=ident.t[:]),
                         reads=[src.b[sl_], ident.b[0]], writes=[P.b[kt]])
            S.op("act", lambda e: e.activation(out=qkT.t[:, t_].rearrange("p a h t -> p (a h t)"), in_=ptv, func=AF.Copy),
                 reads=[P.b[kt]], writes=[qkT.b[t_]])

        def S5(c):
            t_ = st_[c]
            k1 = rP()
            for hh in range(4):
                S.op("pe", lambda e, hh=hh: e.matmul(out=P.t[:, k1, hh * 128:(hh + 1) * 128], lhsT=qkT.t[:, t_, 1, hh, :],
                                                     rhs=qkT.t[:, t_, 0, hh, :], start=True, stop=True),
                     reads=[qkT.b[t_]], writes=[P.b[k1]])
            s = rS()
            ss_[c] = s
            S.op("dve", lambda e: e.tensor_tensor(out=Sm.t[:, s], in0=P.t[:, k1, :].rearrange("p (h j) -> p h j", h=4),
                                                  in1=maskf.t[:].unsqueeze(1).broadcast_to([128, 4, 128]), op=ALU.mult),
                 reads=[P.b[k1], maskf.b[0]], writes=[Sm.b[s]])

        def S6(c):
            b, k_, t_, s = sg_[c], sk_[c], st_[c], ss_[c]
            kd = [rP(), rP()]
            kn = [rP(), rP()]
            for hh in range(4):
                hp, h2 = divmod(hh, 2)
                S.op("pe", lambda e, hh=hh, hp=hp, h2=h2: e.matmul(out=P.t[:, kd[hp], h2 * 256:h2 * 256 + 129],
                                                                   lhsT=ks.t[:, k_, hh, :], rhs=vext.t[:, k_, hh, 0:129],
                                                                   start=True, stop=True),
                     reads=[ks.b[k_], vext.b[k_]], writes=[P.b[kd[hp]]])
            for hh in range(4):
                hp, h2 = divmod(hh, 2)
                o = P.t[:, kn[hp], h2 * 256:h2 * 256 + 129]
                S.op("pe", lambda e, hh=hh, o=o: e.matmul(out=o, lhsT=Sm.t[:, s, hh, :], rhs=vext.t[:, k_, hh, 0:129],
                                                          start=True, stop=False),
                     reads=[Sm.b[s], vext.b[k_]], writes=[P.b[kn[hp]]])
                S.op("pe", lambda e, hh=hh, o=o: e.matmul(out=o, lhsT=qkT.t[:, t_, 0, hh, :], rhs=Cbf.t[:, hh, 0:129],
                                                          start=False, stop=True),
                     reads=[qkT.b[t_], Cbf.b[0]], writes=[P.b[kn[hp]]])
            for hp in range(2):
                S.op("dve", lambda e, hp=hp: e.tensor_tensor(
                    out=Cst.t[:, hp * 2:hp * 2 + 2, :], in0=P.t[:, kd[hp], :].rearrange("p (h x) -> p h x", h=2)[:, :, 0:129],
                    in1=Cst.t[:, hp * 2:hp * 2 + 2, :], op=ALU.add), reads=[P.b[kd[hp]], Cst.b[0]], writes=[Cst.b[0]])
            S.op("dve", lambda e: e.tensor_tensor(out=Cbf.t[:, :, 0:129], in0=Cst.t[:],
                                                  in1=ebt.t[:, b, 4:8].unsqueeze(2).broadcast_to([128, 4, 129]), op=ALU.mult),
                 reads=[Cst.b[0], ebt.b[b]], writes=[Cbf.b[0]])
            S.op("dve", lambda e: e.tensor_tensor(out=Cst.t[:], in0=Cst.t[:],
                                                  in1=ebt.t[:, b, 4:8].unsqueeze(2).broadcast_to([128, 4, 129]), op=ALU.mult),
                 reads=[Cst.b[0], ebt.b[b]], writes=[Cst.b[0]])
            dn = s
            for hp in range(2):
                S.op("dve", lambda e, hp=hp: e.tensor_copy(
                    out=den.t[:, dn, hp * 2:hp * 2 + 2].unsqueeze(2),
                    in_=P.t[:, kn[hp], :].rearrange("p (h x) -> p h x", h=2)[:, :, 128:129]),
                     reads=[P.b[kn[hp]]], writes=[den.b[dn]], disjoint=True)
            S.op("dve", lambda e: e.scalar_tensor_tensor(out=den.t[:, dn, 4:8], in0=den.t[:, dn, 0:4], scalar=-1.0,
                                                         in1=den.t[:, dn, 0:4], op0=ALU.mult, op1=ALU.max),
                 reads=[den.b[dn]], writes=[den.b[dn]], disjoint=True)
            S.op("dve", lambda e: e.tensor_scalar(out=den.t[:, dn, 0:4], in0=den.t[:, dn, 4:8], scalar1=1.0, scalar2=None,
                                                  op0=ALU.max), reads=[den.b[dn]], writes=[den.b[dn]], disjoint=True)
            S.op("dve", lambda e: e.reciprocal(out=den.t[:, dn, 4:8], in_=den.t[:, dn, 0:4]), reads=[den.b[dn]], writes=[den.b[dn]], disjoint=True)
            h = rH()
            sh_[c] = h
            for hp in range(2):
                S.op("dve", lambda e, hp=hp: e.tensor_tensor(
                    out=hd.t[:, h, hp * 2:hp * 2 + 2, :],
                    in0=P.t[:, kn[hp], :].rearrange("p (h x) -> p h x", h=2)[:, :, 0:128],
                    in1=den.t[:, dn, 4 + hp * 2:6 + hp * 2].unsqueeze(2).broadcast_to([128, 2, 128]), op=ALU.mult),
                     reads=[P.b[kn[hp]], den.b[dn]], writes=[hd.b[h]], disjoint=True)
            if not bwd:
                S.dma("sync", hf[c * 128:(c + 1) * 128, :], hd.t[:, h].rearrange("p h d -> p (h d)"), reads=[hd.b[h]])

        def D1(c):
            ti, m = divmod(c, 4)
            a, sl = tiles_slot[ti]
            h = sh_[c]
            d = rD()
            sd_[c] = d
            t0 = c * 128
            S.dma("sync", hfl.t[:, d, :], hf[t0:t0 + 128, :], writes=[hfl.b[d]])
            S.dma("sync", sgl.t[:, d, :], sog[t0:t0 + 128, :], writes=[sgl.b[d]])
            S.dma("sync", ycT.t[:, d, 0:4, :], ysguT.rearrange("(h p) t -> p h t", p=128)[:, :, t0:t0 + 128], writes=[ycT.b[d]])
            S.dma("sync", xr.t[:, d, :], xin[t0:t0 + 128, :], writes=[xr.b[d]])
            S.op("pool", lambda e: e.tensor_tensor(out=hfl.t[:, d, :], in0=hfl.t[:, d, :],
                                                   in1=hd.t[:, h].rearrange("p h d -> p (h d)"), op=ALU.add),
                 reads=[hfl.b[d], hd.b[h]], writes=[hfl.b[d]])
            for hh in range(4):
                S.op("dve", lambda e, hh=hh: e.bn_stats(out=bst.t[:, d, hh, :], in_=hfl.t[:, d, hh * 128:(hh + 1) * 128]),
                     reads=[hfl.b[d]], writes=[bst.b[d]], disjoint=True)
            for hh in range(4):
                S.op("dve", lambda e, hh=hh: e.bn_aggr(out=mv.t[:, d, hh, :], in_=bst.t[:, d, hh, :]),
                     reads=[bst.b[d]], writes=[mv.b[d]], disjoint=True)
            rstd_ops(S, rs.t[:, d, :], mv.t[:, d, :, 1], pw.t[:], mv.b[d], rs.b[d])

        def D1b(c):
            ti, m = divmod(c, 4)
            a, sl = tiles_slot[ti]
            d = sd_[c]
            for hh in range(4):
                S.op("dve", lambda e, hh=hh: e.tensor_scalar(
                    out=hn.t[:, d, hh * 128:(hh + 1) * 128], in0=hfl.t[:, d, hh * 128:(hh + 1) * 128],
                    scalar1=mv.t[:, d, hh, 0:1], scalar2=rs.t[:, d, hh:hh + 1], op0=ALU.subtract, op1=ALU.mult),
                     reads=[hfl.b[d], mv.b[d], rs.b[d]], writes=[hn.b[d]], disjoint=True)
            S.op("dve", lambda e: e.tensor_tensor(out=hn.t[:, d, :], in0=hn.t[:, d, :], in1=mhg.t[:], op=ALU.mult),
                 reads=[hn.b[d], mhg.b[0]], writes=[hn.b[d]], disjoint=True)
            kx = rP()
            for hh in range(4):
                S.op("pe", lambda e, hh=hh: e.matmul(out=P.t[:, kx, hh * 128:(hh + 1) * 128],
                                                     lhsT=xcT.t[:, sl, hh, m * 128:(m + 1) * 128], rhs=sdiag.t[:, hh, :],
                                                     start=True, stop=True), reads=[xcT.b[sl], sdiag.b[0]], writes=[P.b[kx]])
            S.op("dve", lambda e: e.tensor_tensor(out=hn.t[:, d, :], in0=P.t[:, kx, :], in1=hn.t[:, d, :], op=ALU.add),
                 reads=[P.b[kx], hn.b[d]], writes=[hn.b[d]], disjoint=True)
            S.op("pool", lambda e: e.tensor_tensor(out=ym.t[:, d, :], in0=hn.t[:, d, :], in1=sgl.t[:, d, :], op=ALU.mult),
                 reads=[hn.b[d], sgl.b[d]], writes=[ym.b[d]])

        def D2(c):
            d = sd_[c]
            kt = rP()
            ptv = P.t[:, kt, :].bitcast(BF16)
            for hh in range(4):
                S.op("pe", lambda e, hh=hh: e.transpose(out=ptv[:, hh * 128:(hh + 1) * 128], in_=ym.t[:, d, hh * 128:(hh + 1) * 128],
                                                        identity=ident.t[:]), reads=[ym.b[d], ident.b[0]], writes=[P.b[kt]])
            S.op("act", lambda e: e.activation(out=ycT.t[:, d, 4:8, :].rearrange("p h t -> p (h t)"), in_=ptv[:, 0:512], func=AF.Copy),
                 reads=[P.b[kt]], writes=[ycT.b[d]])

        def D3(c):
            d = sd_[c]
            t0 = c * 128
            for half in range(2):
                ko = rP()
                for cc in range(8):
                    S.op("pe", lambda e, cc=cc, half=half, ko=ko: e.matmul(
                        out=P.t[:, ko, :], lhsT=ycT.t[:, d, cc, :], rhs=wout.t[:, cc, half * 512:(half + 1) * 512],
                        start=(cc == 0), stop=(cc == 7)), reads=[ycT.b[d], wout.b[0]], writes=[P.b[ko]])
                S.op("dve", lambda e, half=half, ko=ko: e.tensor_tensor(
                    out=xr.t[:, d, half * 512:(half + 1) * 512], in0=P.t[:, ko, :], in1=xr.t[:, d, half * 512:(half + 1) * 512],
                    op=ALU.add), reads=[P.b[ko], xr.b[d]], writes=[xr.b[d]])
            S.dma("pool", xout[t0:t0 + 128, :], xr.t[:, d, :], reads=[xr.b[d]])

        order = list(range(NCH))
        if bwd:
            order.reverse()
        done_tiles = set()
        stages = [(S6, 0), (S5, 1), (S4, 2), (S3, 3), (S2, 4), (S1, 5)]
        if bwd:
            stages = [(S6, 0), (D1, -1), (D1b, -2), (D2, -3), (D3, -4), (S5, 1), (S4, 2), (S3, 3), (S2, 4), (S1, 5)]
        for i in range(-5, NCH + 4):
            for la in (5, 7):
                if 0 <= i + la < NCH and order[i + la] // 4 not in done_tiles:
                    stageA(order[i + la] // 4)
                    done_tiles.add(order[i + la] // 4)
            for fn, off in stages:
                if 0 <= i + off < NCH:
                    fn(order[i + off])
            if conv_items and i >= 0:
                for _ in range(conv_per_step):
                    if conv_items:
                        conv_items.pop(0)()
        while conv_items:
            conv_items.pop(0)()
        S.emit()


DEPTH = 2
SEQ = 8192
NCORES = 4

PARAM_SHAPES = {
    'ffn1_norm': (DEPTH, D), 'ffn1_w_gate': (DEPTH, D, HID), 'ffn1_w_up': (DEPTH, D, HID), 'ffn1_w_down': (DEPTH, HID, D),
    'mix_norm': (DEPTH, D), 'w_in': (DEPTH, D, 2048), 'sgu_norm': (DEPTH, 4, 128), 'sgu_w': (DEPTH, 4, 128, 128),
    'sgu_b': (DEPTH, 4, 128), 'conv_w': (DEPTH, 5, 512), 'conv_b': (DEPTH, 512), 'w_q': (DEPTH, 128, 4, 4),
    'w_k': (DEPTH, 128, 4, 4), 'w_v': (DEPTH, 128, 4, 4), 'gate_w_fwd': (DEPTH, 1536, 8), 'gate_b_fwd': (DEPTH, 8),
    'gate_w_bwd': (DEPTH, 1536, 8), 'gate_b_bwd': (DEPTH, 8), 'mh_norm': (DEPTH, 4, 128), 'mlstm_skip': (DEPTH, 512),
    'w_out': (DEPTH, D, D), 'ffn2_norm': (DEPTH, D), 'ffn2_w_gate': (DEPTH, D, HID), 'ffn2_w_up': (DEPTH, D, HID),
    'ffn2_w_down': (DEPTH, HID, D), 'final_norm': (D,),
}


def build_program(T=SEQ, depth=DEPTH):
    nc = bass.Bass("TRN2", target_bir_lowering=False)
    x = nc.dram_tensor("x", [T, D], F32, kind="ExternalInput").ap()
    p = {k: nc.dram_tensor(k, list(s), F32, kind="ExternalInput").ap() for k, s in PARAM_SHAPES.items()}
    y = nc.dram_tensor("y", [T, D], F32, kind="ExternalOutput").ap()
    xres = nc.dram_tensor("xres", [T, D], F32, kind="Internal").ap()
    wguA = nc.dram_tensor("wguA", [NJ, 128, 2, 8, 128], BF16, kind="Internal").ap()
    wguB = nc.dram_tensor("wguB", [NJ, 128, 2, 8, 128], BF16, kind="Internal").ap()
    xmT = nc.dram_tensor("xmT", [512, T + 4], F32, kind="Internal").ap()
    sog = nc.dram_tensor("sog", [T, 512], BF16, kind="Internal").ap()
    ysguT = nc.dram_tensor("ysguT", [512, T], BF16, kind="Internal").ap()
    hf = nc.dram_tensor("hf", [T, 512], F32, kind="Internal").ap()
    C = Ctx(nc)
    for l in range(depth):
        last = (l == depth - 1)
        if l == 0:
            convert_wgu_phase(C, p['ffn1_w_gate'][l], p['ffn1_w_up'][l], wguA)
        ffn_phase(C, T, x if l == 0 else xres, xres, p['ffn1_norm'][l], wguA, p['ffn1_w_down'][l])
        mixa_phase(C, T, xres, p['mix_norm'][l], p['w_in'][l], p['sgu_norm'][l], p['sgu_w'][l], p['sgu_b'][l],
                   xmT, sog, ysguT, AF.Gelu_apprx_tanh)
        mixb_phase(C, T, False, xmT, hf, p['conv_w'][l], p['conv_b'][l], p['w_q'][l], p['w_k'][l], p['w_v'][l],
                   p['gate_w_fwd'][l], p['gate_b_fwd'][l])
        mixb_phase(C, T, True, xmT, hf, p['conv_w'][l], p['conv_b'][l], p['w_q'][l], p['w_k'][l], p['w_v'][l],
                   p['gate_w_bwd'][l], p['gate_b_bwd'][l], sog=sog, ysguT=ysguT, mh_norm=p['mh_norm'][l],
                   skip=p['mlstm_skip'][l], w_out=p['w_out'][l], xin=xres, xout=xres,
                   conv_jobs=[(p['ffn2_w_gate'][l], p['ffn2_w_up'][l], wguB)] +
                   ([] if last else [(p['ffn1_w_gate'][l + 1], p['ffn1_w_up'][l + 1], wguA)]))
        ffn_phase(C, T, xres, y if last else xres, p['ffn2_norm'][l], wguB, p['ffn2_w_down'][l],
                  final_g=p['final_norm'] if last else None)
    return nc


def kernel(**inputs):
    x = np.ascontiguousarray(np.asarray(inputs['x'], dtype=np.float32))
    B = x.shape[0]
    params = {k: np.ascontiguousarray(np.asarray(inputs[k], dtype=np.float32)) for k in PARAM_SHAPES}
    nc = build_program()
    in_maps = [dict(params, x=x[b]) for b in range(B)]
    res = run_bass_kernel_spmd(nc, in_maps, core_ids=list(range(B)))
    return np.stack([np.asarray(res.results[b]["y"], dtype=np.float32) for b in range(B)], axis=0)
```
